# Optimizing a Trainium2 kernel written in Bass

```python
import numpy as np
import jax
import jax.numpy as jnp
from jax import lax

D_MODEL = 1024
BATCH = 2
SEQ = 8192
DEPTH = 2

GRID_W = 64
CTX_LEN = 256
MLSTM_HEADS = 4
MLSTM_HEAD_DIM = D_MODEL // MLSTM_HEADS
D_MLSTM = MLSTM_HEADS * MLSTM_HEAD_DIM
MLSTM_CHUNK = 64
QK_CONV = 5
N_GATES = 2 * 2 * MLSTM_HEADS
FOURIER_GROUPS = 4
D_FOURIER = D_MODEL
FOURIER_GROUP_DIM = D_FOURIER // FOURIER_GROUPS
D_CONV = D_MODEL
DW_KERNEL = 31
D_FF = 2816
N_BRANCH = 3
D_BRANCH = D_MODEL
ALPHA = (2 * DEPTH) ** 0.25
BETA = (8 * DEPTH) ** -0.25
LN_EPS = 1e-5
IN_SIZES = (D_MLSTM, D_MLSTM, D_MLSTM, N_GATES, D_MLSTM, D_FOURIER, D_CONV, D_CONV, N_BRANCH * D_MODEL)
N_IN = sum(IN_SIZES)
N_QKVG = 3 * D_MLSTM + N_GATES

kernel_name = "hybrid_mlstm_fnet_conformer_prefix_dit"


def layer_norm(x, g, b):
    xf = x.astype(jnp.float32)
    mu = jnp.mean(xf, axis=-1, keepdims=True)
    var = jnp.mean(jnp.square(xf - mu), axis=-1, keepdims=True)
    return ((xf - mu) * lax.rsqrt(var + LN_EPS) * g + b).astype(x.dtype)


def head_norm(h, g):
    mu = jnp.mean(h, axis=-1, keepdims=True)
    var = jnp.mean(jnp.square(h - mu), axis=-1, keepdims=True)
    gain = g.astype(jnp.float32).reshape(MLSTM_HEADS, MLSTM_HEAD_DIM)[None, :, None, :]
    return (h - mu) * lax.rsqrt(var + LN_EPS) * gain


def sincos_2d(rows, cols, d):
    quarter = d // 4
    omega = 1.0 / (10000.0 ** (jnp.arange(quarter, dtype=jnp.float32) / quarter))
    r = jnp.arange(rows, dtype=jnp.float32)[:, None] * omega
    cl = jnp.arange(cols, dtype=jnp.float32)[:, None] * omega
    row_emb = jnp.concatenate([jnp.sin(r), jnp.cos(r)], axis=-1)
    col_emb = jnp.concatenate([jnp.sin(cl), jnp.cos(cl)], axis=-1)
    emb = jnp.concatenate([
        jnp.broadcast_to(row_emb[:, None, :], (rows, cols, d // 2)),
        jnp.broadcast_to(col_emb[None, :, :], (rows, cols, d // 2))], axis=-1)
    return emb.reshape(rows * cols, d)


def depthwise_conv(x, w):
    k = w.shape[0]
    return lax.conv_general_dilated(
        x, w[:, None, :], window_strides=(1,), padding=[(k // 2, k // 2)],
        dimension_numbers=('NWC', 'WIO', 'NWC'), feature_group_count=x.shape[-1])


def swiglu(u, w1, w3, w2):
    return (jax.nn.silu(u @ w1) * (u @ w3)) @ w2


def mlstm_scan(q, k, v, log_i, log_f, state, with_output):
    bsz, nh, t, dh = q.shape
    nc = t // MLSTM_CHUNK

    def chunks(a):
        a = a.reshape((bsz, nh, nc, MLSTM_CHUNK) + a.shape[3:])
        return jnp.moveaxis(a, 2, 0)

    xs = (chunks(q), chunks(k), chunks(v), chunks(log_i), chunks(log_f))
    scan_order = jnp.tril(jnp.ones((MLSTM_CHUNK, MLSTM_CHUNK), dtype=bool))

    def step(carry, inp):
        c_mat, n_vec, m = carry
        qc, kc, vc, li, lf = inp
        b = jnp.cumsum(lf, axis=-1)
        b_end = b[..., -1]
        g = b_end[..., None] - b + li
        m_new = jnp.maximum(b_end + m, jnp.max(g, axis=-1))
        carry_w = jnp.exp(b_end + m - m_new)
        tok_w = jnp.exp(g - m_new[..., None])
        c_new = carry_w[..., None, None] * c_mat + jnp.einsum('bhsv,bhsk->bhvk', vc * tok_w[..., None], kc)
        n_new = carry_w[..., None] * n_vec + jnp.einsum('bhs,bhsk->bhk', tok_w, kc)
        new = (c_new, n_new, m_new)
        if not with_output:
            return new, None
        d_log = jnp.where(scan_order, b[..., :, None] - b[..., None, :] + li[..., None, :], -jnp.inf)
        inter = b + m[..., None]
        m_t = jnp.maximum(inter, jnp.max(d_log, axis=-1))
        scores = jnp.einsum('bhtk,bhsk->bhts', qc, kc) * jnp.exp(d_log - m_t[..., None])
        inter_w = jnp.exp(inter - m_t)
        num = (jnp.einsum('bhts,bhsv->bhtv', scores, vc)
               + inter_w[..., None] * jnp.einsum('bhvk,bhtk->bhtv', c_mat, qc))
        den = jnp.sum(scores, axis=-1) + inter_w * jnp.einsum('bhk,bhtk->bht', n_vec, qc)
        h = num / jnp.maximum(jnp.abs(den), jnp.exp(-m_t))[..., None]
        return new, h

    state, hs = lax.scan(step, state, xs)
    if not with_output:
        return state, None
    return state, jnp.moveaxis(hs, 0, 2).reshape(bsz, nh, t, dh)


def mlstm_branch(q, k, v, gate_pre, b_gates, qk_w, init_states, with_output):
    bsz, t, _ = q.shape
    q, k = jnp.split(depthwise_conv(jnp.concatenate([q, k], axis=-1), qk_w), 2, axis=-1)

    def heads(a):
        return a.astype(jnp.float32).reshape(bsz, t, MLSTM_HEADS, MLSTM_HEAD_DIM).transpose(0, 2, 1, 3)

    q, k, v = heads(q), heads(k) * MLSTM_HEAD_DIM ** -0.5, heads(v)
    g = (gate_pre + b_gates).astype(jnp.float32).reshape(bsz, t, 2, 2, MLSTM_HEADS).transpose(2, 3, 0, 4, 1)
    log_i = g[:, 0]
    log_f = jax.nn.log_sigmoid(g[:, 1])
    st_f, h_f = mlstm_scan(q, k, v, log_i[0], log_f[0], init_states[0], with_output)
    rev = lambda a: jnp.flip(a, axis=2)
    st_b, h_b = mlstm_scan(rev(q), rev(k), rev(v), rev(log_i[1]), rev(log_f[1]), init_states[1], with_output)
    if not with_output:
        return None, (st_f, st_b)
    return h_f + rev(h_b), (st_f, st_b)


def context_mlstm_states(u, w_in, b_gates, qk_w, init_states):
    q, k, v, gp = jnp.split(u @ w_in[:, :N_QKVG], [D_MLSTM, 2 * D_MLSTM, 3 * D_MLSTM], axis=-1)
    _, states = mlstm_branch(q, k, v, gp, b_gates, qk_w, init_states, False)
    return states


def mixer(u, w_in, b_gates, qk_w, mh_g, dw_w, dw_b, cn_g, cn_b, w_branch, w_out, init_states):
    bsz, t, _ = u.shape
    split_at = np.cumsum(IN_SIZES)[:-1].tolist()
    q, k, v, gp, o, f_in, c_val, c_gate, m_gate = jnp.split(u @ w_in, split_at, axis=-1)
    h, states = mlstm_branch(q, k, v, gp, b_gates, qk_w, init_states, True)
    h_a = head_norm(h, mh_g).transpose(0, 2, 1, 3).reshape(bsz, t, D_MLSTM).astype(u.dtype) * jax.nn.sigmoid(o)
    f = f_in.astype(jnp.float32).reshape(bsz, t, FOURIER_GROUPS, FOURIER_GROUP_DIM)
    h_b = jnp.fft.fftn(f, axes=(1, 3), norm='ortho').real.reshape(bsz, t, D_FOURIER).astype(u.dtype)
    cv = depthwise_conv(c_val * jax.nn.sigmoid(c_gate), dw_w) + dw_b
    h_c = jax.nn.silu(layer_norm(cv, cn_g, cn_b))
    gates = jax.nn.sigmoid(m_gate).reshape(bsz, t, N_BRANCH, D_MODEL)
    merged = (gates[:, :, 0] * (h_a @ w_branch[0])
              + gates[:, :, 1] * (h_b @ w_branch[1])
              + gates[:, :, 2] * (h_c @ w_branch[2]))
    return merged @ w_out, states


def ffn_sublayer(h, mod, w1, w3, w2, g, b):
    u = h * (1 + mod[1]) + mod[0]
    return layer_norm(ALPHA * h + 0.5 * mod[2] * swiglu(u, w1, w3, w2), g, b)


def setup_inputs(seed: int = 0) -> dict:
    key = jax.random.key(seed)
    ks = jax.random.split(key, 24)
    nrm = lambda k, s: jax.random.normal(k, s, dtype=jnp.float32)
    i_bias = 0.1 * nrm(ks[10], (DEPTH, 2, 1, MLSTM_HEADS))
    f_bias = jnp.linspace(3.0, 6.0, MLSTM_HEADS, dtype=jnp.float32) + 0.1 * nrm(ks[11], (DEPTH, 2, 1, MLSTM_HEADS))
    return {
        "x": nrm(ks[0], (BATCH, SEQ, D_MODEL)),
        "c": nrm(ks[1], (BATCH, D_MODEL)),
        "ctx": nrm(ks[2], (BATCH, CTX_LEN, D_MODEL)),
        "c_ctx": nrm(ks[3], (D_MODEL,)),
        "w_ada": nrm(ks[4], (DEPTH, D_MODEL, 9 * D_MODEL)) * (0.5 * D_MODEL ** -0.5),
        "b_ada": 0.02 * nrm(ks[5], (DEPTH, 9 * D_MODEL)),
        "ln_g": 1.0 + 0.02 * nrm(ks[6], (DEPTH, 3, D_MODEL)),
        "ln_b": 0.02 * nrm(ks[7], (DEPTH, 3, D_MODEL)),
        "ffn_w1": nrm(ks[8], (DEPTH, 2, D_MODEL, D_FF)) * D_MODEL ** -0.5,
        "ffn_w3": nrm(ks[9], (DEPTH, 2, D_MODEL, D_FF)) * D_MODEL ** -0.5,
        "ffn_w2": nrm(ks[12], (DEPTH, 2, D_FF, D_MODEL)) * (D_FF ** -0.5 * BETA),
        "w_in": nrm(ks[13], (DEPTH, D_MODEL, N_IN)) * D_MODEL ** -0.5,
        "b_gates": jnp.concatenate([i_bias, f_bias], axis=2).reshape(DEPTH, N_GATES),
        "qk_conv_w": nrm(ks[14], (DEPTH, QK_CONV, 2 * D_MLSTM)) * QK_CONV ** -0.5,
        "mh_norm_g": 1.0 + 0.02 * nrm(ks[15], (DEPTH, D_MLSTM)),
        "dw_w": nrm(ks[16], (DEPTH, DW_KERNEL, D_CONV)) * DW_KERNEL ** -0.5,
        "dw_b": 0.02 * nrm(ks[17], (DEPTH, D_CONV)),
        "conv_norm_g": 1.0 + 0.02 * nrm(ks[18], (DEPTH, D_CONV)),
        "conv_norm_b": 0.02 * nrm(ks[19], (DEPTH, D_CONV)),
        "w_branch": nrm(ks[20], (DEPTH, N_BRANCH, D_BRANCH, D_MODEL)) * D_BRANCH ** -0.5,
        "w_out": nrm(ks[21], (DEPTH, D_MODEL, D_MODEL)) * (D_MODEL ** -0.5 * BETA),
    }


def reference(x, c, ctx, c_ctx, w_ada, b_ada, ln_g, ln_b, ffn_w1, ffn_w3, ffn_w2, w_in, b_gates,
              qk_conv_w, mh_norm_g, dw_w, dw_b, conv_norm_g, conv_norm_b, w_branch, w_out):
    bsz, t, d = x.shape
    rows = t // GRID_W
    x = x + sincos_2d(rows, GRID_W, d).astype(x.dtype)[None]
    xc = ctx
    zero_state = (jnp.zeros((bsz, MLSTM_HEADS, MLSTM_HEAD_DIM, MLSTM_HEAD_DIM), jnp.float32),
                  jnp.zeros((bsz, MLSTM_HEADS, MLSTM_HEAD_DIM), jnp.float32),
                  jnp.zeros((bsz, MLSTM_HEADS), jnp.float32))
    init = (zero_state, zero_state)
    for l in range(DEPTH):
        last = l == DEPTH - 1
        ada_x = (jax.nn.silu(c) @ w_ada[l] + b_ada[l]).reshape(bsz, 3, 3, d).transpose(1, 2, 0, 3)[:, :, :, None, :]
        ada_c = (jax.nn.silu(c_ctx) @ w_ada[l] + b_ada[l]).reshape(3, 3, d)
        mix_args = (w_in[l], b_gates[l], qk_conv_w[l], mh_norm_g[l], dw_w[l], dw_b[l],
                    conv_norm_g[l], conv_norm_b[l], w_branch[l], w_out[l])
        x = ffn_sublayer(x, ada_x[0], ffn_w1[l, 0], ffn_w3[l, 0], ffn_w2[l, 0], ln_g[l, 0], ln_b[l, 0])
        xc = ffn_sublayer(xc, ada_c[0], ffn_w1[l, 0], ffn_w3[l, 0], ffn_w2[l, 0], ln_g[l, 0], ln_b[l, 0])
        uc = xc * (1 + ada_c[1, 1]) + ada_c[1, 0]
        if last:
            ctx_states = context_mlstm_states(uc, w_in[l], b_gates[l], qk_conv_w[l], init)
        else:
            y_c, ctx_states = mixer(uc, *mix_args, init)
            xc = layer_norm(ALPHA * xc + ada_c[1, 2] * y_c, ln_g[l, 1], ln_b[l, 1])
        ux = x * (1 + ada_x[1, 1]) + ada_x[1, 0]
        y_x, _ = mixer(ux, *mix_args, ctx_states)
        x = layer_norm(ALPHA * x + ada_x[1, 2] * y_x, ln_g[l, 1], ln_b[l, 1])
        x = ffn_sublayer(x, ada_x[2], ffn_w1[l, 1], ffn_w3[l, 1], ffn_w2[l, 1], ln_g[l, 2], ln_b[l, 2])
        if not last:
            xc = ffn_sublayer(xc, ada_c[2], ffn_w1[l, 1], ffn_w3[l, 1], ffn_w2[l, 1], ln_g[l, 2], ln_b[l, 2])
    return x
```

```python
import contextlib
import numpy as np
import concourse.bass as bass
import concourse.mybir as mybir
from concourse.bass_utils import run_bass_kernel_spmd

F32 = mybir.dt.float32
BF16 = mybir.dt.bfloat16
AF = mybir.ActivationFunctionType
ALU = mybir.AluOpType

D = 1024
B = 2
T = 8192
DEPTH = 2
NCORE = 8
NUM_DEV = None
TCX = 256
HEADS = 4
DH = 256
DFF = 2816
NFF = DFF // 128
NIN = 10256
GRID_W = 64
ALPHA = (2 * DEPTH) ** 0.25
LN_EPS = 1e-5
TL = T // 4
CL = TCX // 4
NT = TL + CL
TS = TCX + T
OQ, OK_, OV, OG, OO, OF, OCV, OCG, OMG = 0, 1024, 2048, 3072, 3088, 4112, 5136, 6160, 7184

NDMA_SEM = 8
CC_INC = 1
EPOCH = 30000


class Buf:
    __slots__ = ("w", "r")

    def __init__(self):
        self.w = None
        self.r = []


class Sched:
    ENGS = ("pe", "act", "dve", "pool", "sp")
    DQ = ("sp", "pool", "act")

    def __init__(self, nc, stack, same_engine_sync=True):
        self.nc = nc
        self.stack = stack
        self.same = same_engine_sync
        self.count = {e: 0 for e in self.ENGS}
        self.sem = {e: [] for e in self.ENGS}
        self.dsem = {e: [stack.enter_context(nc.semaphore(f"d_{e}_{i}")) for i in range(NDMA_SEM)]
                     for e in self.DQ}
        self.dcount = {e: 0 for e in self.DQ}
        self.waited = {}
        self.engobj = {"pe": nc.tensor, "act": nc.scalar, "dve": nc.vector, "pool": nc.gpsimd,
                       "sp": nc.sync}

    def _csem(self, eng, ep):
        while len(self.sem[eng]) <= ep:
            self.sem[eng].append(self.stack.enter_context(
                self.nc.semaphore(f"s_{eng}_{len(self.sem[eng])}")))
        return self.sem[eng][ep]

    def _tok_wait(self, tok):
        if tok[0] == "c":
            ep = tok[2] // EPOCH
            return (self._csem(tok[1], ep), (ep, tok[2] % EPOCH + 1), ("c", tok[1]))
        if tok[0] == "x":
            return (self.ccsem, (0, CC_INC * (tok[2] + 1)), ("x", "cc"))
        q, j = tok[1], tok[2]
        return (self.dsem[q][j % NDMA_SEM], (0, 16 * (j // NDMA_SEM + 1)), ("d", q, j % NDMA_SEM))

    def _waits_for(self, eng, toks):
        waits = []
        for tok in toks:
            if tok[0] == "c" and tok[1] == eng and (eng == "pe" or not self.same):
                continue
            sem, val, key = self._tok_wait(tok)
            k = (eng, key)
            if self.waited.get(k, (0, 0)) >= val:
                continue
            self.waited[k] = val
            waits.append((sem, val[1]))
        return waits

    def _deps(self, eng, reads, writes):
        toks = []
        for b in reads:
            if b.w is not None:
                toks.append(b.w)
        for b in writes:
            if b.w is not None:
                toks.append(b.w)
            toks.extend(b.r)
        return self._waits_for(eng, toks)

    @staticmethod
    def _mark(tok, reads, writes):
        for b in reads:
            b.r.append(tok)
        for b in writes:
            b.w = tok
            b.r = []

    def _push(self, eng, waits, fn, inc):
        e = self.engobj[eng]
        for sem, val in waits:
            e.wait_ge(sem, val)
        if fn is not None:
            fn(e).then_inc(inc[0], inc[1])

    def op(self, eng, fn, reads=(), writes=()):
        waits = self._deps(eng, reads, writes)
        idx = self.count[eng]
        self.count[eng] += 1
        tok = ("c", eng, idx)
        self._mark(tok, reads, writes)
        self._push(eng, waits, fn, (self._csem(eng, idx // EPOCH), 1))
        return tok

    def dma(self, q, fn, reads=(), writes=()):
        waits = self._deps(q, reads, writes)
        j = self.dcount[q]
        self.dcount[q] += 1
        ring = self.dsem[q][j % NDMA_SEM]
        if j >= NDMA_SEM:
            val = 16 * (j // NDMA_SEM)
            k = (q, ("d", q, j % NDMA_SEM))
            if self.waited.get(k, (0, 0)) < (0, val):
                self.waited[k] = (0, val)
                waits.append((ring, val))
        tok = ("d", q, j)
        self._mark(tok, reads, writes)
        self._push(q, waits, fn, (ring, 16))
        return tok

    def cc(self, fn, reads=(), writes=()):
        waits = self._deps("pool", reads, writes)
        if not hasattr(self, "ccsem"):
            self.ccsem = self.stack.enter_context(self.nc.semaphore("s_cc"))
            self.cccount = 0
        j = self.cccount
        self.cccount += 1
        tok = ("x", "cc", j)
        self._mark(tok, reads, writes)
        self._push("pool", waits, fn, (self.ccsem, CC_INC))
        return tok

    def _all_last(self):
        toks = [("c", e, self.count[e] - 1) for e in self.ENGS if self.count[e] > 0]
        for q in self.DQ:
            for j in range(max(0, self.dcount[q] - NDMA_SEM), self.dcount[q]):
                toks.append(("d", q, j))
        if getattr(self, "cccount", 0) > 0:
            toks.append(("x", "cc", self.cccount - 1))
        return toks

    def barrier(self, engs=None):
        toks = self._all_last()
        for e in (engs or self.ENGS):
            self._push(e, self._waits_for(e, toks), None, None)

    def finish(self):
        self.barrier(engs=("sp",))


class Prog:
    def __init__(self, io):
        self.nc = bass.Bass("TRN2", target_bir_lowering=False, num_devices=NUM_DEV)
        self.io = io
        self.dr = {}
        self.drbuf = {}
        for name, (kind, shape, dt) in io.items():
            k = "ExternalInput" if kind == "in" else "ExternalOutput"
            self.dr[name] = self.nc.dram_tensor(name, list(shape), dt, kind=k).ap()
        self.top = contextlib.ExitStack()
        self.S = Sched(self.nc, self.top)
        self.uid = 0

    def scratch(self, name, shape, dt):
        self.dr[name] = self.nc.dram_tensor(name, list(shape), dt).ap()
        return self.dr[name]

    def dbuf(self, name, key=0):
        k = (name, key)
        if k not in self.drbuf:
            self.drbuf[k] = Buf()
        return self.drbuf[k]

    def sb(self, st, shape, dt, name=None):
        self.uid += 1
        return st.enter_context(self.nc.sbuf_tensor(f"{name or 't'}{self.uid}", list(shape), dt))

    def ps(self, st, shape, dt=F32, name=None):
        self.uid += 1
        return st.enter_context(self.nc.psum_tensor(f"{name or 'p'}{self.uid}", list(shape), dt))


class Rot:
    def __init__(self, items):
        self.items = [(t, Buf()) for t in items]
        self.i = 0

    def next(self):
        it = self.items[self.i % len(self.items)]
        self.i += 1
        return it


def tiles_of(W):
    out = [(t0, W, 0) for t0 in range(0, TL, W)]
    out.append((TL, CL, 1))
    return out


class TP(Prog):
    def __init__(self, io, vec_cols):
        super().__init__(io)
        self.vc = vec_cols
        S, nc = self.S, self.nc
        st = self.top
        nv = io["vecs"][1][1]
        self.vecs = self.sb(st, [128, nv], F32, "vecs")
        self.Bvecs = Buf()
        S.dma("sp", lambda e: e.dma_start(out=self.vecs[:], in_=self.dr["vecs"]), writes=[self.Bvecs])
        self.der = self.sb(st, [128, 1024], F32, "der")
        self.Bder = Buf()
        self.dcol = 0
        self.dnames = {}
        self.ones_mean = self.sb(st, [128, 128], BF16, "onesm")
        self.Bones = Buf()
        S.op("dve", lambda e: e.memset(self.ones_mean[:], 1.0 / D), writes=[self.Bones])
        self.ada = {}
        self.wfull = {}
        self.Bg = {}
        self.Bsend = {}
        pid = self.nc.sync.partition_id()
        self.bat = pid // 4
        self.seg = pid % 4

    def v(self, name, n=8):
        o = self.vc[name]
        return self.vecs[:, o:o + n]

    def dnew(self, name, n=8):
        self.dnames[name] = self.dcol
        self.dcol += n
        assert self.dcol <= 1024
        return self.d(name, n)

    def d(self, name, n=8):
        o = self.dnames[name]
        return self.der[:, o:o + n]

    def dve_small(self, fn):
        self.S.op("dve", fn, reads=[self.Bvecs, self.Bder], writes=[self.Bder])

    def compute_ada_all(self):
        S, nc = self.S, self.nc
        part = self.scratch("ada_part", (128, 512), F32)
        partg = self.scratch("ada_partg", (8 * 128, 512), F32)
        Bpart, Bpartg = Buf(), Buf()
        adas = [self.sb(self.top, [128, 72, 2], F32, "ada") for _ in range(DEPTH)]
        with contextlib.ExitStack() as st:
            c32 = self.sb(st, [128, 3], F32)
            cs = self.sb(st, [128, 3], BF16)
            Bc32, Bcs = Buf(), Buf()
            S.dma("sp", lambda e: e.dma_start(out=c32[:], in_=self.dr["cT3"]), writes=[Bc32])
            S.op("act", lambda e: e.activation(out=cs[:], in_=c32[:], func=AF.Silu), reads=[Bc32], writes=[Bcs])
            psb = self.sb(st, [128, DEPTH * 216], F32); Bpsb = Buf()
            for l in range(DEPTH):
                w = self.sb(st, [128, 9 * D], BF16)
                Bw = Buf()
                S.dma("pool", lambda e, w=w, l=l: e.dma_start(out=w[:], in_=self.dr[f"w_ada{l}"]), writes=[Bw])
                pt = self.ps(st, [128, 72, 3])
                Bpt = Buf()
                for fc in range(72):
                    S.op("pe", lambda e, w=w, fc=fc, pt=pt: e.matmul(
                        pt[:, fc, :], w[:, fc * 128:(fc + 1) * 128], cs[:], start=True, stop=True),
                        reads=[Bw, Bcs], writes=[Bpt])
                S.op("dve", lambda e, pt=pt, l=l: e.tensor_copy(
                    out=psb[:, l * 216:(l + 1) * 216], in_=pt[:].rearrange("p a b -> p (a b)")),
                    reads=[Bpt], writes=[Bpsb])
            S.dma("sp", lambda e: e.dma_start(out=part[:, 0:DEPTH * 216], in_=psb[:]), reads=[Bpsb], writes=[Bpart])
            S.cc(lambda e: e.collective_compute("AllGather", ALU.bypass, replica_groups=[list(range(NCORE))],
                                                ins=[part], outs=[partg]), reads=[Bpart], writes=[Bpartg])
            g8 = self.sb(st, [128, 8, DEPTH * 216], F32); Bg8 = Buf()
            S.dma("sp", lambda e: e.dma_start(out=g8[:], in_=partg.rearrange("(r p) n -> p r n", p=128)[:, :, 0:DEPTH * 216]),
                  reads=[Bpartg], writes=[Bg8])
            acc = self.sb(st, [128, DEPTH * 216], F32); Bacc = Buf()
            S.op("dve", lambda e: e.tensor_tensor(out=acc[:], in0=g8[:, 0, :], in1=g8[:, 1, :], op=ALU.add),
                 reads=[Bg8], writes=[Bacc])
            for r in range(2, 8):
                S.op("dve", lambda e, r=r: e.tensor_tensor(out=acc[:], in0=acc[:], in1=g8[:, r, :], op=ALU.add),
                     reads=[Bg8, Bacc], writes=[Bacc])
            selb = self.v("selb", 2)
            for l in range(DEPTH):
                ada = adas[l]
                Bada = Buf()
                a3 = acc[:, l * 216:(l + 1) * 216].rearrange("p (a b) -> p a b", b=3)
                o = self.vc[f"b_ada{l}"]
                bias = self.vecs[:, o:o + 72]
                S.op("dve", lambda e, ada=ada, a3=a3: e.tensor_scalar_mul(out=ada[:, :, 0], in0=a3[:, :, 0], scalar1=selb[:, 0:1]),
                     reads=[Bacc, self.Bvecs], writes=[Bada])
                S.op("dve", lambda e, ada=ada, a3=a3: e.scalar_tensor_tensor(
                    out=ada[:, :, 0], in0=a3[:, :, 1], scalar=selb[:, 1:2], in1=ada[:, :, 0], op0=ALU.mult, op1=ALU.add),
                    reads=[Bacc, self.Bvecs], writes=[Bada])
                S.op("dve", lambda e, ada=ada, a3=a3: e.tensor_copy(out=ada[:, :, 1], in_=a3[:, :, 2]),
                     reads=[Bacc], writes=[Bada])
                for col in range(2):
                    S.op("dve", lambda e, ada=ada, col=col, bias=bias: e.tensor_tensor(
                        out=ada[:, :, col], in0=ada[:, :, col], in1=bias, op=ALU.add),
                        reads=[self.Bvecs], writes=[Bada])
                self.ada[l] = (ada, Bada)
            S.barrier()

    def ada_ap(self, l, sub, kind, k):
        ada, _ = self.ada[l]
        i = (sub * 3 + kind) * 8
        return ada[:, i:i + 8, k]

    def derive_mod(self, name, l, sub):
        for k in range(2):
            sc = self.dnew(f"{name}_sc{k}")
            self.S.op("dve", lambda e, sc=sc, k=k: e.tensor_scalar_add(
                out=sc, in0=self.ada_ap(l, sub, 1, k), scalar1=1.0),
                reads=[self.ada[l][1]], writes=[self.Bder])
            sh = self.dnew(f"{name}_sh{k}")
            self.S.op("dve", lambda e, sh=sh, k=k: e.tensor_copy(out=sh, in_=self.ada_ap(l, sub, 0, k)),
                      reads=[self.ada[l][1]], writes=[self.Bder])

    def derive_ep(self, name, l, sub, nxt):
        S = self.S
        g = self.v(f"ln_g{l}_{sub}")
        b = self.v(f"ln_b{l}_{sub}")
        rd = [self.ada[l][1], self.Bvecs, self.Bder]
        if nxt is not None:
            rd.append(self.ada[nxt[0]][1])
        for k in range(2):
            gr = self.dnew(f"{name}_gr{k}")
            S.op("dve", lambda e, gr=gr, k=k: e.tensor_scalar_mul(
                out=gr, in0=self.ada_ap(l, sub, 2, k), scalar1=(1.0 if sub == 1 else 0.5)),
                reads=rd, writes=[self.Bder])
        if nxt is not None:
            xs = self.dnew(f"{name}_xs")
            xb = self.dnew(f"{name}_xb")
            S.op("dve", lambda e: e.tensor_scalar_mul(out=xs, in0=g, scalar1=ALPHA), reads=rd, writes=[self.Bder])
            S.op("dve", lambda e: e.tensor_scalar_mul(out=xb, in0=b, scalar1=ALPHA), reads=rd, writes=[self.Bder])
            for k in range(2):
                us = self.dnew(f"{name}_us{k}")
                ub = self.dnew(f"{name}_ub{k}")
                tmp = self.dnew(f"{name}_tmp{k}")
                S.op("dve", lambda e, tmp=tmp, k=k: e.tensor_scalar_add(
                    out=tmp, in0=self.ada_ap(nxt[0], nxt[1], 1, k), scalar1=1.0), reads=rd, writes=[self.Bder])
                S.op("dve", lambda e, us=us, tmp=tmp: e.tensor_tensor(out=us, in0=g, in1=tmp, op=ALU.mult),
                     reads=rd, writes=[self.Bder])
                S.op("dve", lambda e, ub=ub, tmp=tmp: e.tensor_tensor(out=ub, in0=b, in1=tmp, op=ALU.mult),
                     reads=rd, writes=[self.Bder])
                S.op("dve", lambda e, ub=ub, k=k: e.tensor_tensor(
                    out=ub, in0=ub, in1=self.ada_ap(nxt[0], nxt[1], 0, k), op=ALU.add), reads=rd, writes=[self.Bder])
        else:
            xs = self.dnew(f"{name}_xs")
            xb = self.dnew(f"{name}_xb")
            S.op("dve", lambda e: e.tensor_copy(out=xs, in_=g), reads=rd, writes=[self.Bder])
            S.op("dve", lambda e: e.tensor_copy(out=xb, in_=b), reads=rd, writes=[self.Bder])

    def fm(self, name, t0, W, nchunk=None):
        return self.dr[name].rearrange("(c p) t -> p c t", p=128)[:, :, t0:t0 + W]

    def fm_g(self, name, t0, W, k):
        g = self.dr[name].rearrange("(c p) t -> p c t", p=128)
        col = (TCX + self.seg * TL + t0) if k == 0 else (self.seg * CL)
        return g[:, bass.ds(self.bat * 8, 8), bass.ds(col, W)]

    def gather_weight(self, name, K, N):
        S = self.S
        rk = K // 8
        rp = 1 << (rk - 1).bit_length()
        pieces = []
        c0 = 0
        while c0 < N:
            n = min(8192, N - c0)
            npad = 1 << (n - 1).bit_length()
            while rp * npad * 2 < 131072:
                npad *= 2
            sh = self.scratch(f"{name}_sh{c0}", (rp, npad), BF16)
            full = self.scratch(f"{name}_fu{c0}", (8 * rp, npad), BF16)
            Bsh, Bfull = Buf(), Buf()
            S.dma("pool", lambda e, sh=sh, c0=c0, n=n: e.dma_start(out=sh[0:rk, 0:n], in_=self.dr[name][:, c0:c0 + n]), writes=[Bsh])
            S.cc(lambda e, sh=sh, full=full: e.collective_compute("AllGather", ALU.bypass, replica_groups=[list(range(NCORE))],
                                                               ins=[sh], outs=[full]), reads=[Bsh], writes=[Bfull])
            pieces.append((c0, n, full, Bfull))
            c0 += n
        self.wfull[name] = (rk, rp, pieces)

    def load_w(self, st, name, K, N, c0=0, tag="w"):
        rk, rp, pieces = self.wfull[name]
        kc_n = K // 128
        w = self.sb(st, [128, kc_n, N], BF16, tag)
        bufs = [Buf() for _ in range(kc_n)]
        for kc in range(kc_n):
            g0 = kc * 128
            while g0 < (kc + 1) * 128:
                r = g0 // rk
                n = min((kc + 1) * 128 - g0, (r + 1) * rk - g0)
                row = r * rp + (g0 - r * rk)
                p0 = g0 - kc * 128
                for (pc0, pn, full, Bfull) in pieces:
                    lo, hi = max(c0, pc0), min(c0 + N, pc0 + pn)
                    if lo >= hi:
                        continue
                    self.S.dma("sp", lambda e, kc=kc, p0=p0, n=n, row=row, lo=lo, hi=hi, full=full, pc0=pc0: e.dma_start(
                        out=w[p0:p0 + n, kc, lo - c0:hi - c0], in_=full[row:row + n, lo - pc0:hi - pc0]),
                        reads=[Bfull], writes=[bufs[kc]])
                g0 += n
        return w, bufs

    def step_prep(self, xin, pos, xa_out, u_out, mod):
        S = self.S
        with contextlib.ExitStack() as st:
            xr = Rot([self.sb(st, [128, 8, 512], F32) for _ in range(2)])
            pr = Rot([self.sb(st, [128, 8, 512], F32) for _ in range(2)])
            xo = Rot([self.sb(st, [128, 8, 512], F32) for _ in range(2)])
            uo = Rot([self.sb(st, [128, 8, 512], BF16) for _ in range(2)])
            for (t0, W, k) in tiles_of(512):
                x, Bx = xr.next()
                S.dma("sp", lambda e, x=x, t0=t0, W=W: e.dma_start(out=x[:, :, 0:W], in_=self.fm(xin, t0, W)), writes=[Bx])
                if k == 0:
                    p, Bp = pr.next()
                    S.dma("sp", lambda e, p=p, t0=t0, W=W: e.dma_start(out=p[:, :, 0:W], in_=self.fm(pos, t0, W)), writes=[Bp])
                    S.op("dve", lambda e, x=x, p=p, W=W: e.tensor_tensor(out=x[:, :, 0:W], in0=x[:, :, 0:W], in1=p[:, :, 0:W], op=ALU.add),
                         reads=[Bx, Bp], writes=[Bx])
                xa, Bxa = xo.next()
                u, Bu = uo.next()
                S.op("act", lambda e, xa=xa, x=x, W=W: e.activation(out=xa[:, :, 0:W], in_=x[:, :, 0:W], func=AF.Identity, scale=ALPHA),
                     reads=[Bx], writes=[Bxa])
                sc, sh = self.d(f"{mod}_sc{k}"), self.d(f"{mod}_sh{k}")
                for c in range(8):
                    S.op("act", lambda e, u=u, x=x, W=W, c=c, sc=sc, sh=sh: e.activation(
                        out=u[:, c, 0:W], in_=x[:, c, 0:W], func=AF.Identity, bias=sh[:, c:c + 1], scale=sc[:, c:c + 1]),
                        reads=[Bx, self.Bder], writes=[Bu])
                S.dma("sp", lambda e, xa=xa, t0=t0, W=W: e.dma_start(out=self.fm(xa_out, t0, W), in_=xa[:, :, 0:W]),
                      reads=[Bxa], writes=[self.dbuf(xa_out, t0)])
                S.dma("sp", lambda e, u=u, t0=t0, W=W: e.dma_start(out=self.fm(u_out, t0, W), in_=u[:, :, 0:W]),
                      reads=[Bu], writes=[self.dbuf(u_out, t0)])
            S.barrier()

    def step_ffn_up(self, l, i, u_in, a_out):
        S = self.S
        with contextlib.ExitStack() as st:
            w1, B1 = self.load_w(st, f"w1_{l}{i}", D, DFF, tag="w1")
            w3, B3 = self.load_w(st, f"w3_{l}{i}", D, DFF, tag="w3")
            ur = Rot([self.sb(st, [128, 8, 512], BF16) for _ in range(2)])
            ar = Rot([self.sb(st, [128, NFF, 512], BF16) for _ in range(2)])
            sr = Rot([self.sb(st, [128, 512], F32) for _ in range(2)])
            pr = Rot([self.ps(st, [128, 512]) for _ in range(6)])
            for (t0, W, k) in tiles_of(512):
                u, Bu = ur.next()
                S.dma("sp", lambda e, u=u, t0=t0, W=W: e.dma_start(out=u[:, :, 0:W], in_=self.fm(u_in, t0, W)),
                      reads=[self.dbuf(u_in, t0)], writes=[Bu])
                a, Ba = ar.next()
                for fc in range(NFF):
                    ph, Bph = pr.next()
                    pg, Bpg = pr.next()
                    for kc in range(8):
                        S.op("pe", lambda e, ph=ph, kc=kc, fc=fc, u=u, W=W: e.matmul(
                            ph[:, 0:W], w1[:, kc, fc * 128:(fc + 1) * 128], u[:, kc, 0:W], start=(kc == 0), stop=(kc == 7)),
                            reads=[B1[kc], Bu], writes=[Bph])
                    for kc in range(8):
                        S.op("pe", lambda e, pg=pg, kc=kc, fc=fc, u=u, W=W: e.matmul(
                            pg[:, 0:W], w3[:, kc, fc * 128:(fc + 1) * 128], u[:, kc, 0:W], start=(kc == 0), stop=(kc == 7)),
                            reads=[B3[kc], Bu], writes=[Bpg])
                    s, Bs = sr.next()
                    S.op("act", lambda e, s=s, ph=ph, W=W: e.activation(out=s[:, 0:W], in_=ph[:, 0:W], func=AF.Silu),
                         reads=[Bph], writes=[Bs])
                    S.op("dve", lambda e, a=a, fc=fc, s=s, pg=pg, W=W: e.tensor_tensor(
                        out=a[:, fc, 0:W], in0=pg[:, 0:W], in1=s[:, 0:W], op=ALU.mult),
                        reads=[Bpg, Bs], writes=[Ba])
                S.dma("sp", lambda e, a=a, t0=t0, W=W: e.dma_start(out=self.fm(a_out, t0, W), in_=a[:, :, 0:W]),
                      reads=[Ba], writes=[self.dbuf(a_out, t0)])
            S.barrier()

    def ln_stats(self, st_tiles, z, Bz, W, nch=8):
        S = self.S
        zb_r, zq_r, pm, Bpm, pq, Bpq, mean, Bmean, rstd, Brstd = st_tiles
        for c in range(nch):
            zb, Bzb = zb_r.next()
            zq, Bzq = zq_r.next()
            S.op("act", lambda e, zb=zb, c=c: e.activation(out=zb[:, 0:W], in_=z[:, c, 0:W], func=AF.Copy),
                 reads=[Bz], writes=[Bzb])
            S.op("act", lambda e, zq=zq, c=c: e.activation(out=zq[:, 0:W], in_=z[:, c, 0:W], func=AF.Square),
                 reads=[Bz], writes=[Bzq])
            S.op("pe", lambda e, zb=zb, c=c: e.matmul(pm[:, 0:W], self.ones_mean[:], zb[:, 0:W], start=(c == 0), stop=(c == nch - 1)),
                 reads=[Bzb, self.Bones], writes=[Bpm])
            S.op("pe", lambda e, zq=zq, c=c: e.matmul(pq[:, 0:W], self.ones_mean[:], zq[:, 0:W], start=(c == 0), stop=(c == nch - 1)),
                 reads=[Bzq, self.Bones], writes=[Bpq])
        S.op("act", lambda e: e.activation(out=mean[:, 0:W], in_=pm[:, 0:W], func=AF.Copy), reads=[Bpm], writes=[Bmean])
        S.op("dve", lambda e: e.tensor_tensor(out=rstd[:, 0:W], in0=mean[:, 0:W], in1=mean[:, 0:W], op=ALU.mult),
             reads=[Bmean], writes=[Brstd])
        S.op("dve", lambda e: e.tensor_tensor(out=rstd[:, 0:W], in0=pq[:, 0:W], in1=rstd[:, 0:W], op=ALU.subtract),
             reads=[Bpq, Brstd], writes=[Brstd])
        S.op("dve", lambda e: e.tensor_scalar(out=rstd[:, 0:W], in0=rstd[:, 0:W], scalar1=0.0, scalar2=LN_EPS, op0=ALU.max, op1=ALU.add),
             reads=[Brstd], writes=[Brstd])
        S.op("act", lambda e: e.activation(out=rstd[:, 0:W], in_=rstd[:, 0:W], func=AF.Sqrt), reads=[Brstd], writes=[Brstd])
        S.op("dve", lambda e: e.reciprocal(out=rstd[:, 0:W], in_=rstd[:, 0:W]), reads=[Brstd], writes=[Brstd])
        return mean, Bmean, rstd, Brstd

    def alloc_ln(self, st, Wmax):
        zb_r = Rot([self.sb(st, [128, Wmax], BF16) for _ in range(2)])
        zq_r = Rot([self.sb(st, [128, Wmax], BF16) for _ in range(2)])
        pm = self.ps(st, [128, 512]); pq = self.ps(st, [128, 512])
        mean = self.sb(st, [128, Wmax], F32); rstd = self.sb(st, [128, Wmax], F32)
        return (zb_r, zq_r, pm, Buf(), pq, Buf(), mean, Buf(), rstd, Buf())

    def ln_out(self, lnt, z, Bz, W, k, ep, t0, xa_out, u_out, xo, Bxo, uo, Buo, final):
        S = self.S
        mean, Bmean, rstd, Brstd = self.ln_stats(lnt, z, Bz, W)
        xs, xb = self.d(f"{ep}_xs"), self.d(f"{ep}_xb")
        for c in range(8):
            S.op("dve", lambda e, c=c: e.tensor_tensor(out=z[:, c, 0:W], in0=z[:, c, 0:W], in1=mean[:, 0:W], op=ALU.subtract),
                 reads=[Bz, Bmean], writes=[Bz])
            S.op("dve", lambda e, c=c: e.tensor_tensor(out=z[:, c, 0:W], in0=z[:, c, 0:W], in1=rstd[:, 0:W], op=ALU.mult),
                 reads=[Bz, Brstd], writes=[Bz])
            S.op("act", lambda e, c=c: e.activation(out=xo[:, c, 0:W], in_=z[:, c, 0:W], func=AF.Identity,
                                                    bias=xb[:, c:c + 1], scale=xs[:, c:c + 1]),
                 reads=[Bz, self.Bder], writes=[Bxo])
            if not final:
                us, ub = self.d(f"{ep}_us{k}"), self.d(f"{ep}_ub{k}")
                S.op("act", lambda e, c=c, us=us, ub=ub: e.activation(out=uo[:, c, 0:W], in_=z[:, c, 0:W], func=AF.Identity,
                                                                    bias=ub[:, c:c + 1], scale=us[:, c:c + 1]),
                     reads=[Bz, self.Bder], writes=[Buo])
        S.dma("sp", lambda e: e.dma_start(out=self.fm(xa_out, t0, W), in_=xo[:, :, 0:W]),
              reads=[Bxo], writes=[self.dbuf(xa_out, t0)])
        if not final:
            S.dma("sp", lambda e: e.dma_start(out=self.fm(u_out, t0, W), in_=uo[:, :, 0:W]),
                  reads=[Buo], writes=[self.dbuf(u_out, t0)])

    def step_ffn_down(self, l, i, a_in, xa_in, xa_out, u_out, ep, final=False):
        S = self.S
        with contextlib.ExitStack() as st:
            w2, B2 = self.load_w(st, f"w2_{l}{i}", DFF, D, tag="w2")
            ar = Rot([self.sb(st, [128, NFF, 512], BF16) for _ in range(2)])
            xr = Rot([self.sb(st, [128, 8, 512], F32) for _ in range(2)])
            z = self.sb(st, [128, 8, 512], F32); Bz = Buf()
            xo = self.sb(st, [128, 8, 512], F32); Bxo = Buf()
            uo = self.sb(st, [128, 8, 512], BF16); Buo = Buf()
            lnt = self.alloc_ln(st, 512)
            pr = Rot([self.ps(st, [128, 512]) for _ in range(4)])
            for (t0, W, k) in tiles_of(512):
                a, Ba = ar.next()
                S.dma("sp", lambda e, a=a, t0=t0, W=W: e.dma_start(out=a[:, :, 0:W], in_=self.fm(a_in, t0, W)),
                      reads=[self.dbuf(a_in, t0)], writes=[Ba])
                x, Bx = xr.next()
                S.dma("sp", lambda e, x=x, t0=t0, W=W: e.dma_start(out=x[:, :, 0:W], in_=self.fm(xa_in, t0, W)),
                      reads=[self.dbuf(xa_in, t0)], writes=[Bx])
                gr = self.d(f"{ep}_gr{k}")
                for dc in range(8):
                    py, Bpy = pr.next()
                    for fc in range(NFF):
                        S.op("pe", lambda e, py=py, fc=fc, dc=dc, a=a, W=W: e.matmul(
                            py[:, 0:W], w2[:, fc, dc * 128:(dc + 1) * 128], a[:, fc, 0:W], start=(fc == 0), stop=(fc == NFF - 1)),
                            reads=[B2[fc], Ba], writes=[Bpy])
                    S.op("dve", lambda e, py=py, dc=dc, x=x, W=W, gr=gr: e.scalar_tensor_tensor(
                        out=z[:, dc, 0:W], in0=py[:, 0:W], scalar=gr[:, dc:dc + 1], in1=x[:, dc, 0:W], op0=ALU.mult, op1=ALU.add),
                        reads=[Bpy, Bx, self.Bder], writes=[Bz])
                self.ln_out(lnt, z, Bz, W, k, ep, t0, xa_out, u_out, xo, Bxo, uo, Buo, final)
            S.barrier()

    def step_inproj(self, l, u_in):
        S = self.S
        with contextlib.ExitStack() as st:
            w, Bw = self.load_w(st, f"w_in{l}", D, OF, tag="wi1")
            ur = Rot([self.sb(st, [128, 8, 512], BF16) for _ in range(2)])
            qk = self.sb(st, [128, 16, 512], BF16); Bqk = Buf()
            vt = self.sb(st, [128, 4, 1024], BF16); Bvt = Buf()
            so = self.sb(st, [128, 4, 1024], F32); Bso = Buf()
            gt = self.sb(st, [128, 4, 16], F32); Bgt = Buf()
            pr = Rot([self.ps(st, [128, 512]) for _ in range(6)])
            cnt = 0
            for (t0, W, k) in tiles_of(512):
                u, Bu = ur.next()
                S.dma("sp", lambda e, u=u, t0=t0, W=W: e.dma_start(out=u[:, :, 0:W], in_=self.fm(u_in, t0, W)),
                      reads=[self.dbuf(u_in, t0)], writes=[Bu])
                for oc in range(16):
                    p, Bp = pr.next()
                    for kc in range(8):
                        S.op("pe", lambda e, p=p, kc=kc, oc=oc, u=u, W=W: e.matmul(
                            p[:, 0:W], w[:, kc, oc * 128:(oc + 1) * 128], u[:, kc, 0:W], start=(kc == 0), stop=(kc == 7)),
                            reads=[Bw[kc], Bu], writes=[Bp])
                    eng = "act" if oc % 2 == 0 else "dve"
                    if eng == "act":
                        S.op("act", lambda e, p=p, oc=oc, W=W: e.activation(out=qk[:, oc, 0:W], in_=p[:, 0:W], func=AF.Copy),
                             reads=[Bp], writes=[Bqk])
                    else:
                        S.op("dve", lambda e, p=p, oc=oc, W=W: e.tensor_copy(out=qk[:, oc, 0:W], in_=p[:, 0:W]),
                             reads=[Bp], writes=[Bqk])
                S.dma("sp", lambda e, t0=t0, W=W: e.dma_start(out=self.fm("qk_s", t0, W), in_=qk[:, :, 0:W]),
                      reads=[Bqk], writes=[self.dbuf("qk_s", t0)])
                nsub = (W + 127) // 128
                for sbi in range(nsub):
                    m = min(128, W - sbi * 128)
                    ts = slice(sbi * 128, sbi * 128 + m)
                    for (col0, ncol, kind) in ((OV, 512, "v"), (OV + 512, 512, "v2"), (OO, 512, "o"), (OO + 512, 512, "o2"), (OG, 16, "g")):
                        p, Bp = pr.next()
                        for kc in range(8):
                            S.op("pe", lambda e, p=p, kc=kc, u=u, ts=ts, m=m, col0=col0, ncol=ncol: e.matmul(
                                p[0:m, 0:ncol], u[:, kc, ts], w[:, kc, col0:col0 + ncol], start=(kc == 0), stop=(kc == 7)),
                                reads=[Bw[kc], Bu], writes=[Bp])
                        if kind in ("v", "v2"):
                            o0 = 0 if kind == "v" else 512
                            S.op("dve", lambda e, p=p, m=m, sbi=sbi, o0=o0: e.tensor_copy(out=vt[0:m, sbi, o0:o0 + 512], in_=p[0:m, 0:512]),
                                 reads=[Bp], writes=[Bvt])
                        elif kind in ("o", "o2"):
                            o0 = 0 if kind == "o" else 512
                            S.op("act", lambda e, p=p, m=m, sbi=sbi, o0=o0: e.activation(out=so[0:m, sbi, o0:o0 + 512], in_=p[0:m, 0:512], func=AF.Sigmoid),
                                 reads=[Bp], writes=[Bso])
                        else:
                            S.op("dve", lambda e, p=p, m=m, sbi=sbi: e.tensor_copy(
                                out=gt[0:m, sbi, :].rearrange("p (h k) -> p h k", h=4),
                                in_=p[0:m, 0:16].rearrange("p (k h) -> p h k", h=4)),
                                 reads=[Bp], writes=[Bgt])
                gdst = self.dr["g_s"].rearrange("(h t) k -> t h k", h=4)
                if W == 512:
                    tm = lambda name: self.dr[name][t0:t0 + W, :].rearrange("(s p) c -> p s c", p=128)
                    S.dma("sp", lambda e, tm=tm: e.dma_start(out=tm("v_s"), in_=vt[:]), reads=[Bvt], writes=[self.dbuf("v_s", t0)])
                    S.dma("sp", lambda e, tm=tm: e.dma_start(out=tm("so_s"), in_=so[:]), reads=[Bso], writes=[self.dbuf("so_s", t0)])
                    for sbi in range(4):
                        S.dma("sp", lambda e, sbi=sbi, t0=t0: e.dma_start(
                            out=gdst[t0 + sbi * 128:t0 + (sbi + 1) * 128], in_=gt[:, sbi, :].rearrange("p (h k) -> p h k", h=4)),
                            reads=[Bgt], writes=[self.dbuf("g_s", (t0, sbi))])
                else:
                    S.dma("sp", lambda e, t0=t0, W=W: e.dma_start(out=self.dr["v_s"][t0:t0 + W, :], in_=vt[0:W, 0, :]), reads=[Bvt], writes=[self.dbuf("v_s", t0)])
                    S.dma("sp", lambda e, t0=t0, W=W: e.dma_start(out=self.dr["so_s"][t0:t0 + W, :], in_=so[0:W, 0, :]), reads=[Bso], writes=[self.dbuf("so_s", t0)])
                    S.dma("sp", lambda e, t0=t0, W=W: e.dma_start(
                        out=gdst[t0:t0 + W], in_=gt[0:W, 0, :].rearrange("p (h k) -> p h k", h=4)),
                        reads=[Bgt], writes=[self.dbuf("g_s", (t0, 0))])
            S.barrier()
        with contextlib.ExitStack() as st:
            N2 = NIN - OF
            w, Bw = self.load_w(st, f"w_in{l}", D, N2, c0=OF, tag="wi2")
            ur = Rot([self.sb(st, [128, 8, 512], BF16) for _ in range(2)])
            ft = self.sb(st, [128, 4, 1024], BF16); Bft = Buf()
            glu = self.sb(st, [128, 8, 512], BF16); Bglu = Buf()
            mgr = Rot([self.sb(st, [128, 8, 512], F32) for _ in range(2)])
            sgr = Rot([self.sb(st, [128, 512], F32) for _ in range(2)])
            pr = Rot([self.ps(st, [128, 512]) for _ in range(6)])
            for (t0, W, k) in tiles_of(512):
                u, Bu = ur.next()
                S.dma("sp", lambda e, u=u, t0=t0, W=W: e.dma_start(out=u[:, :, 0:W], in_=self.fm(u_in, t0, W)),
                      reads=[self.dbuf(u_in, t0)], writes=[Bu])
                nsub = (W + 127) // 128
                for sbi in range(nsub):
                    m = min(128, W - sbi * 128)
                    ts = slice(sbi * 128, sbi * 128 + m)
                    for half in range(2):
                        p, Bp = pr.next()
                        c0 = half * 512
                        for kc in range(8):
                            S.op("pe", lambda e, p=p, kc=kc, u=u, ts=ts, m=m, c0=c0: e.matmul(
                                p[0:m, 0:512], u[:, kc, ts], w[:, kc, c0:c0 + 512], start=(kc == 0), stop=(kc == 7)),
                                reads=[Bw[kc], Bu], writes=[Bp])
                        S.op("dve", lambda e, p=p, m=m, sbi=sbi, c0=c0: e.tensor_copy(out=ft[0:m, sbi, c0:c0 + 512], in_=p[0:m, 0:512]),
                             reads=[Bp], writes=[Bft])
                if W == 512:
                    S.dma("sp", lambda e, t0=t0, W=W: e.dma_start(
                        out=self.dr["f_s"][t0:t0 + W, :].rearrange("(s p) c -> p s c", p=128), in_=ft[:]),
                        reads=[Bft], writes=[self.dbuf("f_s", t0)])
                else:
                    S.dma("sp", lambda e, t0=t0, W=W: e.dma_start(out=self.dr["f_s"][t0:t0 + W, :], in_=ft[0:W, 0, :]),
                          reads=[Bft], writes=[self.dbuf("f_s", t0)])
                cv0, cg0, mg0 = OCV - OF, OCG - OF, OMG - OF
                for oc in range(8):
                    pv, Bpv = pr.next()
                    pg, Bpg = pr.next()
                    for kc in range(8):
                        S.op("pe", lambda e, pv=pv, kc=kc, oc=oc, u=u, W=W: e.matmul(
                            pv[:, 0:W], w[:, kc, cv0 + oc * 128:cv0 + (oc + 1) * 128], u[:, kc, 0:W], start=(kc == 0), stop=(kc == 7)),
                            reads=[Bw[kc], Bu], writes=[Bpv])
                    for kc in range(8):
                        S.op("pe", lambda e, pg=pg, kc=kc, oc=oc, u=u, W=W: e.matmul(
                            pg[:, 0:W], w[:, kc, cg0 + oc * 128:cg0 + (oc + 1) * 128], u[:, kc, 0:W], start=(kc == 0), stop=(kc == 7)),
                            reads=[Bw[kc], Bu], writes=[Bpg])
                    sg, Bsg = sgr.next()
                    S.op("act", lambda e, sg=sg, pg=pg, W=W: e.activation(out=sg[:, 0:W], in_=pg[:, 0:W], func=AF.Sigmoid),
                         reads=[Bpg], writes=[Bsg])
                    S.op("dve", lambda e, sg=sg, pv=pv, oc=oc, W=W: e.tensor_tensor(out=glu[:, oc, 0:W], in0=pv[:, 0:W], in1=sg[:, 0:W], op=ALU.mult),
                         reads=[Bpv, Bsg], writes=[Bglu])
                S.dma("sp", lambda e, t0=t0, W=W: e.dma_start(out=self.fm("glu_s", t0, W), in_=glu[:, :, 0:W]),
                      reads=[Bglu], writes=[self.dbuf("glu_s", t0)])
                for br in range(3):
                    mg, Bmg = mgr.next()
                    for oc in range(8):
                        p, Bp = pr.next()
                        c0 = mg0 + br * 1024 + oc * 128
                        for kc in range(8):
                            S.op("pe", lambda e, p=p, kc=kc, c0=c0, u=u, W=W: e.matmul(
                                p[:, 0:W], w[:, kc, c0:c0 + 128], u[:, kc, 0:W], start=(kc == 0), stop=(kc == 7)),
                                reads=[Bw[kc], Bu], writes=[Bp])
                        S.op("act", lambda e, p=p, mg=mg, oc=oc, W=W: e.activation(out=mg[:, oc, 0:W], in_=p[:, 0:W], func=AF.Sigmoid),
                             reads=[Bp], writes=[Bmg])
                    S.dma("sp", lambda e, mg=mg, br=br, t0=t0, W=W: e.dma_start(
                        out=self.dr["mg"][br * 1024:(br + 1) * 1024, :].rearrange("(c p) t -> p c t", p=128)[:, :, t0:t0 + W],
                        in_=mg[:, :, 0:W]), reads=[Bmg], writes=[self.dbuf("mg", (t0, br))])
            S.barrier()

    def step_postmix(self, l, xa_in, xa_out, u_out, ep):
        S = self.S
        WP = 256
        with contextlib.ExitStack() as st:
            wb = [self.load_w(st, f"wb_{l}{br}", D, D, tag=f"wb{br}") for br in range(3)]
            wo, Bwo = self.load_w(st, f"wo_{l}", D, D, tag="wo")
            cvr = Rot([self.sb(st, [128, 8, WP], F32) for _ in range(2)])
            har = Rot([self.sb(st, [128, 8, WP], BF16) for _ in range(2)])
            hbr = Rot([self.sb(st, [128, 8, WP], BF16) for _ in range(2)])
            mgr = Rot([self.sb(st, [128, 24, WP], F32) for _ in range(1)])
            xr = Rot([self.sb(st, [128, 8, WP], F32) for _ in range(2)])
            prod = Rot([self.sb(st, [128, WP], F32) for _ in range(2)])
            hc = self.sb(st, [128, 8, WP], BF16); Bhc = Buf()
            mer = self.sb(st, [128, 8, WP], BF16); Bmer = Buf()
            macc = Rot([self.sb(st, [128, WP], F32) for _ in range(2)])
            z = self.sb(st, [128, 8, WP], F32); Bz = Buf()
            xo = self.sb(st, [128, 8, WP], F32); Bxo = Buf()
            uo = self.sb(st, [128, 8, WP], BF16); Buo = Buf()
            lnt = self.alloc_ln(st, WP)
            pr = Rot([self.ps(st, [128, 512]) for _ in range(6)])
            cg, cb = self.v(f"cn_g{l}"), self.v(f"cn_b{l}")
            for (t0, W, k) in tiles_of(WP):
                cv, Bcv = cvr.next(); ha, Bha = har.next(); hb, Bhb = hbr.next(); mg, Bmg = mgr.next(); x, Bx = xr.next()
                S.dma("sp", lambda e, cv=cv, t0=t0, W=W: e.dma_start(out=cv[:, :, 0:W], in_=self.fm("cv_tp", t0, W)), writes=[Bcv])
                S.dma("sp", lambda e, ha=ha, t0=t0, W=W: e.dma_start(out=ha[:, :, 0:W], in_=self.fm("ha_tp", t0, W)), writes=[Bha])
                S.dma("sp", lambda e, hb=hb, t0=t0, W=W: e.dma_start(out=hb[:, :, 0:W], in_=self.fm("hb_tp", t0, W)), writes=[Bhb])
                S.dma("sp", lambda e, mg=mg, t0=t0, W=W: e.dma_start(out=mg[:, :, 0:W], in_=self.fm("mg", t0, W)), writes=[Bmg])
                S.dma("sp", lambda e, x=x, t0=t0, W=W: e.dma_start(out=x[:, :, 0:W], in_=self.fm(xa_in, t0, W)), writes=[Bx])
                mean, Bmean, rstd, Brstd = self.ln_stats(lnt, cv, Bcv, W)
                for c in range(8):
                    S.op("dve", lambda e, c=c, cv=cv, W=W: e.tensor_tensor(out=cv[:, c, 0:W], in0=cv[:, c, 0:W], in1=mean[:, 0:W], op=ALU.subtract),
                         reads=[Bcv, Bmean], writes=[Bcv])
                    S.op("dve", lambda e, c=c, cv=cv, W=W: e.tensor_tensor(out=cv[:, c, 0:W], in0=cv[:, c, 0:W], in1=rstd[:, 0:W], op=ALU.mult),
                         reads=[Bcv, Brstd], writes=[Bcv])
                    S.op("act", lambda e, c=c, cv=cv, W=W: e.activation(out=hc[:, c, 0:W], in_=cv[:, c, 0:W], func=AF.Silu,
                                                                        bias=cb[:, c:c + 1], scale=cg[:, c:c + 1]),
                         reads=[Bcv, self.Bvecs], writes=[Bhc])
                hs = [(ha, Bha), (hb, Bhb), (hc, Bhc)]
                for dc in range(8):
                    ac, Bac = macc.next()
                    for br in range(3):
                        p, Bp = pr.next()
                        wbt, Bwb = wb[br]
                        h, Bh = hs[br]
                        for kc in range(8):
                            S.op("pe", lambda e, p=p, kc=kc, dc=dc, wbt=wbt, h=h, W=W: e.matmul(
                                p[:, 0:W], wbt[:, kc, dc * 128:(dc + 1) * 128], h[:, kc, 0:W], start=(kc == 0), stop=(kc == 7)),
                                reads=[Bwb[kc], Bh], writes=[Bp])
                        if br == 0:
                            S.op("dve", lambda e, p=p, ac=ac, mg=mg, dc=dc, W=W: e.tensor_tensor(
                                out=ac[:, 0:W], in0=p[:, 0:W], in1=mg[:, dc, 0:W], op=ALU.mult), reads=[Bp, Bmg], writes=[Bac])
                        else:
                            pd, Bpd = prod.next()
                            S.op("dve", lambda e, p=p, mg=mg, dc=dc, br=br, W=W, pd=pd: e.tensor_tensor(
                                out=pd[:, 0:W], in0=p[:, 0:W], in1=mg[:, br * 8 + dc, 0:W], op=ALU.mult), reads=[Bp, Bmg], writes=[Bpd])
                            if br == 1:
                                S.op("pool", lambda e, ac=ac, pd=pd, W=W: e.tensor_tensor(
                                    out=ac[:, 0:W], in0=ac[:, 0:W], in1=pd[:, 0:W], op=ALU.add), reads=[Bac, Bpd], writes=[Bac])
                            else:
                                S.op("pool", lambda e, ac=ac, dc=dc, pd=pd, W=W: e.tensor_tensor(
                                    out=mer[:, dc, 0:W], in0=ac[:, 0:W], in1=pd[:, 0:W], op=ALU.add), reads=[Bac, Bpd], writes=[Bmer])
                gr = self.d(f"{ep}_gr{k}")
                for dc in range(8):
                    py, Bpy = pr.next()
                    for kc in range(8):
                        S.op("pe", lambda e, py=py, kc=kc, dc=dc, W=W: e.matmul(
                            py[:, 0:W], wo[:, kc, dc * 128:(dc + 1) * 128], mer[:, kc, 0:W], start=(kc == 0), stop=(kc == 7)),
                            reads=[Bwo[kc], Bmer], writes=[Bpy])
                    S.op("dve", lambda e, py=py, dc=dc, x=x, W=W, gr=gr: e.scalar_tensor_tensor(
                        out=z[:, dc, 0:W], in0=py[:, 0:W], scalar=gr[:, dc:dc + 1], in1=x[:, dc, 0:W], op0=ALU.mult, op1=ALU.add),
                        reads=[Bpy, Bx, self.Bder], writes=[Bz])
                self.ln_out(lnt, z, Bz, W, k, ep, t0, xa_out, u_out, xo, Bxo, uo, Buo, False)
            S.barrier()


NCH = TS // 128
SEQS = ((0, TCX), (TCX, T))
MP_QKW, MP_DWW, MP_DWB, MP_BG, MP_N = 0, 20, 82, 84, 88


class MixMixin:
    def ag(self, tag, src_ap, Bsrcs, rows, cols, dt, dst_cols=None):
        S = self.S
        if ("pk_" + tag) not in self.dr:
            self.scratch("pk_" + tag, (rows, cols), dt)
            self.scratch("gg_" + tag, (8 * rows, cols), dt)
        pk, g = self.dr["pk_" + tag], self.dr["gg_" + tag]
        nb = rows * cols * (4 if dt == F32 else 2)
        assert nb & (nb - 1) == 0 and 131072 <= nb <= 4194304, (tag, nb)
        Bpk, Bgg = Buf(), Buf()
        if callable(src_ap):
            dst, src_ap = src_ap(pk)
        else:
            dst = pk if dst_cols is None else pk[:, 0:dst_cols]
        S.dma("sp", lambda e: e.dma_start(out=dst, in_=src_ap), reads=Bsrcs, writes=[Bpk])
        S.cc(lambda e: e.collective_compute("AllGather", ALU.bypass, replica_groups=[list(range(NCORE))],
                                            ins=[pk], outs=[g]), reads=[Bpk], writes=[Bgg])
        self.Bg[tag] = Bgg
        return g, Bgg

    def exchange_A(self):
        S = self.S
        dr = self.dr
        nb = []
        G = {}
        G["q"] = self.ag("q", dr["qk_s"][0:1024, 0:TL], nb, 1024, TL, BF16)
        G["k"] = self.ag("k", dr["qk_s"][1024:2048, 0:TL], nb, 1024, TL, BF16)
        G["qkc"] = self.ag("qkc", dr["qk_s"][:, TL:NT], nb, 2048, 2 * CL, BF16, dst_cols=CL)
        G["glu"] = self.ag("glu", dr["glu_s"][:, 0:TL], nb, 1024, TL, BF16)
        G["gluc"] = self.ag("gluc", dr["glu_s"][:, TL:NT], nb, 1024, 2 * CL, BF16, dst_cols=CL)
        for nm, dt in (("v", BF16), ("f", BF16)):
            G[nm] = self.ag(nm, dr[nm + "_s"][0:TL, :], nb, TL, 1024, dt)
            G[nm + "c"] = self.ag(nm + "c", dr[nm + "_s"][TL:NT, :], nb, CL, 1024, dt)
        G["so0"] = self.ag("so0", dr["so_s"][0:1024, :], nb, 1024, 1024, F32)
        G["so1"] = self.ag("so1", dr["so_s"][1024:2048, :], nb, 1024, 1024, F32)
        G["soc"] = self.ag("soc", dr["so_s"][TL:NT, :], nb, CL, 1024, F32)
        gsv = dr["g_s"].rearrange("(h t) k -> h t k", h=4)
        G["g"] = self.ag("g", lambda pk: (pk.rearrange("(h t) k -> h t k", h=4), gsv[:, 0:TL, :]), nb, 4 * TL, 4, F32)
        G["gc"] = self.ag("gc", lambda pk: (pk.rearrange("(h t) k -> h t k", h=4)[:, :, 0:4], gsv[:, TL:NT, :]), nb, 4 * CL, 128, F32)
        B1, J1 = bass.ds(self.bat, 1), bass.ds(self.seg, 1)

        def cp(dst, src, Bgg):
            S.dma("sp", lambda e: e.dma_start(out=dst, in_=src), reads=[Bgg], writes=[Buf()])
        for half, nm in ((0, "q"), (1, "k")):
            g, Bgg = G[nm]
            v5 = g.rearrange("(b s j p) c -> b s j p c", b=2, s=4, j=4)
            cp(dr["qk_loc"][half * 256:(half + 1) * 256, TCX:TS].rearrange("p (s t) -> s p t", s=4),
               v5[B1, :, J1, :, :].rearrange("a s j p c -> (a s) (j p) c"), Bgg)
        g, Bgg = G["qkc"]
        v6 = g.rearrange("(b s h j p) c -> b s h j p c", b=2, s=4, h=2, j=4)
        cp(dr["qk_loc"][:, 0:TCX].rearrange("(h p) (s t) -> s h p t", h=2, s=4),
           v6[B1, :, :, J1, :, 0:CL].rearrange("a s h j p c -> (a s) h (j p) c"), Bgg)
        g, Bgg = G["glu"]
        v5 = g.rearrange("(b s j p) c -> b s j p c", b=2, s=4, j=4)
        cp(dr["glu_loc"][:, TCX:TS].rearrange("p (s t) -> s p t", s=4), v5[B1, :, J1, :, :].rearrange("a s j p c -> (a s) (j p) c"), Bgg)
        g, Bgg = G["gluc"]
        v5 = g.rearrange("(b s j p) c -> b s j p c", b=2, s=4, j=4)
        cp(dr["glu_loc"][:, 0:TCX].rearrange("p (s t) -> s p t", s=4), v5[B1, :, J1, :, 0:CL].rearrange("a s j p c -> (a s) (j p) c"), Bgg)
        for nm in ("v", "f"):
            g, Bgg = G[nm]
            v5 = g.rearrange("(b s t) (j c) -> b s t j c", b=2, s=4, j=4)
            cp(dr[nm + "_loc"][TCX:TS, :].rearrange("(s t) c -> s t c", s=4), v5[B1, :, :, J1, :].rearrange("a s t j c -> (a s) t (j c)"), Bgg)
            g, Bgg = G[nm + "c"]
            v5 = g.rearrange("(b s t) (j c) -> b s t j c", b=2, s=4, j=4)
            cp(dr[nm + "_loc"][0:TCX, :].rearrange("(s t) c -> s t c", s=4), v5[B1, :, :, J1, :].rearrange("a s t j c -> (a s) t (j c)"), Bgg)
        for hh in range(2):
            g, Bgg = G[f"so{hh}"]
            v5 = g.rearrange("(b s t) (j c) -> b s t j c", b=2, s=4, j=4)
            cp(dr["so_loc"][TCX:TS, :].rearrange("(s h t) c -> s h t c", s=4, h=2)[:, hh],
               v5[B1, :, :, J1, :].rearrange("a s t j c -> (a s) t (j c)"), Bgg)
        g, Bgg = G["soc"]
        v5 = g.rearrange("(b s t) (j c) -> b s t j c", b=2, s=4, j=4)
        cp(dr["so_loc"][0:TCX, :].rearrange("(s t) c -> s t c", s=4), v5[B1, :, :, J1, :].rearrange("a s t j c -> (a s) t (j c)"), Bgg)
        g, Bgg = G["g"]
        v5 = g.rearrange("(b s h t) k -> b s h t k", b=2, s=4, h=4)
        cp(dr["g_loc"][TCX:TS, :].rearrange("(s t) k -> s t k", s=4), v5[B1, :, J1, :, :].rearrange("a s h t k -> (a s) (h t) k"), Bgg)
        g, Bgg = G["gc"]
        v5 = g.rearrange("(b s h t) k -> b s h t k", b=2, s=4, h=4)
        cp(dr["g_loc"][0:TCX, :].rearrange("(s t) k -> s t k", s=4), v5[B1, :, J1, :, 0:4].rearrange("a s h t k -> (a s) (h t) k"), Bgg)
        S.barrier()
        self.Bloc = {n: Buf() for n in ("qk", "v", "so", "g", "f", "glu")}

    def exchange_B(self):
        dr = self.dr
        for nm, dt in (("ha", BF16), ("hb", BF16)):
            self.ag(nm + "l", dr[nm + "_s"][:, TCX:TS], self.Bsend[nm], 256, T, dt)
            self.ag(nm + "c", dr[nm + "_s"][:, 0:TCX], self.Bsend[nm], 256, TCX, dt)
        self.ag("cvl0", dr["cv_s"][0:128, TCX:TS], self.Bsend["cv"], 128, T, F32)
        self.ag("cvl1", dr["cv_s"][128:256, TCX:TS], self.Bsend["cv"], 128, T, F32)
        self.ag("cvc", dr["cv_s"][:, 0:TCX], self.Bsend["cv"], 256, TCX, F32)
        self.unpack_B()

    def unpack_B(self):
        S = self.S
        B1, J1 = bass.ds(self.bat, 1), bass.ds(self.seg, 1)

        def cp(dst, src, Bgg):
            S.dma("sp", lambda e: e.dma_start(out=dst, in_=src), reads=[Bgg], writes=[Buf()])
        for nm in ("ha", "hb"):
            d3 = self.dr[nm + "_tp"].rearrange("(c p) t -> c p t", p=128)
            g = self.dr["gg_" + nm + "l"].rearrange("(b c p) (s t) -> b c p s t", b=2, c=8, s=4)
            cp(d3[:, :, 0:TL], g[B1, :, :, J1, :].rearrange("a c p s t -> (a c) p (s t)"), self.Bg[nm + "l"])
            g = self.dr["gg_" + nm + "c"].rearrange("(b c p) (s t) -> b c p s t", b=2, c=8, s=4)
            cp(d3[:, :, TL:NT], g[B1, :, :, J1, :].rearrange("a c p s t -> (a c) p (s t)"), self.Bg[nm + "c"])
        d4 = self.dr["cv_tp"].rearrange("(r h p) t -> r h p t", h=2, p=128)
        for hh in range(2):
            g = self.dr[f"gg_cvl{hh}"].rearrange("(b r p) (s t) -> b r p s t", b=2, r=4, s=4)
            cp(d4[:, hh, :, 0:TL], g[B1, :, :, J1, :].rearrange("a r p s t -> (a r) p (s t)"), self.Bg[f"cvl{hh}"])
        d3 = self.dr["cv_tp"].rearrange("(c p) t -> c p t", p=128)
        g = self.dr["gg_cvc"].rearrange("(b c p) (s t) -> b c p s t", b=2, c=8, s=4)
        cp(d3[:, :, TL:NT], g[B1, :, :, J1, :].rearrange("a c p s t -> (a c) p (s t)"), self.Bg["cvc"])
        S.barrier()

    def load_mix_consts(self):
        S = self.S
        st = self.top
        self.cst = self.sb(st, [128, 5, 128], F32, "cst")
        self.Bcst = Buf()
        S.dma("sp", lambda e: e.dma_start(out=self.cst[:], in_=self.dr["cmask"]), writes=[self.Bcst])
        self.cb = self.sb(st, [128, 4, 128], BF16, "cstb")
        self.Bcb = Buf()
        S.op("dve", lambda e: e.tensor_copy(out=self.cb[:, 0:3, :], in_=self.cst[:, 0:3, :]), reads=[self.Bcst], writes=[self.Bcb])
        S.op("dve", lambda e: e.memset(self.cb[:, 3, :], 1.0), writes=[self.Bcb])
        self.mixp = self.sb(st, [128, DEPTH, MP_N], F32, "mixp")
        self.Bmixp = Buf()
        S.dma("sp", lambda e: e.dma_start(out=self.mixp[:], in_=self.dr["mixp"]), writes=[self.Bmixp])
        self.mhg = self.sb(st, [128, DEPTH, 256], F32, "mhg")
        self.Bmhg = Buf()
        S.dma("sp", lambda e: e.dma_start(out=self.mhg[:], in_=self.dr["mhg"]), writes=[self.Bmhg])

    def dwconv(self, l, src, nch, ktaps, wcol0, scale_fn, emit_out):
        S = self.S
        half = ktaps // 2
        with contextlib.ExitStack() as st:
            dg = self.sb(st, [128, nch * ktaps, 128], BF16, "dg")
            Bdg = Buf()
            for idx in range(nch * ktaps):
                S.op("dve", lambda e, idx=idx: e.tensor_scalar(
                    out=dg[:, idx, :], in0=self.cst[:, 0, :], scalar1=self.mixp[:, l, wcol0 + idx:wcol0 + idx + 1],
                    scalar2=scale_fn(idx // ktaps), op0=ALU.mult, op1=ALU.mult),
                    reads=[self.Bcst, self.Bmixp], writes=[Bdg])
            xr = Rot([self.sb(st, [128, nch, 512 + 2 * half], BF16) for _ in range(2)])
            pr = Rot([self.ps(st, [128, 512]) for _ in range(4)])
            srcv = self.dr[src].rearrange("(c p) t -> p c t", p=128)
            for (off, L) in SEQS:
                for t0 in range(0, L, 512):
                    n = min(512, L - t0)
                    x, Bx = xr.next()
                    lo, hi = max(t0 - half, 0), min(t0 + n + half, L)
                    if lo > t0 - half or hi < t0 + n + half:
                        S.op("pool", lambda e, x=x: e.memset(x[:], 0.0), writes=[Bx])
                    d0 = lo - (t0 - half)
                    S.dma("sp", lambda e, x=x, d0=d0, lo=lo, hi=hi, off=off: e.dma_start(
                        out=x[:, :, d0:d0 + hi - lo], in_=srcv[:, :, off + lo:off + hi]),
                        reads=[self.Bloc_src], writes=[Bx])
                    for ch in range(nch):
                        p, Bp = pr.next()
                        for jt in range(ktaps):
                            S.op("pe", lambda e, p=p, ch=ch, jt=jt, x=x, n=n: e.matmul(
                                p[:, 0:n], dg[:, ch * ktaps + jt, :], x[:, ch, jt:jt + n], start=(jt == 0), stop=(jt == ktaps - 1)),
                                reads=[Bdg, Bx], writes=[Bp])
                        emit_out(p, Bp, ch, off, t0, n)

    def mlstm(self, l):
        S = self.S
        dr = self.dr
        hF = dr["hF"]; hB = dr["hB"]
        self.BhF = [Buf() for _ in range(NCH)]
        self.BhB = [Buf() for _ in range(NCH)]
        with contextlib.ExitStack() as st:
            qkc = self.sb(st, [128, 4, TS], BF16, "qkc")
            Bqkc = Buf()
            self.Bloc_src = self.Bloc["qk"]
            cnt = [0]

            def out_qk(p, Bp, ch, off, t0, n):
                cnt[0] += 1
                if cnt[0] % 2:
                    S.op("act", lambda e: e.activation(out=qkc[:, ch, off + t0:off + t0 + n], in_=p[:, 0:n], func=AF.Copy),
                         reads=[Bp], writes=[Bqkc])
                else:
                    S.op("dve", lambda e: e.tensor_copy(out=qkc[:, ch, off + t0:off + t0 + n], in_=p[:, 0:n]),
                         reads=[Bp], writes=[Bqkc])
            self.dwconv(l, "qk_loc", 4, 5, MP_QKW, lambda ch: (DH ** -0.5 if ch >= 2 else 1.0), out_qk)
            S.barrier()
            vaug = self.sb(st, [128, NCH, 257], BF16, "vaug"); Bv = Buf()
            S.dma("sp", lambda e: e.dma_start(out=vaug[:, :, 0:256], in_=dr["v_loc"].rearrange("(c p) d -> p c d", p=128)),
                  reads=[self.Bloc["v"]], writes=[Bv])
            S.op("pool", lambda e: e.memset(vaug[:, :, 256:257], 1.0), writes=[Bv])
            G = self.sb(st, [128, NCH, 4], F32, "G"); BG = Buf()
            S.dma("sp", lambda e: e.dma_start(out=G[:], in_=dr["g_loc"].rearrange("(c p) g -> p c g", p=128)),
                  reads=[self.Bloc["g"]], writes=[BG])
            for col in range(4):
                S.op("dve", lambda e, col=col: e.tensor_scalar_add(
                    out=G[:, :, col], in0=G[:, :, col], scalar1=self.mixp[:, l, MP_BG + col:MP_BG + col + 1]),
                    reads=[BG, self.Bmixp], writes=[BG])
            tmp = self.sb(st, [128, NCH], F32); Btmp = Buf()
            for col in (1, 3):
                S.op("act", lambda e, col=col: e.activation(out=tmp[:], in_=G[:, :, col], func=AF.Exp, scale=-1.0), reads=[BG], writes=[Btmp])
                S.op("act", lambda e: e.activation(out=tmp[:], in_=tmp[:], func=AF.Ln, bias=1.0), reads=[Btmp], writes=[Btmp])
                S.op("dve", lambda e, col=col: e.tensor_scalar_mul(out=G[:, :, col], in0=tmp[:], scalar1=-1.0), reads=[Btmp], writes=[BG])
            NG = NCH * 4
            Gf = G[:].rearrange("p c g -> p (c g)")
            Ghi = self.sb(st, [128, NG], BF16); Glo = self.sb(st, [128, NG], BF16); Gd = self.sb(st, [128, NG], F32)
            BGh = Buf()
            S.op("dve", lambda e: e.tensor_copy(out=Ghi[:], in_=Gf), reads=[BG], writes=[BGh])
            S.op("dve", lambda e: e.tensor_tensor(out=Gd[:], in0=Gf, in1=Ghi[:], op=ALU.subtract), reads=[BG, BGh], writes=[BGh])
            S.op("dve", lambda e: e.tensor_copy(out=Glo[:], in_=Gd[:]), reads=[BGh], writes=[BGh])
            cum = self.sb(st, [128, 3, NCH, 4], F32, "cum"); Bcum = Buf()
            with contextlib.ExitStack() as st2:
                pc = self.ps(st2, [128, 512]); Bpc = Buf()
                for i, m in enumerate((1, 2, 3)):
                    S.op("pe", lambda e, m=m: e.matmul(pc[:, 0:NG], self.cb[:, m, :], Ghi[:], start=True, stop=False), reads=[self.Bcb, BGh], writes=[Bpc])
                    S.op("pe", lambda e, m=m: e.matmul(pc[:, 0:NG], self.cb[:, m, :], Glo[:], start=False, stop=True), reads=[self.Bcb, BGh], writes=[Bpc])
                    S.op("dve", lambda e, i=i: e.tensor_copy(out=cum[:, i].rearrange("p c g -> p (c g)"), in_=pc[:, 0:NG]), reads=[Bpc], writes=[Bcum])
                S.barrier()
            sm = self.sb(st, [128, 2, 5, NCH], F32, "sm"); Bsm = Buf()
            bTs = [(self.sb(st, [NCH, 2, 128], BF16, "bT"), Buf()) for _ in range(2)]
            with contextlib.ExitStack() as st2:
                pT = self.ps(st2, [128, 128], BF16); BpT = Buf()
                hl = self.sb(st2, [128, 2, NCH], BF16); Bhl = Buf()
                for d in range(2):
                    ci, cf = 2 * d, 2 * d + 1
                    bcol = cum[:, d, :, cf]
                    S.op("dve", lambda e, d=d, ci=ci, bcol=bcol: e.tensor_tensor(out=sm[:, d, 0, :], in0=G[:, :, ci], in1=bcol, op=ALU.subtract),
                         reads=[BG, Bcum], writes=[Bsm])
                    S.op("dve", lambda e, d=d, cf=cf: e.tensor_tensor(out=sm[:, d, 4, :], in0=sm[:, d, 0, :], in1=cum[:, 2, :, cf], op=ALU.add),
                         reads=[Bsm, Bcum], writes=[Bsm])
                    S.op("act", lambda e, d=d: e.activation(out=sm[:, d, 1, :], in_=sm[:, d, 4, :], func=AF.Exp), reads=[Bsm], writes=[Bsm])
                    S.op("act", lambda e, d=d, cf=cf: e.activation(out=sm[:, d, 2, :], in_=cum[:, 2, :, cf], func=AF.Exp), reads=[Bcum], writes=[Bsm])
                    S.op("dve", lambda e, bcol=bcol: e.tensor_copy(out=hl[:, 0, :], in_=bcol), reads=[Bcum], writes=[Bhl])
                    S.op("dve", lambda e, d=d, bcol=bcol: e.tensor_tensor(out=sm[:, d, 3, :], in0=bcol, in1=hl[:, 0, :], op=ALU.subtract),
                         reads=[Bcum, Bhl], writes=[Bsm])
                    S.op("dve", lambda e, d=d: e.tensor_copy(out=hl[:, 1, :], in_=sm[:, d, 3, :]), reads=[Bsm], writes=[Bhl])
                    bT, BbT = bTs[d]
                    for h in range(2):
                        S.op("pe", lambda e, h=h: e.transpose(pT[0:NCH, :], hl[:, h, :], self.cb[:, 0, :]), reads=[Bhl, self.Bcb], writes=[BpT])
                        S.op("act", lambda e, h=h, bT=bT: e.activation(out=bT[:, h, :], in_=pT[0:NCH, :], func=AF.Copy), reads=[BpT], writes=[BbT])
                S.barrier()
            oneh = self.sb(st, [NCH, TS], BF16, "oneh"); Boh = Buf()
            S.dma("pool", lambda e: e.dma_start(out=oneh[:], in_=dr["onehot"]), writes=[Boh])
            ktok = self.sb(st, [128, NCH, 256], BF16, "ktok"); Bkt = Buf()
            with contextlib.ExitStack() as st2:
                ptr = Rot([self.ps(st2, [128, 128], BF16) for _ in range(4)])
                for c in range(NCH):
                    for dch in range(2):
                        p, Bp = ptr.next()
                        S.op("pe", lambda e, p=p, c=c, dch=dch: e.transpose(p[:], qkc[:, 2 + dch, c * 128:(c + 1) * 128], self.cb[:, 0, :]),
                             reads=[Bqkc, self.Bcb], writes=[Bp])
                        if (c + dch) % 2:
                            S.op("act", lambda e, p=p, c=c, dch=dch: e.activation(out=ktok[:, c, dch * 128:(dch + 1) * 128], in_=p[:], func=AF.Copy), reads=[Bp], writes=[Bkt])
                        else:
                            S.op("dve", lambda e, p=p, c=c, dch=dch: e.tensor_copy(out=ktok[:, c, dch * 128:(dch + 1) * 128], in_=p[:]), reads=[Bp], writes=[Bkt])
                S.barrier()
            dirs = []
            for d in range(2):
                C32 = self.sb(st, [128, 2, 257], F32, "C32"); Cb = self.sb(st, [128, 2, 257], BF16, "Cb")
                BC32, BCb = Buf(), Buf()
                S.op("pool", lambda e, C32=C32: e.memset(C32[:], 0.0), writes=[BC32])
                S.op("pool", lambda e, Cb=Cb: e.memset(Cb[:], 0.0), writes=[BCb])
                bankA = self.ps(st, [128, 512]); pnum = self.ps(st, [128, 512]); pst0 = self.ps(st, [128, 512]); pst1 = self.ps(st, [128, 512])
                dirs.append(dict(C32=C32, Cb=Cb, BC32=BC32, BCb=BCb, bankA=bankA, BST=Buf(), BBt=Buf(), pnum=pnum, Bnum=Buf(),
                                 pst=(pst0, pst1), Bst=(Buf(), Buf()),
                                 tmp=self.sb(st, [128, 128], F32), Btmp=Buf(), DT=self.sb(st, [128, 128], F32), BDT=Buf(),
                                 PT=self.sb(st, [128, 128], BF16), BPT=Buf(), eBt=self.sb(st, [128, 128], F32), BeBt=Buf(),
                                 qs=self.sb(st, [128, 2, 128], BF16), Bqs=Buf(), vw=self.sb(st, [128, 257], BF16), Bvw=Buf(),
                                 dn=self.sb(st, [128, 2], F32), Bdn=Buf(),
                                 hr=Rot([self.sb(st, [128, 256], F32) for _ in range(2)])))
            orderF = list(range(NCH))
            orderB = [1, 0] + list(range(NCH - 1, 1, -1))
            for step in range(NCH):
                for d, c in ((0, orderF[step]), (1, orderB[step])):
                    X = dirs[d]
                    bT, BbT = bTs[d]
                    cs = slice(c * 128, (c + 1) * 128)
                    ST = X["bankA"][:, 0:128]; Bt = X["bankA"][:, 128:256]
                    for dch in range(2):
                        S.op("pe", lambda e, dch=dch: e.matmul(ST, qkc[:, 2 + dch, cs], qkc[:, dch, cs], start=(dch == 0), stop=(dch == 1)),
                             reads=[Bqkc], writes=[X["BST"]])
                    for h in range(2):
                        S.op("pe", lambda e, h=h: e.matmul(Bt, oneh[:, cs], bT[:, h, :], start=(h == 0), stop=(h == 1)),
                             reads=[Boh, BbT], writes=[X["BBt"]])
                    S.op("dve", lambda e: e.tensor_tensor(out=X["tmp"][:], in0=Bt, in1=self.cst[:, 3 + d, :], op=ALU.add),
                         reads=[X["BBt"], self.Bcst], writes=[X["Btmp"]])
                    S.op("act", lambda e: e.activation(out=X["DT"][:], in_=X["tmp"][:], func=AF.Exp, bias=sm[:, d, 0, c:c + 1]),
                         reads=[X["Btmp"], Bsm], writes=[X["BDT"]])
                    S.op("dve", lambda e: e.tensor_tensor(out=X["PT"][:], in0=ST, in1=X["DT"][:], op=ALU.mult),
                         reads=[X["BST"], X["BDT"]], writes=[X["BPT"]])
                    S.op("act", lambda e: e.activation(out=X["eBt"][:], in_=Bt, func=AF.Exp), reads=[X["BBt"]], writes=[X["BeBt"]])
                    for dch in range(2):
                        S.op("dve", lambda e, dch=dch: e.tensor_tensor(out=X["qs"][:, dch, :], in0=qkc[:, dch, cs], in1=X["eBt"][:], op=ALU.mult),
                             reads=[Bqkc, X["BeBt"]], writes=[X["Bqs"]])
                    S.op("dve", lambda e: e.tensor_scalar_mul(out=X["vw"][:], in0=vaug[:, c, :], scalar1=sm[:, d, 1, c:c + 1]),
                         reads=[Bv, Bsm], writes=[X["Bvw"]])
                    num = X["pnum"][:, 0:257]
                    S.op("pe", lambda e: e.matmul(num, X["PT"][:], vaug[:, c, :], start=True, stop=False),
                         reads=[X["BPT"], Bv], writes=[X["Bnum"]])
                    for dch in range(2):
                        S.op("pe", lambda e, dch=dch: e.matmul(num, X["qs"][:, dch, :], X["Cb"][:, dch, :], start=False, stop=(dch == 1)),
                             reads=[X["Bqs"], X["BCb"]], writes=[X["Bnum"]])
                    S.op("act", lambda e: e.activation(out=X["dn"][:, 0:1], in_=X["pnum"][:, 256:257], func=AF.Abs),
                         reads=[X["Bnum"]], writes=[X["Bdn"]])
                    S.op("dve", lambda e: e.tensor_scalar_max(out=X["dn"][:, 0:1], in0=X["dn"][:, 0:1], scalar1=1.0),
                         reads=[X["Bdn"]], writes=[X["Bdn"]])
                    S.op("dve", lambda e: e.reciprocal(out=X["dn"][:, 1:2], in_=X["dn"][:, 0:1]), reads=[X["Bdn"]], writes=[X["Bdn"]])
                    h, Bh = X["hr"].next()
                    S.op("act", lambda e, h=h: e.activation(out=h[:], in_=X["pnum"][:, 0:256], func=AF.Identity, scale=X["dn"][:, 1:2]),
                         reads=[X["Bnum"], X["Bdn"]], writes=[Bh])
                    S.dma("sp", lambda e, h=h: e.dma_start(out=(hF if d == 0 else hB)[cs, :], in_=h[:]),
                          reads=[Bh], writes=[(self.BhF if d == 0 else self.BhB)[c]])
                    for dch in range(2):
                        S.op("pe", lambda e, dch=dch: e.matmul(X["pst"][dch][:, 0:257], ktok[:, c, dch * 128:(dch + 1) * 128], X["vw"][:],
                                                               start=True, stop=True), reads=[Bkt, X["Bvw"]], writes=[X["Bst"][dch]])
                    for dch in range(2):
                        S.op("dve", lambda e, dch=dch: e.scalar_tensor_tensor(
                            out=X["C32"][:, dch, :], in0=X["C32"][:, dch, :], scalar=sm[:, d, 2, c:c + 1], in1=X["pst"][dch][:, 0:257],
                            op0=ALU.mult, op1=ALU.add), reads=[X["BC32"], X["Bst"][dch], Bsm], writes=[X["BC32"]])
                    S.op("act", lambda e: e.activation(out=X["Cb"][:], in_=X["C32"][:], func=AF.Copy), reads=[X["BC32"]], writes=[X["BCb"]])
            S.barrier()

    def headnorm(self, l):
        S = self.S
        dr = self.dr
        with contextlib.ExitStack() as st:
            hs = self.sb(st, [128, NCH, 256], F32, "hs"); Bhs = Buf()
            s12 = self.sb(st, [128, 4, NCH], F32, "s12"); Bs = Buf()
            junk = self.sb(st, [128, 256], F32); Bj = Buf()
            S.op("dve", lambda e: e.memset(s12[:], 0.0), writes=[Bs])
            ar = Rot([self.sb(st, [128, 256], F32) for _ in range(2)])
            br = Rot([self.sb(st, [128, 256], F32) for _ in range(2)])
            for c in range(NCH):
                cs = slice(c * 128, (c + 1) * 128)
                a, Ba = ar.next(); b, Bb = br.next()
                S.dma("sp", lambda e, a=a: e.dma_start(out=a[:], in_=dr["hF"][cs, :]), reads=[self.BhF[c]], writes=[Ba])
                S.dma("sp", lambda e, b=b: e.dma_start(out=b[:], in_=dr["hB"][cs, :]), reads=[self.BhB[c]], writes=[Bb])
                S.op("dve", lambda e, a=a, b=b: e.tensor_tensor(out=hs[:, c, :], in0=a[:], in1=b[:], op=ALU.add), reads=[Ba, Bb], writes=[Bhs])
                S.op("act", lambda e: e.activation(out=junk[:], in_=hs[:, c, :], func=AF.Identity, accum_out=s12[:, 0, c:c + 1]),
                     reads=[Bhs], writes=[Bs, Bj])
                S.op("act", lambda e: e.activation(out=junk[:], in_=hs[:, c, :], func=AF.Square, accum_out=s12[:, 1, c:c + 1]),
                     reads=[Bhs], writes=[Bs, Bj])
            inv = 1.0 / DH
            S.op("dve", lambda e: e.tensor_scalar_mul(out=s12[:, 0, :], in0=s12[:, 0, :], scalar1=inv), reads=[Bs], writes=[Bs])
            S.op("dve", lambda e: e.tensor_tensor(out=s12[:, 3, :], in0=s12[:, 0, :], in1=s12[:, 0, :], op=ALU.mult), reads=[Bs], writes=[Bs])
            S.op("dve", lambda e: e.scalar_tensor_tensor(out=s12[:, 2, :], in0=s12[:, 1, :], scalar=inv, in1=s12[:, 3, :], op0=ALU.mult, op1=ALU.subtract),
                 reads=[Bs], writes=[Bs])
            S.op("dve", lambda e: e.tensor_scalar(out=s12[:, 2, :], in0=s12[:, 2, :], scalar1=0.0, scalar2=LN_EPS, op0=ALU.max, op1=ALU.add), reads=[Bs], writes=[Bs])
            S.op("act", lambda e: e.activation(out=s12[:, 2, :], in_=s12[:, 2, :], func=AF.Sqrt), reads=[Bs], writes=[Bs])
            S.op("dve", lambda e: e.reciprocal(out=s12[:, 2, :], in_=s12[:, 2, :]), reads=[Bs], writes=[Bs])
            S.op("dve", lambda e: e.scalar_tensor_tensor(out=s12[:, 3, :], in0=s12[:, 0, :], scalar=-1.0, in1=s12[:, 2, :], op0=ALU.mult, op1=ALU.mult),
                 reads=[Bs], writes=[Bs])
            hafm = self.sb(st, [128, 2, TS], BF16, "hafm"); Bhafm = Buf()
            sor = Rot([self.sb(st, [128, 256], F32) for _ in range(2)])
            t1r = Rot([self.sb(st, [128, 256], F32) for _ in range(2)])
            habr = Rot([self.sb(st, [128, 256], BF16) for _ in range(2)])
            ptr = Rot([self.ps(st, [128, 128], BF16) for _ in range(4)])
            for c in range(NCH):
                cs = slice(c * 128, (c + 1) * 128)
                so, Bso = sor.next(); t1, Bt1 = t1r.next(); hab, Bhab = habr.next()
                S.dma("sp", lambda e, so=so: e.dma_start(out=so[:], in_=dr["so_loc"][cs, :]), reads=[self.Bloc["so"]], writes=[Bso])
                S.op("act", lambda e, t1=t1: e.activation(out=t1[:], in_=hs[:, c, :], func=AF.Identity, scale=s12[:, 2, c:c + 1], bias=s12[:, 3, c:c + 1]),
                     reads=[Bhs, Bs], writes=[Bt1])
                S.op("dve", lambda e, t1=t1: e.tensor_tensor(out=t1[:], in0=t1[:], in1=self.mhg[:, l, :], op=ALU.mult), reads=[Bt1, self.Bmhg], writes=[Bt1])
                S.op("dve", lambda e, t1=t1, so=so, hab=hab: e.tensor_tensor(out=hab[:], in0=t1[:], in1=so[:], op=ALU.mult), reads=[Bt1, Bso], writes=[Bhab])
                for dch in range(2):
                    p, Bp = ptr.next()
                    S.op("pe", lambda e, p=p, hab=hab, dch=dch: e.transpose(p[:], hab[:, dch * 128:(dch + 1) * 128], self.cb[:, 0, :]),
                         reads=[Bhab, self.Bcb], writes=[Bp])
                    S.op("act", lambda e, p=p, dch=dch: e.activation(out=hafm[:, dch, cs], in_=p[:], func=AF.Copy), reads=[Bp], writes=[Bhafm])
            self.Bsend["ha"] = [Buf()]
            S.dma("sp", lambda e: e.dma_start(out=dr["ha_s"].rearrange("(c p) t -> p c t", p=128), in_=hafm[:]),
                  reads=[Bhafm], writes=self.Bsend["ha"])
            S.barrier()

    def fnet(self, l):
        S = self.S
        dr = self.dr
        with contextlib.ExitStack() as st:
            hbfm = self.sb(st, [128, 2, TS], BF16, "hbfm"); Bhb = Buf()
            cg = self.sb(st, [128, 2, 2, 2, 256], BF16, "cg"); Bcg = Buf()
            S.dma("pool", lambda e: e.dma_start(out=cg[:], in_=dr["dft_cg"]), writes=[Bcg])
            for si, (off, L) in enumerate(SEQS):
                N1 = L // 128
                with contextlib.ExitStack() as st2:
                    w1 = self.sb(st2, [N1, 2, N1], BF16, "w1"); Bw1 = Buf()
                    S.dma("pool", lambda e, w1=w1, si=si: e.dma_start(out=w1[:], in_=dr[f"dft_w1_{si}"]), writes=[Bw1])
                    m2 = self.sb(st2, [128, N1, 3, 128], BF16, "m2"); Bm2 = Buf()
                    S.dma("pool", lambda e, m2=m2, si=si: e.dma_start(out=m2[:], in_=dr[f"dft_m2_{si}"]), writes=[Bm2])
                    yd = dr[f"dft_y{si}"]
                    Byd = Buf()
                    fsrc = dr["f_loc"][off:off + L, :].rearrange("(t1 t2) c -> t1 (t2 c)", t2=128)
                    xr = Rot([self.sb(st2, [N1, 4096], BF16) for _ in range(2)])
                    yr = Rot([self.sb(st2, [N1, 2, 4096], BF16) for _ in range(2)])
                    pr = Rot([self.ps(st2, [128, 512]) for _ in range(4)])
                    for blk in range(8):
                        x, Bx = xr.next(); y, By = yr.next()
                        S.dma("sp", lambda e, x=x, blk=blk: e.dma_start(out=x[:], in_=fsrc[:, blk * 4096:(blk + 1) * 4096]),
                              reads=[self.Bloc["f"]], writes=[Bx])
                        for sub in range(8):
                            for ri in range(2):
                                p, Bp = pr.next()
                                S.op("pe", lambda e, p=p, ri=ri, x=x, sub=sub: e.matmul(p[0:N1, :], w1[:, ri, :], x[:, sub * 512:(sub + 1) * 512], start=True, stop=True),
                                     reads=[Bw1, Bx], writes=[Bp])
                                if ri == 0:
                                    S.op("act", lambda e, p=p, y=y, sub=sub, ri=ri: e.activation(out=y[:, ri, sub * 512:(sub + 1) * 512], in_=p[0:N1, :], func=AF.Copy), reads=[Bp], writes=[By])
                                else:
                                    S.op("dve", lambda e, p=p, y=y, sub=sub, ri=ri: e.tensor_copy(out=y[:, ri, sub * 512:(sub + 1) * 512], in_=p[0:N1, :]), reads=[Bp], writes=[By])
                        S.dma("sp", lambda e, y=y, blk=blk: e.dma_start(
                            out=yd[:, :, blk * 4096:(blk + 1) * 4096].rearrange("r k n -> k r n"), in_=y[:]), reads=[By], writes=[Byd])
                    KB = min(8, N1)
                    ykr = Rot([self.sb(st2, [128, KB, 2, 256], BF16) for _ in range(2)])
                    xkr = Rot([self.sb(st2, [128, 2, 2, 128], BF16) for _ in range(2)])
                    ydv = yd.rearrange("r k (t c) -> t k r c", c=256)
                    for kb in range(0, N1, KB):
                        yk, Byk = ykr.next()
                        for ri in range(2):
                            S.dma("sp", lambda e, yk=yk, kb=kb, ri=ri: e.dma_start(out=yk[:, :, ri, :], in_=ydv[:, kb:kb + KB, ri, :]),
                                  reads=[Byd], writes=[Byk])
                        for ki in range(KB):
                            k1 = kb + ki
                            xk, Bxk = xkr.next()
                            for cch in range(2):
                                for ro in range(2):
                                    p, Bp = pr.next()
                                    ta, tb = (0, 2) if ro == 0 else (1, 0)
                                    S.op("pe", lambda e, p=p, cch=cch, ta=ta, yk=yk, ki=ki, k1=k1: e.matmul(
                                        p[:, 0:128], yk[:, ki, 0, cch * 128:(cch + 1) * 128], m2[:, k1, ta, :], start=True, stop=False),
                                        reads=[Byk, Bm2], writes=[Bp])
                                    S.op("pe", lambda e, p=p, cch=cch, tb=tb, yk=yk, ki=ki, k1=k1: e.matmul(
                                        p[:, 0:128], yk[:, ki, 1, cch * 128:(cch + 1) * 128], m2[:, k1, tb, :], start=False, stop=True),
                                        reads=[Byk, Bm2], writes=[Bp])
                                    if ro == 0:
                                        S.op("act", lambda e, p=p, xk=xk, cch=cch, ro=ro: e.activation(out=xk[:, cch, ro, :], in_=p[:, 0:128], func=AF.Copy), reads=[Bp], writes=[Bxk])
                                    else:
                                        S.op("dve", lambda e, p=p, xk=xk, cch=cch, ro=ro: e.tensor_copy(out=xk[:, cch, ro, :], in_=p[:, 0:128]), reads=[Bp], writes=[Bxk])
                            for oc in range(2):
                                p, Bp = pr.next()
                                i = 0
                                for cch in range(2):
                                    for ri in range(2):
                                        S.op("pe", lambda e, p=p, cch=cch, ri=ri, oc=oc, xk=xk, i=i: e.matmul(
                                            p[:, 0:128], cg[:, si, cch, ri, oc * 128:(oc + 1) * 128], xk[:, cch, ri, :], start=(i == 0), stop=(i == 3)),
                                            reads=[Bcg, Bxk], writes=[Bp])
                                        i += 1
                                dst = hbfm[:, oc, off + k1:off + k1 + N1 * 127 + 1:N1]
                                if oc == 0:
                                    S.op("act", lambda e, p=p, dst=dst: e.activation(out=dst, in_=p[:, 0:128], func=AF.Copy), reads=[Bp], writes=[Bhb])
                                else:
                                    S.op("dve", lambda e, p=p, dst=dst: e.tensor_copy(out=dst, in_=p[:, 0:128]), reads=[Bp], writes=[Bhb])
                    S.barrier()
            self.Bsend["hb"] = [Buf()]
            S.dma("sp", lambda e: e.dma_start(out=dr["hb_s"].rearrange("(c p) t -> p c t", p=128), in_=hbfm[:]),
                  reads=[Bhb], writes=self.Bsend["hb"])
            S.barrier()

    def convbranch(self, l):
        S = self.S
        dr = self.dr
        self.Bloc_src = self.Bloc["glu"]
        self.Bsend["cv"] = []
        with contextlib.ExitStack() as st:
            cvr = Rot([self.sb(st, [128, 512], F32) for _ in range(3)])
            dst = dr["cv_s"].rearrange("(c p) t -> p c t", p=128)

            def out_cv(p, Bp, ch, off, t0, n):
                cv, Bcv = cvr.next()
                S.op("act", lambda e: e.activation(out=cv[:, 0:n], in_=p[:, 0:n], func=AF.Identity,
                                                   bias=self.mixp[:, l, MP_DWB + ch:MP_DWB + ch + 1]), reads=[Bp, self.Bmixp], writes=[Bcv])
                bb = Buf()
                self.Bsend["cv"].append(bb)
                S.dma("sp", lambda e: e.dma_start(out=dst[:, ch, off + t0:off + t0 + n], in_=cv[:, 0:n]), reads=[Bcv], writes=[bb])
            self.dwconv(l, "glu_loc", 2, 31, MP_DWW, lambda ch: 1.0, out_cv)
            S.barrier()


def vec_layout():
    cols = {}
    o = 0
    for l in range(DEPTH):
        for s in range(3):
            cols[f"ln_g{l}_{s}"] = o; o += 8
            cols[f"ln_b{l}_{s}"] = o; o += 8
        cols[f"cn_g{l}"] = o; o += 8
        cols[f"cn_b{l}"] = o; o += 8
        cols[f"b_ada{l}"] = o; o += 72
    cols["selb"] = o; o += 2
    return cols, o


class MK(TP, MixMixin):
    pass


WEIGHTS = {}
for _l in range(DEPTH):
    WEIGHTS[f"w_in{_l}"] = (D, NIN)
    WEIGHTS[f"wo_{_l}"] = (D, D)
    for _i in range(2):
        WEIGHTS[f"w1_{_l}{_i}"] = (D, DFF)
        WEIGHTS[f"w3_{_l}{_i}"] = (D, DFF)
        WEIGHTS[f"w2_{_l}{_i}"] = (DFF, D)
    for _b in range(3):
        WEIGHTS[f"wb_{_l}{_b}"] = (D, D)


def mix_io():
    io = {"cmask": ("in", (128, 5, 128), F32), "mixp": ("in", (128, DEPTH, MP_N), F32),
          "mhg": ("in", (128, DEPTH, 256), F32), "onehot": ("in", (NCH, TS), F32),
          "dft_cg": ("in", (128, 2, 2, 2, 256), F32)}
    for si, (off, L) in enumerate(SEQS):
        N1 = L // 128
        io[f"dft_w1_{si}"] = ("in", (N1, 2, N1), F32)
        io[f"dft_m2_{si}"] = ("in", (128, N1, 3, 128), F32)
    return io


def mix_scratch(P):
    for si, (off, L) in enumerate(SEQS):
        P.scratch(f"dft_y{si}", (2, L // 128, 128 * 256), BF16)
    P.scratch("hF", (TS, 256), F32)
    P.scratch("hB", (TS, 256), F32)


def build_mixtest(parts):
    cols, nv = vec_layout()
    io = {"vecs": ("in", (128, nv), F32)}
    io.update(mix_io())
    io.update({"qk_loc": ("in", (512, TS), BF16), "v_loc": ("in", (TS, 256), BF16), "so_loc": ("in", (TS, 256), F32),
               "g_loc": ("in", (TS, 4), F32), "f_loc": ("in", (TS, 256), BF16), "glu_loc": ("in", (256, TS), BF16),
               "ha_s": ("out", (256, TS), BF16), "hb_s": ("out", (256, TS), BF16), "cv_s": ("out", (256, TS), F32)})
    P = MK(io, cols)
    P.Bloc = {n: Buf() for n in ("qk", "v", "so", "g", "f", "glu")}
    mix_scratch(P)
    P.load_mix_consts()
    if "mlstm" in parts:
        P.mlstm(0)
        P.headnorm(0)
    if "fnet" in parts:
        P.fnet(0)
    if "conv" in parts:
        P.convbranch(0)
    P.S.finish()
    P.top.close()
    return P.nc


def build_full(nsteps=99, dbg=None):
    global NUM_DEV
    NUM_DEV = NCORE
    cols, nv = vec_layout()
    io = {"vecs": ("in", (128, nv), F32), "cT3": ("in", (128, 3), F32),
          "xin": ("in", (D, NT), F32), "pos": ("in", (D, TL), F32),
          "xout": ("out", (D, NT), F32)}
    for l in range(DEPTH):
        io[f"w_ada{l}"] = ("in", (128, 9 * D), F32)
    for nm, (K, N) in WEIGHTS.items():
        io[nm] = ("in", (K // 8, N), F32)
    io.update(mix_io())
    P = MK(io, cols)
    S = P.S
    mix_scratch(P)
    sc = P.scratch
    for nm, shape, dt in (("qk", (2048, NT), BF16), ("v", (NT, 1024), BF16), ("so", (NT, 1024), F32),
                          ("g", (4 * NT, 4), F32), ("f", (NT, 1024), BF16), ("glu", (1024, NT), BF16)):
        sc(nm + "_s", shape, dt)
    for nm, shape, dt in (("qk", (512, TS), BF16), ("v", (TS, 256), BF16), ("so", (TS, 256), F32),
                          ("g", (TS, 4), F32), ("f", (TS, 256), BF16), ("glu", (256, TS), BF16)):
        sc(nm + "_loc", shape, dt)
    for nm, dt in (("ha", BF16), ("hb", BF16), ("cv", F32)):
        sc(nm + "_s", (256, TS), dt)
        sc(nm + "_tp", (D, NT), dt)
    sc("mg", (3072, NT), F32)
    for nm in ("xa0", "xa1", "xa2", "xa3"):
        sc(nm, (D, NT), F32)
    for nm in ("u0", "u1", "u2", "u3"):
        sc(nm, (D, NT), BF16)
    sc("a", (DFF, NT), BF16)

    P.load_mix_consts()
    P.compute_ada_all()
    P.derive_mod("m00", 0, 0)
    P.derive_ep("e00", 0, 0, (0, 1)); P.derive_ep("e01", 0, 1, (0, 2)); P.derive_ep("e02", 0, 2, (1, 0))
    P.derive_ep("e10", 1, 0, (1, 1)); P.derive_ep("e11", 1, 1, (1, 2)); P.derive_ep("e12", 1, 2, None)

    def gw(names):
        for nm in names:
            P.gather_weight(nm, *WEIGHTS[nm])

    steps = []
    steps.append(lambda: gw(["w1_00", "w3_00", "w2_00", "w_in0"]))
    steps.append(lambda: P.step_prep("xin", "pos", "xa0", "u0", "m00"))
    for l in range(DEPTH):
        steps.append(lambda l=l: P.step_ffn_up(l, 0, "u0", "a"))
        steps.append(lambda l=l: P.step_ffn_down(l, 0, "a", "xa0", "xa1", "u1", f"e{l}0"))
        steps.append(lambda l=l: P.step_inproj(l, "u1"))
        steps.append(lambda l=l: P.exchange_A())
        if l == 0:
            steps.append(lambda: gw(["wb_00", "wb_01", "wb_02", "wo_0", "w1_01", "w3_01", "w2_01", "w1_10", "w3_10", "w2_10", "w_in1"]))
        else:
            steps.append(lambda: gw(["wb_10", "wb_11", "wb_12", "wo_1", "w1_11", "w3_11", "w2_11"]))
        steps.append(lambda l=l: P.mlstm(l))
        steps.append(lambda l=l: P.headnorm(l))
        steps.append(lambda l=l: P.fnet(l))
        steps.append(lambda l=l: P.convbranch(l))
        steps.append(lambda l=l: P.exchange_B())
        steps.append(lambda l=l: P.step_postmix(l, "xa1", "xa2", "u2", f"e{l}1"))
        steps.append(lambda l=l: P.step_ffn_up(l, 1, "u2", "a"))
        if l == 0:
            steps.append(lambda l=l: P.step_ffn_down(l, 1, "a", "xa2", "xa0", "u0", f"e{l}2"))
        else:
            steps.append(lambda l=l: P.step_ffn_down(l, 1, "a", "xa2", "xout", "u3", f"e{l}2", final=True))
    for i, fn in enumerate(steps):
        if i < nsteps:
            fn()
    if dbg:
        for nm in dbg:
            src = P.dr[nm]
            dst = P.nc.dram_tensor("dbg_" + nm, list(src.shape), src.dtype, kind="ExternalOutput").ap()
            S.dma("sp", lambda e, src=src, dst=dst: e.dma_start(out=dst, in_=src))
    S.finish()
    P.top.close()
    return P.nc


def fmaj(vec):
    return np.ascontiguousarray(np.asarray(vec, np.float32).reshape(-1, 128).T)


def sincos_table():
    quarter = D // 4
    omega = (1.0 / (10000.0 ** (np.arange(quarter, dtype=np.float32) / np.float32(quarter)))).astype(np.float32)
    rows = T // GRID_W
    r = np.arange(rows, dtype=np.float32)[:, None] * omega
    cl = np.arange(GRID_W, dtype=np.float32)[:, None] * omega
    row_emb = np.concatenate([np.sin(r), np.cos(r)], axis=-1).astype(np.float32)
    col_emb = np.concatenate([np.sin(cl), np.cos(cl)], axis=-1).astype(np.float32)
    emb = np.concatenate([np.broadcast_to(row_emb[:, None, :], (rows, GRID_W, D // 2)),
                          np.broadcast_to(col_emb[None, :, :], (rows, GRID_W, D // 2))], axis=-1)
    return emb.reshape(T, D).astype(np.float32)


def const_tables():
    c = {}
    s = np.arange(128)[:, None]; t = np.arange(128)[None, :]
    cm = np.zeros((128, 5, 128), np.float32)
    cm[:, 0] = np.eye(128)
    cm[:, 1] = (s <= t)
    cm[:, 2] = (s >= t)
    cm[:, 3] = np.where(s <= t, 0.0, -30000.0)
    cm[:, 4] = np.where(s >= t, 0.0, -30000.0)
    c["cmask"] = cm
    oh = np.zeros((NCH, TS), np.float32)
    for ch in range(NCH):
        oh[ch, ch * 128:(ch + 1) * 128] = 1.0
    c["onehot"] = oh
    cg = np.zeros((128, 2, 2, 2, 256), np.float32)
    cc = np.arange(256, dtype=np.float64)
    ang = 2 * np.pi * np.outer(cc, cc) / 256.0
    for si, (off, L) in enumerate(SEQS):
        N1 = L // 128
        scale = 1.0 / np.sqrt(L * 256.0)
        for cch in range(2):
            cg[:, si, cch, 0, :] = (np.cos(ang) * scale)[cch * 128:(cch + 1) * 128]
            cg[:, si, cch, 1, :] = (np.sin(ang) * scale)[cch * 128:(cch + 1) * 128]
        k1 = np.arange(N1, dtype=np.float64)
        a1 = 2 * np.pi * np.outer(k1, k1) / N1
        w1 = np.zeros((N1, 2, N1), np.float32)
        w1[:, 0, :] = np.cos(a1); w1[:, 1, :] = -np.sin(a1)
        c[f"dft_w1_{si}"] = w1
        t2 = np.arange(128, dtype=np.float64)[:, None, None]
        kk1 = np.arange(N1, dtype=np.float64)[None, :, None]
        k2 = np.arange(128, dtype=np.float64)[None, None, :]
        th = 2 * np.pi * (kk1 * t2 / L + k2 * t2 / 128.0)
        m2 = np.zeros((128, N1, 3, 128), np.float32)
        m2[:, :, 0, :] = np.cos(th); m2[:, :, 1, :] = -np.sin(th); m2[:, :, 2, :] = np.sin(th)
        c[f"dft_m2_{si}"] = m2
    c["dft_cg"] = cg
    return c


def make_vecs(inp, b):
    cols, nv = vec_layout()
    v = np.zeros((128, nv), np.float32)
    for l in range(DEPTH):
        for s in range(3):
            v[:, cols[f"ln_g{l}_{s}"]:cols[f"ln_g{l}_{s}"] + 8] = fmaj(inp["ln_g"][l, s])
            v[:, cols[f"ln_b{l}_{s}"]:cols[f"ln_b{l}_{s}"] + 8] = fmaj(inp["ln_b"][l, s])
        v[:, cols[f"cn_g{l}"]:cols[f"cn_g{l}"] + 8] = fmaj(inp["conv_norm_g"][l])
        v[:, cols[f"cn_b{l}"]:cols[f"cn_b{l}"] + 8] = fmaj(inp["conv_norm_b"][l])
        v[:, cols[f"b_ada{l}"]:cols[f"b_ada{l}"] + 72] = fmaj(inp["b_ada"][l])
    v[:, cols["selb"] + b] = 1.0
    return v


def mix_params(inp, j):
    mp = np.zeros((128, DEPTH, MP_N), np.float32)
    mh = np.zeros((128, DEPTH, 256), np.float32)
    for l in range(DEPTH):
        qkw = inp["qk_conv_w"][l]
        for ch in range(4):
            base = (0 if ch < 2 else 1024) + j * 256 + (ch % 2) * 128
            mp[:, l, MP_QKW + ch * 5:MP_QKW + ch * 5 + 5] = qkw[:, base:base + 128].T
        dww = inp["dw_w"][l]
        for ch in range(2):
            base = j * 256 + ch * 128
            mp[:, l, MP_DWW + ch * 31:MP_DWW + (ch + 1) * 31] = dww[:, base:base + 128].T
            mp[:, l, MP_DWB + ch] = inp["dw_b"][l][base:base + 128]
        bg = inp["b_gates"][l].reshape(2, 2, HEADS)
        mp[:, l, MP_BG:MP_BG + 4] = np.array([bg[0, 0, j], bg[0, 1, j], bg[1, 0, j], bg[1, 1, j]], np.float32)[None, :]
        mh[:, l, :] = inp["mh_norm_g"][l][j * 256:(j + 1) * 256][None, :]
    return mp, mh


def host_inputs(inp):
    pos_tab = sincos_table()
    consts = const_tables()
    wsrc = {}
    for l in range(DEPTH):
        wsrc[f"w_in{l}"] = inp["w_in"][l]
        wsrc[f"wo_{l}"] = inp["w_out"][l]
        wsrc[f"w_ada{l}"] = inp["w_ada"][l]
        for i in range(2):
            wsrc[f"w1_{l}{i}"] = inp["ffn_w1"][l, i]
            wsrc[f"w3_{l}{i}"] = inp["ffn_w3"][l, i]
            wsrc[f"w2_{l}{i}"] = inp["ffn_w2"][l, i]
        for br in range(3):
            wsrc[f"wb_{l}{br}"] = inp["w_branch"][l, br]
    maps = []
    for r in range(NCORE):
        b, s = r // 4, r % 4
        m = dict(consts)
        m["vecs"] = make_vecs(inp, b)
        sl = slice(r * 128, (r + 1) * 128)
        m["cT3"] = np.ascontiguousarray(np.stack([inp["c"][0][sl], inp["c"][1][sl], inp["c_ctx"][sl]], axis=-1).astype(np.float32))
        xin = np.concatenate([inp["x"][b, TL * s:TL * (s + 1), :].T, inp["ctx"][b, CL * s:CL * (s + 1), :].T], axis=1)
        m["xin"] = np.ascontiguousarray(xin, dtype=np.float32)
        m["pos"] = np.ascontiguousarray(pos_tab[TL * s:TL * (s + 1), :].T)
        m["mixp"], m["mhg"] = mix_params(inp, s)
        for nm, w in wsrc.items():
            k = w.shape[0] // 8
            m[nm] = np.ascontiguousarray(w[r * k:(r + 1) * k], dtype=np.float32)
        maps.append(m)
    return maps


_NC_CACHE = {}


def kernel(**inputs):
    inp = {k: np.asarray(v) for k, v in inputs.items()}
    if "nc" not in _NC_CACHE:
        _NC_CACHE["nc"] = build_full()
    maps = host_inputs(inp)
    res = run_bass_kernel_spmd(_NC_CACHE["nc"], maps, core_ids=list(range(NCORE)))
    out = np.empty((B, T, D), np.float32)
    for r in range(NCORE):
        b, s = r // 4, r % 4
        out[b, TL * s:TL * (s + 1), :] = np.asarray(res.results[r]["xout"])[:, 0:TL].T
    return out
```

```python
import contextlib
import numpy as np
import concourse.bass as bass
import concourse.mybir as mybir
from concourse.bass_utils import run_bass_kernel_spmd

F32 = mybir.dt.float32
BF16 = mybir.dt.bfloat16
AF = mybir.ActivationFunctionType
ALU = mybir.AluOpType

D = 1024
B = 2
T = 8192
DEPTH = 2
NCORE = 8
NR = 8
GROUPS = [list(range(8))]
KK = D // NR // 128
NUM_DEV = None
TCX = 256
HEADS = 4
DH = 256
DFF = 2816
NFF = DFF // 128
NIN = 10256
GRID_W = 64
ALPHA = (2 * DEPTH) ** 0.25
LN_EPS = 1e-5
TL = T // 4
CL = TCX // 4
NT = TL + CL
TS = TCX + T
OQ, OK_, OV, OG, OO, OF, OCV, OCG, OMG = 0, 1024, 2048, 3072, 3088, 4112, 5136, 6160, 7184

NDMA_SEM = 8
CC_INC = 1
EPOCH = 30000


class Buf:
    __slots__ = ("w", "r")

    def __init__(self):
        self.w = None
        self.r = []


class Sched:
    ENGS = ("pe", "act", "dve", "pool", "sp")
    DQ = ("sp", "pool", "act")

    def __init__(self, nc, stack, same_engine_sync=True):
        self.nc = nc
        self.stack = stack
        self.same = same_engine_sync
        self.count = {e: 0 for e in self.ENGS}
        self.sem = {e: [] for e in self.ENGS}
        self.dsem = {e: [stack.enter_context(nc.semaphore(f"d_{e}_{i}")) for i in range(NDMA_SEM)]
                     for e in self.DQ}
        self.dcount = {e: 0 for e in self.DQ}
        self.waited = {}
        self.engobj = {"pe": nc.tensor, "act": nc.scalar, "dve": nc.vector, "pool": nc.gpsimd,
                       "sp": nc.sync}

    def _csem(self, eng, ep):
        while len(self.sem[eng]) <= ep:
            self.sem[eng].append(self.stack.enter_context(
                self.nc.semaphore(f"s_{eng}_{len(self.sem[eng])}")))
        return self.sem[eng][ep]

    def _tok_wait(self, tok):
        if tok[0] == "c":
            ep = tok[2] // EPOCH
            return (self._csem(tok[1], ep), (ep, tok[2] % EPOCH + 1), ("c", tok[1]))
        if tok[0] == "x":
            return (self.ccsem, (0, CC_INC * (tok[2] + 1)), ("x", "cc"))
        q, j = tok[1], tok[2]
        return (self.dsem[q][j % NDMA_SEM], (0, 16 * (j // NDMA_SEM + 1)), ("d", q, j % NDMA_SEM))

    def _waits_for(self, eng, toks):
        waits = []
        for tok in toks:
            if tok[0] == "c" and tok[1] == eng and (eng == "pe" or not self.same):
                continue
            sem, val, key = self._tok_wait(tok)
            k = (eng, key)
            if self.waited.get(k, (0, 0)) >= val:
                continue
            self.waited[k] = val
            waits.append((sem, val[1]))
        return waits

    def _deps(self, eng, reads, writes):
        toks = []
        for b in reads:
            if b.w is not None:
                toks.append(b.w)
        for b in writes:
            if b.w is not None:
                toks.append(b.w)
            toks.extend(b.r)
        return self._waits_for(eng, toks)

    @staticmethod
    def _mark(tok, reads, writes):
        for b in reads:
            b.r.append(tok)
        for b in writes:
            b.w = tok
            b.r = []

    def _push(self, eng, waits, fn, inc):
        e = self.engobj[eng]
        for sem, val in waits:
            e.wait_ge(sem, val)
        if fn is not None:
            fn(e).then_inc(inc[0], inc[1])

    def op(self, eng, fn, reads=(), writes=()):
        waits = self._deps(eng, reads, writes)
        idx = self.count[eng]
        self.count[eng] += 1
        tok = ("c", eng, idx)
        self._mark(tok, reads, writes)
        self._push(eng, waits, fn, (self._csem(eng, idx // EPOCH), 1))
        return tok

    def dma(self, q, fn, reads=(), writes=()):
        waits = self._deps(q, reads, writes)
        j = self.dcount[q]
        self.dcount[q] += 1
        ring = self.dsem[q][j % NDMA_SEM]
        if j >= NDMA_SEM:
            val = 16 * (j // NDMA_SEM)
            k = (q, ("d", q, j % NDMA_SEM))
            if self.waited.get(k, (0, 0)) < (0, val):
                self.waited[k] = (0, val)
                waits.append((ring, val))
        tok = ("d", q, j)
        self._mark(tok, reads, writes)
        self._push(q, waits, fn, (ring, 16))
        return tok

    def cc(self, fn, reads=(), writes=()):
        waits = self._deps("pool", reads, writes)
        if not hasattr(self, "ccsem"):
            self.ccsem = self.stack.enter_context(self.nc.semaphore("s_cc"))
            self.cccount = 0
        j = self.cccount
        self.cccount += 1
        tok = ("x", "cc", j)
        self._mark(tok, reads, writes)
        self._push("pool", waits, fn, (self.ccsem, CC_INC))
        return tok

    def _all_last(self):
        toks = [("c", e, self.count[e] - 1) for e in self.ENGS if self.count[e] > 0]
        for q in self.DQ:
            for j in range(max(0, self.dcount[q] - NDMA_SEM), self.dcount[q]):
                toks.append(("d", q, j))
        return toks

    def barrier(self, engs=None):
        toks = self._all_last()
        for e in (engs or self.ENGS):
            self._push(e, self._waits_for(e, toks), None, None)

    def finish(self):
        toks = self._all_last()
        if getattr(self, "cccount", 0) > 0:
            toks.append(("x", "cc", self.cccount - 1))
        self._push("sp", self._waits_for("sp", toks), None, None)


class Prog:
    def __init__(self, io):
        self.nc = bass.Bass("TRN2", target_bir_lowering=False, num_devices=NUM_DEV)
        self.io = io
        self.dr = {}
        self.drbuf = {}
        for name, (kind, shape, dt) in io.items():
            k = "ExternalInput" if kind == "in" else "ExternalOutput"
            self.dr[name] = self.nc.dram_tensor(name, list(shape), dt, kind=k).ap()
        self.top = contextlib.ExitStack()
        self.S = Sched(self.nc, self.top)
        self.uid = 0

    def scratch(self, name, shape, dt):
        self.dr[name] = self.nc.dram_tensor(name, list(shape), dt).ap()
        return self.dr[name]

    def dbuf(self, name, key=0):
        k = (name, key)
        if k not in self.drbuf:
            self.drbuf[k] = Buf()
        return self.drbuf[k]

    def sb(self, st, shape, dt, name=None):
        self.uid += 1
        return st.enter_context(self.nc.sbuf_tensor(f"{name or 't'}{self.uid}", list(shape), dt))

    def ps(self, st, shape, dt=F32, name=None):
        self.uid += 1
        return st.enter_context(self.nc.psum_tensor(f"{name or 'p'}{self.uid}", list(shape), dt))


class Rot:
    def __init__(self, items):
        self.items = [(t, Buf()) for t in items]
        self.i = 0

    def next(self):
        it = self.items[self.i % len(self.items)]
        self.i += 1
        return it


def tiles_of(W):
    out = [(t0, W, 0) for t0 in range(0, TL, W)]
    out.append((TL, CL, 1))
    return out


class TP(Prog):
    def __init__(self, io, vec_cols):
        super().__init__(io)
        self.vc = vec_cols
        S, nc = self.S, self.nc
        st = self.top
        nv = io["vecs"][1][1]
        self.vecs = self.sb(st, [128, nv], F32, "vecs")
        self.Bvecs = Buf()
        S.dma("sp", lambda e: e.dma_start(out=self.vecs[:], in_=self.dr["vecs"]), writes=[self.Bvecs])
        self.der = self.sb(st, [128, 1024], F32, "der")
        self.Bder = Buf()
        self.dcol = 0
        self.dnames = {}
        self.ones_mean = self.sb(st, [128, 128], BF16, "onesm")
        self.Bones = Buf()
        S.op("dve", lambda e: e.memset(self.ones_mean[:], 1.0 / D), writes=[self.Bones])
        self.ada = {}
        self.wfull = {}
        self.Bg = {}
        self.Bsend = {}
        self.G = {}
        self.Bloc = {}
        pid = self.nc.sync.partition_id()
        self.bat = pid // 4
        self.seg = pid % 4

    def v(self, name, n=8):
        o = self.vc[name]
        return self.vecs[:, o:o + n]

    def dnew(self, name, n=8):
        self.dnames[name] = self.dcol
        self.dcol += n
        assert self.dcol <= 1024
        return self.d(name, n)

    def d(self, name, n=8):
        o = self.dnames[name]
        return self.der[:, o:o + n]

    def dve_small(self, fn):
        self.S.op("dve", fn, reads=[self.Bvecs, self.Bder], writes=[self.Bder])

    def compute_ada_all(self):
        S, nc = self.S, self.nc
        part = self.scratch("ada_part", (128, 512), F32)
        partg = self.scratch("ada_partg", (NR * 128, 512), F32)
        Bpart, Bpartg = Buf(), Buf()
        adas = [self.sb(self.top, [128, 72, 2], F32, "ada") for _ in range(DEPTH)]
        with contextlib.ExitStack() as st:
            c32 = self.sb(st, [128, KK, 3], F32)
            cs = self.sb(st, [128, KK, 3], BF16)
            Bc32, Bcs = Buf(), Buf()
            S.dma("sp", lambda e: e.dma_start(out=c32[:], in_=self.dr["cT3"]), writes=[Bc32])
            S.op("act", lambda e: e.activation(out=cs[:], in_=c32[:], func=AF.Silu), reads=[Bc32], writes=[Bcs])
            psb = self.sb(st, [128, DEPTH * 216], F32); Bpsb = Buf()
            w = self.sb(st, [128, KK, 9 * D], BF16)
            pt = self.ps(st, [128, 72, 3])
            Bw, Bpt = Buf(), Buf()
            for l in range(DEPTH):
                S.dma("pool", lambda e, l=l: e.dma_start(out=w[:], in_=self.dr[f"w_ada{l}"].rearrange("(k p) n -> p k n", p=128)), writes=[Bw])
                for fc in range(72):
                    for kk in range(KK):
                        S.op("pe", lambda e, fc=fc, kk=kk: e.matmul(
                            pt[:, fc, :], w[:, kk, fc * 128:(fc + 1) * 128], cs[:, kk, :], start=(kk == 0), stop=(kk == KK - 1)),
                            reads=[Bw, Bcs], writes=[Bpt])
                S.op("dve", lambda e, l=l: e.tensor_copy(
                    out=psb[:, l * 216:(l + 1) * 216], in_=pt[:].rearrange("p a b -> p (a b)")),
                    reads=[Bpt], writes=[Bpsb])
            S.dma("sp", lambda e: e.dma_start(out=part[:, 0:DEPTH * 216], in_=psb[:]), reads=[Bpsb], writes=[Bpart])
            S.cc(lambda e: e.collective_compute("AllGather", ALU.bypass, replica_groups=GROUPS,
                                                ins=[part], outs=[partg]), reads=[Bpart], writes=[Bpartg])
            g8 = self.sb(st, [128, NR, DEPTH * 216], F32); Bg8 = Buf()
            S.dma("sp", lambda e: e.dma_start(out=g8[:], in_=partg.rearrange("(r p) n -> p r n", p=128)[:, :, 0:DEPTH * 216]),
                  reads=[Bpartg], writes=[Bg8])
            acc = self.sb(st, [128, DEPTH * 216], F32); Bacc = Buf()
            S.op("dve", lambda e: e.tensor_tensor(out=acc[:], in0=g8[:, 0, :], in1=g8[:, 1, :], op=ALU.add),
                 reads=[Bg8], writes=[Bacc])
            for r in range(2, NR):
                S.op("dve", lambda e, r=r: e.tensor_tensor(out=acc[:], in0=acc[:], in1=g8[:, r, :], op=ALU.add),
                     reads=[Bg8, Bacc], writes=[Bacc])
            selb = self.v("selb", 2)
            for l in range(DEPTH):
                ada = adas[l]
                Bada = Buf()
                a3 = acc[:, l * 216:(l + 1) * 216].rearrange("p (a b) -> p a b", b=3)
                o = self.vc[f"b_ada{l}"]
                bias = self.vecs[:, o:o + 72]
                S.op("dve", lambda e, ada=ada, a3=a3: e.tensor_scalar_mul(out=ada[:, :, 0], in0=a3[:, :, 0], scalar1=selb[:, 0:1]),
                     reads=[Bacc, self.Bvecs], writes=[Bada])
                S.op("dve", lambda e, ada=ada, a3=a3: e.scalar_tensor_tensor(
                    out=ada[:, :, 0], in0=a3[:, :, 1], scalar=selb[:, 1:2], in1=ada[:, :, 0], op0=ALU.mult, op1=ALU.add),
                    reads=[Bacc, self.Bvecs], writes=[Bada])
                S.op("dve", lambda e, ada=ada, a3=a3: e.tensor_copy(out=ada[:, :, 1], in_=a3[:, :, 2]),
                     reads=[Bacc], writes=[Bada])
                for col in range(2):
                    S.op("dve", lambda e, ada=ada, col=col, bias=bias: e.tensor_tensor(
                        out=ada[:, :, col], in0=ada[:, :, col], in1=bias, op=ALU.add),
                        reads=[self.Bvecs], writes=[Bada])
                self.ada[l] = (ada, Bada)
            S._push("dve", S._waits_for("dve", [("x", "cc", S.cccount - 1)]), None, None)
            S.barrier()

    def ada_ap(self, l, sub, kind, k):
        ada, _ = self.ada[l]
        i = (sub * 3 + kind) * 8
        return ada[:, i:i + 8, k]

    def derive_mod(self, name, l, sub):
        for k in range(2):
            sc = self.dnew(f"{name}_sc{k}")
            self.S.op("dve", lambda e, sc=sc, k=k: e.tensor_scalar_add(
                out=sc, in0=self.ada_ap(l, sub, 1, k), scalar1=1.0),
                reads=[self.ada[l][1]], writes=[self.Bder])
            sh = self.dnew(f"{name}_sh{k}")
            self.S.op("dve", lambda e, sh=sh, k=k: e.tensor_copy(out=sh, in_=self.ada_ap(l, sub, 0, k)),
                      reads=[self.ada[l][1]], writes=[self.Bder])

    def derive_ep(self, name, l, sub, nxt):
        S = self.S
        g = self.v(f"ln_g{l}_{sub}")
        b = self.v(f"ln_b{l}_{sub}")
        rd = [self.ada[l][1], self.Bvecs, self.Bder]
        if nxt is not None:
            rd.append(self.ada[nxt[0]][1])
        for k in range(2):
            gr = self.dnew(f"{name}_gr{k}")
            S.op("dve", lambda e, gr=gr, k=k: e.tensor_scalar_mul(
                out=gr, in0=self.ada_ap(l, sub, 2, k), scalar1=(1.0 if sub == 1 else 0.5)),
                reads=rd, writes=[self.Bder])
        if nxt is not None:
            xs = self.dnew(f"{name}_xs")
            xb = self.dnew(f"{name}_xb")
            S.op("dve", lambda e: e.tensor_scalar_mul(out=xs, in0=g, scalar1=ALPHA), reads=rd, writes=[self.Bder])
            S.op("dve", lambda e: e.tensor_scalar_mul(out=xb, in0=b, scalar1=ALPHA), reads=rd, writes=[self.Bder])
            for k in range(2):
                us = self.dnew(f"{name}_us{k}")
                ub = self.dnew(f"{name}_ub{k}")
                tmp = self.dnew(f"{name}_tmp{k}")
                S.op("dve", lambda e, tmp=tmp, k=k: e.tensor_scalar_add(
                    out=tmp, in0=self.ada_ap(nxt[0], nxt[1], 1, k), scalar1=1.0), reads=rd, writes=[self.Bder])
                S.op("dve", lambda e, us=us, tmp=tmp: e.tensor_tensor(out=us, in0=g, in1=tmp, op=ALU.mult),
                     reads=rd, writes=[self.Bder])
                S.op("dve", lambda e, ub=ub, tmp=tmp: e.tensor_tensor(out=ub, in0=b, in1=tmp, op=ALU.mult),
                     reads=rd, writes=[self.Bder])
                S.op("dve", lambda e, ub=ub, k=k: e.tensor_tensor(
                    out=ub, in0=ub, in1=self.ada_ap(nxt[0], nxt[1], 0, k), op=ALU.add), reads=rd, writes=[self.Bder])
        else:
            xs = self.dnew(f"{name}_xs")
            xb = self.dnew(f"{name}_xb")
            S.op("dve", lambda e: e.tensor_copy(out=xs, in_=g), reads=rd, writes=[self.Bder])
            S.op("dve", lambda e: e.tensor_copy(out=xb, in_=b), reads=rd, writes=[self.Bder])

    def fm(self, name, t0, W, nchunk=None):
        return self.dr[name].rearrange("(c p) t -> p c t", p=128)[:, :, t0:t0 + W]

    def fm_g(self, name, t0, W, k):
        g = self.dr[name].rearrange("(c p) t -> p c t", p=128)
        col = (TCX + self.seg * TL + t0) if k == 0 else (self.seg * CL)
        return g[:, bass.ds(self.bat * 8, 8), bass.ds(col, W)]

    def gather_weight(self, name, K, N):
        S = self.S
        rk = K // NR
        rp = 1 << (rk - 1).bit_length()
        pieces = []
        c0 = 0
        while c0 < N:
            n = min(8192, N - c0)
            npad = 1 << (n - 1).bit_length()
            while rp * npad * 2 < 131072:
                npad *= 2
            sh = self.scratch(f"{name}_sh{c0}", (rp, npad), BF16)
            full = self.scratch(f"{name}_fu{c0}", (NR * rp, npad), BF16)
            Bsh, Bfull = Buf(), Buf()
            S.dma("pool", lambda e, sh=sh, c0=c0, n=n: e.dma_start(out=sh[0:rk, 0:n], in_=self.dr[name][:, c0:c0 + n]), writes=[Bsh])
            S.cc(lambda e, sh=sh, full=full: e.collective_compute("AllGather", ALU.bypass, replica_groups=GROUPS,
                                                               ins=[sh], outs=[full]), reads=[Bsh], writes=[Bfull])
            pieces.append((c0, n, full, Bfull))
            c0 += n
        self.wfull[name] = (rk, rp, pieces)

    def load_w(self, st, name, K, N, c0=0, tag="w"):
        rk, rp, pieces = self.wfull[name]
        kc_n = K // 128
        w = self.sb(st, [128, kc_n, N], BF16, tag)
        bufs = [Buf() for _ in range(kc_n)]
        for kc in range(kc_n):
            g0 = kc * 128
            while g0 < (kc + 1) * 128:
                r = g0 // rk
                n = min((kc + 1) * 128 - g0, (r + 1) * rk - g0)
                row = r * rp + (g0 - r * rk)
                p0 = g0 - kc * 128
                for (pc0, pn, full, Bfull) in pieces:
                    lo, hi = max(c0, pc0), min(c0 + N, pc0 + pn)
                    if lo >= hi:
                        continue
                    self.S.dma("sp", lambda e, kc=kc, p0=p0, n=n, row=row, lo=lo, hi=hi, full=full, pc0=pc0: e.dma_start(
                        out=w[p0:p0 + n, kc, lo - c0:hi - c0], in_=full[row:row + n, lo - pc0:hi - pc0]),
                        reads=[Bfull], writes=[bufs[kc]])
                g0 += n
        return w, bufs

    def step_prep(self, xin, pos, xa_out, u_out, mod):
        S = self.S
        with contextlib.ExitStack() as st:
            xr = Rot([self.sb(st, [128, 8, 512], F32) for _ in range(2)])
            pr = Rot([self.sb(st, [128, 8, 512], F32) for _ in range(2)])
            xo = Rot([self.sb(st, [128, 8, 512], F32) for _ in range(2)])
            uo = Rot([self.sb(st, [128, 8, 512], BF16) for _ in range(2)])
            for (t0, W, k) in tiles_of(512):
                x, Bx = xr.next()
                S.dma("sp", lambda e, x=x, t0=t0, W=W: e.dma_start(out=x[:, :, 0:W], in_=self.fm(xin, t0, W)), writes=[Bx])
                if k == 0:
                    p, Bp = pr.next()
                    S.dma("sp", lambda e, p=p, t0=t0, W=W: e.dma_start(out=p[:, :, 0:W], in_=self.fm(pos, t0, W)), writes=[Bp])
                    S.op("dve", lambda e, x=x, p=p, W=W: e.tensor_tensor(out=x[:, :, 0:W], in0=x[:, :, 0:W], in1=p[:, :, 0:W], op=ALU.add),
                         reads=[Bx, Bp], writes=[Bx])
                xa, Bxa = xo.next()
                u, Bu = uo.next()
                S.op("act", lambda e, xa=xa, x=x, W=W: e.activation(out=xa[:, :, 0:W], in_=x[:, :, 0:W], func=AF.Identity, scale=ALPHA),
                     reads=[Bx], writes=[Bxa])
                sc, sh = self.d(f"{mod}_sc{k}"), self.d(f"{mod}_sh{k}")
                for c in range(8):
                    S.op("act", lambda e, u=u, x=x, W=W, c=c, sc=sc, sh=sh: e.activation(
                        out=u[:, c, 0:W], in_=x[:, c, 0:W], func=AF.Identity, bias=sh[:, c:c + 1], scale=sc[:, c:c + 1]),
                        reads=[Bx, self.Bder], writes=[Bu])
                S.dma("sp", lambda e, xa=xa, t0=t0, W=W: e.dma_start(out=self.fm(xa_out, t0, W), in_=xa[:, :, 0:W]),
                      reads=[Bxa], writes=[self.dbuf(xa_out, t0)])
                S.dma("sp", lambda e, u=u, t0=t0, W=W: e.dma_start(out=self.fm(u_out, t0, W), in_=u[:, :, 0:W]),
                      reads=[Bu], writes=[self.dbuf(u_out, t0)])
            S.barrier()

    def step_ffn_up(self, l, i, u_in, a_out):
        S = self.S
        with contextlib.ExitStack() as st:
            w1, B1 = self.load_w(st, f"w1_{l}{i}", D, DFF, tag="w1")
            w3, B3 = self.load_w(st, f"w3_{l}{i}", D, DFF, tag="w3")
            ur = Rot([self.sb(st, [128, 8, 512], BF16) for _ in range(2)])
            ar = Rot([self.sb(st, [128, NFF, 512], BF16) for _ in range(2)])
            sr = Rot([self.sb(st, [128, 512], F32) for _ in range(2)])
            pr = Rot([self.ps(st, [128, 512]) for _ in range(6)])
            for (t0, W, k) in tiles_of(512):
                u, Bu = ur.next()
                S.dma("sp", lambda e, u=u, t0=t0, W=W: e.dma_start(out=u[:, :, 0:W], in_=self.fm(u_in, t0, W)),
                      reads=[self.dbuf(u_in, t0)], writes=[Bu])
                a, Ba = ar.next()
                for fc in range(NFF):
                    ph, Bph = pr.next()
                    pg, Bpg = pr.next()
                    for kc in range(8):
                        S.op("pe", lambda e, ph=ph, kc=kc, fc=fc, u=u, W=W: e.matmul(
                            ph[:, 0:W], w1[:, kc, fc * 128:(fc + 1) * 128], u[:, kc, 0:W], start=(kc == 0), stop=(kc == 7)),
                            reads=[B1[kc], Bu], writes=[Bph])
                    for kc in range(8):
                        S.op("pe", lambda e, pg=pg, kc=kc, fc=fc, u=u, W=W: e.matmul(
                            pg[:, 0:W], w3[:, kc, fc * 128:(fc + 1) * 128], u[:, kc, 0:W], start=(kc == 0), stop=(kc == 7)),
                            reads=[B3[kc], Bu], writes=[Bpg])
                    s, Bs = sr.next()
                    S.op("act", lambda e, s=s, ph=ph, W=W: e.activation(out=s[:, 0:W], in_=ph[:, 0:W], func=AF.Silu),
                         reads=[Bph], writes=[Bs])
                    S.op("dve", lambda e, a=a, fc=fc, s=s, pg=pg, W=W: e.tensor_tensor(
                        out=a[:, fc, 0:W], in0=pg[:, 0:W], in1=s[:, 0:W], op=ALU.mult),
                        reads=[Bpg, Bs], writes=[Ba])
                S.dma("sp", lambda e, a=a, t0=t0, W=W: e.dma_start(out=self.fm(a_out, t0, W), in_=a[:, :, 0:W]),
                      reads=[Ba], writes=[self.dbuf(a_out, t0)])
            S.barrier()

    def ln_stats(self, st_tiles, z, Bz, W, nch=8):
        S = self.S
        zb_r, zq_r, pm, Bpm, pq, Bpq, mean, Bmean, rstd, Brstd = st_tiles
        for c in range(nch):
            zb, Bzb = zb_r.next()
            zq, Bzq = zq_r.next()
            S.op("act", lambda e, zb=zb, c=c: e.activation(out=zb[:, 0:W], in_=z[:, c, 0:W], func=AF.Copy),
                 reads=[Bz], writes=[Bzb])
            S.op("act", lambda e, zq=zq, c=c: e.activation(out=zq[:, 0:W], in_=z[:, c, 0:W], func=AF.Square),
                 reads=[Bz], writes=[Bzq])
            S.op("pe", lambda e, zb=zb, c=c: e.matmul(pm[:, 0:W], self.ones_mean[:], zb[:, 0:W], start=(c == 0), stop=(c == nch - 1)),
                 reads=[Bzb, self.Bones], writes=[Bpm])
            S.op("pe", lambda e, zq=zq, c=c: e.matmul(pq[:, 0:W], self.ones_mean[:], zq[:, 0:W], start=(c == 0), stop=(c == nch - 1)),
                 reads=[Bzq, self.Bones], writes=[Bpq])
        S.op("act", lambda e: e.activation(out=mean[:, 0:W], in_=pm[:, 0:W], func=AF.Copy), reads=[Bpm], writes=[Bmean])
        S.op("dve", lambda e: e.tensor_tensor(out=rstd[:, 0:W], in0=mean[:, 0:W], in1=mean[:, 0:W], op=ALU.mult),
             reads=[Bmean], writes=[Brstd])
        S.op("dve", lambda e: e.tensor_tensor(out=rstd[:, 0:W], in0=pq[:, 0:W], in1=rstd[:, 0:W], op=ALU.subtract),
             reads=[Bpq, Brstd], writes=[Brstd])
        S.op("dve", lambda e: e.tensor_scalar(out=rstd[:, 0:W], in0=rstd[:, 0:W], scalar1=0.0, scalar2=LN_EPS, op0=ALU.max, op1=ALU.add),
             reads=[Brstd], writes=[Brstd])
        S.op("act", lambda e: e.activation(out=rstd[:, 0:W], in_=rstd[:, 0:W], func=AF.Sqrt), reads=[Brstd], writes=[Brstd])
        S.op("dve", lambda e: e.reciprocal(out=rstd[:, 0:W], in_=rstd[:, 0:W]), reads=[Brstd], writes=[Brstd])
        return mean, Bmean, rstd, Brstd

    def alloc_ln(self, st, Wmax):
        zb_r = Rot([self.sb(st, [128, Wmax], BF16) for _ in range(2)])
        zq_r = Rot([self.sb(st, [128, Wmax], BF16) for _ in range(2)])
        pm = self.ps(st, [128, 512]); pq = self.ps(st, [128, 512])
        mean = self.sb(st, [128, Wmax], F32); rstd = self.sb(st, [128, Wmax], F32)
        return (zb_r, zq_r, pm, Buf(), pq, Buf(), mean, Buf(), rstd, Buf())

    def ln_out(self, lnt, z, Bz, W, k, ep, t0, xa_out, u_out, xo, Bxo, uo, Buo, final):
        S = self.S
        mean, Bmean, rstd, Brstd = self.ln_stats(lnt, z, Bz, W)
        xs, xb = self.d(f"{ep}_xs"), self.d(f"{ep}_xb")
        for c in range(8):
            S.op("dve", lambda e, c=c: e.tensor_tensor(out=z[:, c, 0:W], in0=z[:, c, 0:W], in1=mean[:, 0:W], op=ALU.subtract),
                 reads=[Bz, Bmean], writes=[Bz])
            S.op("dve", lambda e, c=c: e.tensor_tensor(out=z[:, c, 0:W], in0=z[:, c, 0:W], in1=rstd[:, 0:W], op=ALU.mult),
                 reads=[Bz, Brstd], writes=[Bz])
            S.op("act", lambda e, c=c: e.activation(out=xo[:, c, 0:W], in_=z[:, c, 0:W], func=AF.Identity,
                                                    bias=xb[:, c:c + 1], scale=xs[:, c:c + 1]),
                 reads=[Bz, self.Bder], writes=[Bxo])
            if not final:
                us, ub = self.d(f"{ep}_us{k}"), self.d(f"{ep}_ub{k}")
                S.op("act", lambda e, c=c, us=us, ub=ub: e.activation(out=uo[:, c, 0:W], in_=z[:, c, 0:W], func=AF.Identity,
                                                                    bias=ub[:, c:c + 1], scale=us[:, c:c + 1]),
                     reads=[Bz, self.Bder], writes=[Buo])
        S.dma("sp", lambda e: e.dma_start(out=self.fm(xa_out, t0, W), in_=xo[:, :, 0:W]),
              reads=[Bxo], writes=[self.dbuf(xa_out, t0)])
        if not final:
            S.dma("sp", lambda e: e.dma_start(out=self.fm(u_out, t0, W), in_=uo[:, :, 0:W]),
                  reads=[Buo], writes=[self.dbuf(u_out, t0)])

    def step_ffn_down(self, l, i, a_in, xa_in, xa_out, u_out, ep, final=False):
        S = self.S
        with contextlib.ExitStack() as st:
            w2, B2 = self.load_w(st, f"w2_{l}{i}", DFF, D, tag="w2")
            ar = Rot([self.sb(st, [128, NFF, 512], BF16) for _ in range(2)])
            xr = Rot([self.sb(st, [128, 8, 512], F32) for _ in range(2)])
            z = self.sb(st, [128, 8, 512], F32); Bz = Buf()
            xo = self.sb(st, [128, 8, 512], F32); Bxo = Buf()
            uo = self.sb(st, [128, 8, 512], BF16); Buo = Buf()
            lnt = self.alloc_ln(st, 512)
            pr = Rot([self.ps(st, [128, 512]) for _ in range(4)])
            for (t0, W, k) in tiles_of(512):
                a, Ba = ar.next()
                S.dma("sp", lambda e, a=a, t0=t0, W=W: e.dma_start(out=a[:, :, 0:W], in_=self.fm(a_in, t0, W)),
                      reads=[self.dbuf(a_in, t0)], writes=[Ba])
                x, Bx = xr.next()
                S.dma("sp", lambda e, x=x, t0=t0, W=W: e.dma_start(out=x[:, :, 0:W], in_=self.fm(xa_in, t0, W)),
                      reads=[self.dbuf(xa_in, t0)], writes=[Bx])
                gr = self.d(f"{ep}_gr{k}")
                for dc in range(8):
                    py, Bpy = pr.next()
                    for fc in range(NFF):
                        S.op("pe", lambda e, py=py, fc=fc, dc=dc, a=a, W=W: e.matmul(
                            py[:, 0:W], w2[:, fc, dc * 128:(dc + 1) * 128], a[:, fc, 0:W], start=(fc == 0), stop=(fc == NFF - 1)),
                            reads=[B2[fc], Ba], writes=[Bpy])
                    S.op("dve", lambda e, py=py, dc=dc, x=x, W=W, gr=gr: e.scalar_tensor_tensor(
                        out=z[:, dc, 0:W], in0=py[:, 0:W], scalar=gr[:, dc:dc + 1], in1=x[:, dc, 0:W], op0=ALU.mult, op1=ALU.add),
                        reads=[Bpy, Bx, self.Bder], writes=[Bz])
                self.ln_out(lnt, z, Bz, W, k, ep, t0, xa_out, u_out, xo, Bxo, uo, Buo, final)
            S.barrier()

    def step_inproj(self, l, u_in):
        S = self.S
        with contextlib.ExitStack() as st:
            w, Bw = self.load_w(st, f"w_in{l}", D, OF, tag="wi1")
            ur = Rot([self.sb(st, [128, 8, 512], BF16) for _ in range(2)])
            qk = self.sb(st, [128, 16, 512], BF16); Bqk = Buf()
            vt = self.sb(st, [128, 4, 1024], BF16); Bvt = Buf()
            so = self.sb(st, [128, 4, 1024], F32); Bso = Buf()
            gt = self.sb(st, [128, 4, 16], F32); Bgt = Buf()
            pr = Rot([self.ps(st, [128, 512]) for _ in range(6)])
            cnt = 0
            for (t0, W, k) in tiles_of(512):
                u, Bu = ur.next()
                S.dma("sp", lambda e, u=u, t0=t0, W=W: e.dma_start(out=u[:, :, 0:W], in_=self.fm(u_in, t0, W)),
                      reads=[self.dbuf(u_in, t0)], writes=[Bu])
                for oc in range(16):
                    p, Bp = pr.next()
                    for kc in range(8):
                        S.op("pe", lambda e, p=p, kc=kc, oc=oc, u=u, W=W: e.matmul(
                            p[:, 0:W], w[:, kc, oc * 128:(oc + 1) * 128], u[:, kc, 0:W], start=(kc == 0), stop=(kc == 7)),
                            reads=[Bw[kc], Bu], writes=[Bp])
                    eng = "act" if oc % 2 == 0 else "dve"
                    if eng == "act":
                        S.op("act", lambda e, p=p, oc=oc, W=W: e.activation(out=qk[:, oc, 0:W], in_=p[:, 0:W], func=AF.Copy),
                             reads=[Bp], writes=[Bqk])
                    else:
                        S.op("dve", lambda e, p=p, oc=oc, W=W: e.tensor_copy(out=qk[:, oc, 0:W], in_=p[:, 0:W]),
                             reads=[Bp], writes=[Bqk])
                S.dma("sp", lambda e, t0=t0, W=W: e.dma_start(out=self.fm("qk_s", t0, W), in_=qk[:, :, 0:W]),
                      reads=[Bqk], writes=[self.dbuf("qk_s", t0)])
                nsub = (W + 127) // 128
                for sbi in range(nsub):
                    m = min(128, W - sbi * 128)
                    ts = slice(sbi * 128, sbi * 128 + m)
                    for (col0, ncol, kind) in ((OV, 512, "v"), (OV + 512, 512, "v2"), (OO, 512, "o"), (OO + 512, 512, "o2"), (OG, 16, "g")):
                        p, Bp = pr.next()
                        for kc in range(8):
                            S.op("pe", lambda e, p=p, kc=kc, u=u, ts=ts, m=m, col0=col0, ncol=ncol: e.matmul(
                                p[0:m, 0:ncol], u[:, kc, ts], w[:, kc, col0:col0 + ncol], start=(kc == 0), stop=(kc == 7)),
                                reads=[Bw[kc], Bu], writes=[Bp])
                        if kind in ("v", "v2"):
                            o0 = 0 if kind == "v" else 512
                            S.op("dve", lambda e, p=p, m=m, sbi=sbi, o0=o0: e.tensor_copy(out=vt[0:m, sbi, o0:o0 + 512], in_=p[0:m, 0:512]),
                                 reads=[Bp], writes=[Bvt])
                        elif kind in ("o", "o2"):
                            o0 = 0 if kind == "o" else 512
                            S.op("act", lambda e, p=p, m=m, sbi=sbi, o0=o0: e.activation(out=so[0:m, sbi, o0:o0 + 512], in_=p[0:m, 0:512], func=AF.Sigmoid),
                                 reads=[Bp], writes=[Bso])
                        else:
                            S.op("dve", lambda e, p=p, m=m, sbi=sbi: e.tensor_copy(
                                out=gt[0:m, sbi, :].rearrange("p (h k) -> p h k", h=4),
                                in_=p[0:m, 0:16].rearrange("p (k h) -> p h k", h=4)),
                                 reads=[Bp], writes=[Bgt])
                gdst = self.dr["g_s"].rearrange("(h t) k -> t h k", h=4)
                if W == 512:
                    tm = lambda name: self.dr[name][t0:t0 + W, :].rearrange("(s p) c -> p s c", p=128)
                    S.dma("sp", lambda e, tm=tm: e.dma_start(out=tm("v_s"), in_=vt[:]), reads=[Bvt], writes=[self.dbuf("v_s", t0)])
                    S.dma("sp", lambda e, tm=tm: e.dma_start(out=tm("so_s"), in_=so[:]), reads=[Bso], writes=[self.dbuf("so_s", t0)])
                    for sbi in range(4):
                        S.dma("sp", lambda e, sbi=sbi, t0=t0: e.dma_start(
                            out=gdst[t0 + sbi * 128:t0 + (sbi + 1) * 128], in_=gt[:, sbi, :].rearrange("p (h k) -> p h k", h=4)),
                            reads=[Bgt], writes=[self.dbuf("g_s", (t0, sbi))])
                else:
                    S.dma("sp", lambda e, t0=t0, W=W: e.dma_start(out=self.dr["v_s"][t0:t0 + W, :], in_=vt[0:W, 0, :]), reads=[Bvt], writes=[self.dbuf("v_s", t0)])
                    S.dma("sp", lambda e, t0=t0, W=W: e.dma_start(out=self.dr["so_s"][t0:t0 + W, :], in_=so[0:W, 0, :]), reads=[Bso], writes=[self.dbuf("so_s", t0)])
                    S.dma("sp", lambda e, t0=t0, W=W: e.dma_start(
                        out=gdst[t0:t0 + W], in_=gt[0:W, 0, :].rearrange("p (h k) -> p h k", h=4)),
                        reads=[Bgt], writes=[self.dbuf("g_s", (t0, 0))])
            S.barrier()
        self.exchange_A1()
        with contextlib.ExitStack() as st:
            N2 = NIN - OF
            w, Bw = self.load_w(st, f"w_in{l}", D, N2, c0=OF, tag="wi2")
            ur = Rot([self.sb(st, [128, 8, 512], BF16) for _ in range(2)])
            ft = self.sb(st, [128, 4, 1024], BF16); Bft = Buf()
            glu = self.sb(st, [128, 8, 512], BF16); Bglu = Buf()
            mgr = Rot([self.sb(st, [128, 8, 512], F32) for _ in range(2)])
            sgr = Rot([self.sb(st, [128, 512], F32) for _ in range(2)])
            pr = Rot([self.ps(st, [128, 512]) for _ in range(6)])
            for (t0, W, k) in tiles_of(512):
                u, Bu = ur.next()
                S.dma("sp", lambda e, u=u, t0=t0, W=W: e.dma_start(out=u[:, :, 0:W], in_=self.fm(u_in, t0, W)),
                      reads=[self.dbuf(u_in, t0)], writes=[Bu])
                nsub = (W + 127) // 128
                for sbi in range(nsub):
                    m = min(128, W - sbi * 128)
                    ts = slice(sbi * 128, sbi * 128 + m)
                    for half in range(2):
                        p, Bp = pr.next()
                        c0 = half * 512
                        for kc in range(8):
                            S.op("pe", lambda e, p=p, kc=kc, u=u, ts=ts, m=m, c0=c0: e.matmul(
                                p[0:m, 0:512], u[:, kc, ts], w[:, kc, c0:c0 + 512], start=(kc == 0), stop=(kc == 7)),
                                reads=[Bw[kc], Bu], writes=[Bp])
                        S.op("dve", lambda e, p=p, m=m, sbi=sbi, c0=c0: e.tensor_copy(out=ft[0:m, sbi, c0:c0 + 512], in_=p[0:m, 0:512]),
                             reads=[Bp], writes=[Bft])
                if W == 512:
                    S.dma("sp", lambda e, t0=t0, W=W: e.dma_start(
                        out=self.dr["f_s"][t0:t0 + W, :].rearrange("(s p) c -> p s c", p=128), in_=ft[:]),
                        reads=[Bft], writes=[self.dbuf("f_s", t0)])
                else:
                    S.dma("sp", lambda e, t0=t0, W=W: e.dma_start(out=self.dr["f_s"][t0:t0 + W, :], in_=ft[0:W, 0, :]),
                          reads=[Bft], writes=[self.dbuf("f_s", t0)])
                cv0, cg0, mg0 = OCV - OF, OCG - OF, OMG - OF
                for oc in range(8):
                    pv, Bpv = pr.next()
                    pg, Bpg = pr.next()
                    for kc in range(8):
                        S.op("pe", lambda e, pv=pv, kc=kc, oc=oc, u=u, W=W: e.matmul(
                            pv[:, 0:W], w[:, kc, cv0 + oc * 128:cv0 + (oc + 1) * 128], u[:, kc, 0:W], start=(kc == 0), stop=(kc == 7)),
                            reads=[Bw[kc], Bu], writes=[Bpv])
                    for kc in range(8):
                        S.op("pe", lambda e, pg=pg, kc=kc, oc=oc, u=u, W=W: e.matmul(
                            pg[:, 0:W], w[:, kc, cg0 + oc * 128:cg0 + (oc + 1) * 128], u[:, kc, 0:W], start=(kc == 0), stop=(kc == 7)),
                            reads=[Bw[kc], Bu], writes=[Bpg])
                    sg, Bsg = sgr.next()
                    S.op("act", lambda e, sg=sg, pg=pg, W=W: e.activation(out=sg[:, 0:W], in_=pg[:, 0:W], func=AF.Sigmoid),
                         reads=[Bpg], writes=[Bsg])
                    S.op("dve", lambda e, sg=sg, pv=pv, oc=oc, W=W: e.tensor_tensor(out=glu[:, oc, 0:W], in0=pv[:, 0:W], in1=sg[:, 0:W], op=ALU.mult),
                         reads=[Bpv, Bsg], writes=[Bglu])
                S.dma("sp", lambda e, t0=t0, W=W: e.dma_start(out=self.fm("glu_s", t0, W), in_=glu[:, :, 0:W]),
                      reads=[Bglu], writes=[self.dbuf("glu_s", t0)])
                for br in range(3):
                    mg, Bmg = mgr.next()
                    for oc in range(8):
                        p, Bp = pr.next()
                        c0 = mg0 + br * 1024 + oc * 128
                        for kc in range(8):
                            S.op("pe", lambda e, p=p, kc=kc, c0=c0, u=u, W=W: e.matmul(
                                p[:, 0:W], w[:, kc, c0:c0 + 128], u[:, kc, 0:W], start=(kc == 0), stop=(kc == 7)),
                                reads=[Bw[kc], Bu], writes=[Bp])
                        S.op("act", lambda e, p=p, mg=mg, oc=oc, W=W: e.activation(out=mg[:, oc, 0:W], in_=p[:, 0:W], func=AF.Sigmoid),
                             reads=[Bp], writes=[Bmg])
                    S.dma("sp", lambda e, mg=mg, br=br, t0=t0, W=W: e.dma_start(
                        out=self.dr["mg"][br * 1024:(br + 1) * 1024, :].rearrange("(c p) t -> p c t", p=128)[:, :, t0:t0 + W],
                        in_=mg[:, :, 0:W]), reads=[Bmg], writes=[self.dbuf("mg", (t0, br))])
            S.barrier()
        self.exchange_A2()

    def step_postmix(self, l, xa_in, xa_out, u_out, ep):
        S = self.S
        WP = 256
        with contextlib.ExitStack() as st:
            wb = [self.load_w(st, f"wb_{l}{br}", D, D, tag=f"wb{br}") for br in range(3)]
            wo, Bwo = self.load_w(st, f"wo_{l}", D, D, tag="wo")
            cvr = Rot([self.sb(st, [128, 8, WP], F32) for _ in range(2)])
            har = Rot([self.sb(st, [128, 8, WP], BF16) for _ in range(2)])
            hbr = Rot([self.sb(st, [128, 8, WP], BF16) for _ in range(2)])
            mgr = Rot([self.sb(st, [128, 24, WP], F32) for _ in range(1)])
            xr = Rot([self.sb(st, [128, 8, WP], F32) for _ in range(2)])
            prod = Rot([self.sb(st, [128, WP], F32) for _ in range(2)])
            hc = self.sb(st, [128, 8, WP], BF16); Bhc = Buf()
            mer = self.sb(st, [128, 8, WP], BF16); Bmer = Buf()
            macc = Rot([self.sb(st, [128, WP], F32) for _ in range(2)])
            z = self.sb(st, [128, 8, WP], F32); Bz = Buf()
            xo = self.sb(st, [128, 8, WP], F32); Bxo = Buf()
            uo = self.sb(st, [128, 8, WP], BF16); Buo = Buf()
            lnt = self.alloc_ln(st, WP)
            pr = Rot([self.ps(st, [128, 512]) for _ in range(6)])
            cg, cb = self.v(f"cn_g{l}"), self.v(f"cn_b{l}")
            for (t0, W, k) in tiles_of(WP):
                cv, Bcv = cvr.next(); ha, Bha = har.next(); hb, Bhb = hbr.next(); mg, Bmg = mgr.next(); x, Bx = xr.next()
                S.dma("sp", lambda e, cv=cv, t0=t0, W=W: e.dma_start(out=cv[:, :, 0:W], in_=self.fm("cv_tp", t0, W)), writes=[Bcv])
                S.dma("sp", lambda e, ha=ha, t0=t0, W=W: e.dma_start(out=ha[:, :, 0:W], in_=self.fm("ha_tp", t0, W)), writes=[Bha])
                S.dma("sp", lambda e, hb=hb, t0=t0, W=W: e.dma_start(out=hb[:, :, 0:W], in_=self.fm("hb_tp", t0, W)), writes=[Bhb])
                S.dma("sp", lambda e, mg=mg, t0=t0, W=W: e.dma_start(out=mg[:, :, 0:W], in_=self.fm("mg", t0, W)), writes=[Bmg])
                S.dma("sp", lambda e, x=x, t0=t0, W=W: e.dma_start(out=x[:, :, 0:W], in_=self.fm(xa_in, t0, W)), writes=[Bx])
                mean, Bmean, rstd, Brstd = self.ln_stats(lnt, cv, Bcv, W)
                for c in range(8):
                    S.op("dve", lambda e, c=c, cv=cv, W=W: e.tensor_tensor(out=cv[:, c, 0:W], in0=cv[:, c, 0:W], in1=mean[:, 0:W], op=ALU.subtract),
                         reads=[Bcv, Bmean], writes=[Bcv])
                    S.op("dve", lambda e, c=c, cv=cv, W=W: e.tensor_tensor(out=cv[:, c, 0:W], in0=cv[:, c, 0:W], in1=rstd[:, 0:W], op=ALU.mult),
                         reads=[Bcv, Brstd], writes=[Bcv])
                    S.op("act", lambda e, c=c, cv=cv, W=W: e.activation(out=hc[:, c, 0:W], in_=cv[:, c, 0:W], func=AF.Silu,
                                                                        bias=cb[:, c:c + 1], scale=cg[:, c:c + 1]),
                         reads=[Bcv, self.Bvecs], writes=[Bhc])
                hs = [(ha, Bha), (hb, Bhb), (hc, Bhc)]
                for dc in range(8):
                    ac, Bac = macc.next()
                    for br in range(3):
                        p, Bp = pr.next()
                        wbt, Bwb = wb[br]
                        h, Bh = hs[br]
                        for kc in range(8):
                            S.op("pe", lambda e, p=p, kc=kc, dc=dc, wbt=wbt, h=h, W=W: e.matmul(
                                p[:, 0:W], wbt[:, kc, dc * 128:(dc + 1) * 128], h[:, kc, 0:W], start=(kc == 0), stop=(kc == 7)),
                                reads=[Bwb[kc], Bh], writes=[Bp])
                        if br == 0:
                            S.op("dve", lambda e, p=p, ac=ac, mg=mg, dc=dc, W=W: e.tensor_tensor(
                                out=ac[:, 0:W], in0=p[:, 0:W], in1=mg[:, dc, 0:W], op=ALU.mult), reads=[Bp, Bmg], writes=[Bac])
                        else:
                            pd, Bpd = prod.next()
                            S.op("dve", lambda e, p=p, mg=mg, dc=dc, br=br, W=W, pd=pd: e.tensor_tensor(
                                out=pd[:, 0:W], in0=p[:, 0:W], in1=mg[:, br * 8 + dc, 0:W], op=ALU.mult), reads=[Bp, Bmg], writes=[Bpd])
                            if br == 1:
                                S.op("pool", lambda e, ac=ac, pd=pd, W=W: e.tensor_tensor(
                                    out=ac[:, 0:W], in0=ac[:, 0:W], in1=pd[:, 0:W], op=ALU.add), reads=[Bac, Bpd], writes=[Bac])
                            else:
                                S.op("pool", lambda e, ac=ac, dc=dc, pd=pd, W=W: e.tensor_tensor(
                                    out=mer[:, dc, 0:W], in0=ac[:, 0:W], in1=pd[:, 0:W], op=ALU.add), reads=[Bac, Bpd], writes=[Bmer])
                gr = self.d(f"{ep}_gr{k}")
                for dc in range(8):
                    py, Bpy = pr.next()
                    for kc in range(8):
                        S.op("pe", lambda e, py=py, kc=kc, dc=dc, W=W: e.matmul(
                            py[:, 0:W], wo[:, kc, dc * 128:(dc + 1) * 128], mer[:, kc, 0:W], start=(kc == 0), stop=(kc == 7)),
                            reads=[Bwo[kc], Bmer], writes=[Bpy])
                    S.op("dve", lambda e, py=py, dc=dc, x=x, W=W, gr=gr: e.scalar_tensor_tensor(
                        out=z[:, dc, 0:W], in0=py[:, 0:W], scalar=gr[:, dc:dc + 1], in1=x[:, dc, 0:W], op0=ALU.mult, op1=ALU.add),
                        reads=[Bpy, Bx, self.Bder], writes=[Bz])
                self.ln_out(lnt, z, Bz, W, k, ep, t0, xa_out, u_out, xo, Bxo, uo, Buo, False)
            S.barrier()


NCH = TS // 128
SEQS = ((0, TCX), (TCX, T))
MP_QKW, MP_DWW, MP_DWB, MP_BG, MP_N = 0, 20, 82, 84, 88


class MixMixin:
    def ag(self, tag, src_ap, Bsrcs, rows, cols, dt, dst_cols=None):
        S = self.S
        if ("pk_" + tag) not in self.dr:
            self.scratch("pk_" + tag, (rows, cols), dt)
            self.scratch("gg_" + tag, (NR * rows, cols), dt)
        pk, g = self.dr["pk_" + tag], self.dr["gg_" + tag]
        nb = rows * cols * (4 if dt == F32 else 2)
        assert nb & (nb - 1) == 0 and 131072 <= nb <= 4194304, (tag, nb)
        Bpk, Bgg = Buf(), Buf()
        if callable(src_ap):
            dst, src_ap = src_ap(pk)
        else:
            dst = pk if dst_cols is None else pk[:, 0:dst_cols]
        S.dma("sp", lambda e: e.dma_start(out=dst, in_=src_ap), reads=Bsrcs, writes=[Bpk])
        S.cc(lambda e: e.collective_compute("AllGather", ALU.bypass, replica_groups=GROUPS,
                                            ins=[pk], outs=[g]), reads=[Bpk], writes=[Bgg])
        self.Bg[tag] = Bgg
        return g, Bgg

    def exchange_A1(self):
        dr = self.dr
        nb = []
        G = self.G
        G["q"] = self.ag("q", dr["qk_s"][0:1024, 0:TL], nb, 1024, TL, BF16)
        G["k"] = self.ag("k", dr["qk_s"][1024:2048, 0:TL], nb, 1024, TL, BF16)
        G["qkc"] = self.ag("qkc", dr["qk_s"][:, TL:NT], nb, 2048, 2 * CL, BF16, dst_cols=CL)
        G["v"] = self.ag("v", dr["v_s"][0:TL, :], nb, TL, 1024, BF16)
        G["vc"] = self.ag("vc", dr["v_s"][TL:NT, :], nb, CL, 1024, BF16)
        gsv = dr["g_s"].rearrange("(h t) k -> h t k", h=4)
        G["g"] = self.ag("g", lambda pk: (pk.rearrange("(h t) k -> h t k", h=4), gsv[:, 0:TL, :]), nb, 4 * TL, 4, F32)
        G["gc"] = self.ag("gc", lambda pk: (pk.rearrange("(h t) k -> h t k", h=4)[:, :, 0:4], gsv[:, TL:NT, :]), nb, 4 * CL, 128, F32)
        G["so0"] = self.ag("so0", dr["so_s"][0:1024, :], nb, 1024, 1024, F32)
        G["so1"] = self.ag("so1", dr["so_s"][1024:2048, :], nb, 1024, 1024, F32)
        G["soc"] = self.ag("soc", dr["so_s"][TL:NT, :], nb, CL, 1024, F32)

    def exchange_A2(self):
        dr = self.dr
        nb = []
        G = self.G
        G["f"] = self.ag("f", dr["f_s"][0:TL, :], nb, TL, 1024, BF16)
        G["fc"] = self.ag("fc", dr["f_s"][TL:NT, :], nb, CL, 1024, BF16)
        G["glu"] = self.ag("glu", dr["glu_s"][:, 0:TL], nb, 1024, TL, BF16)
        G["gluc"] = self.ag("gluc", dr["glu_s"][:, TL:NT], nb, 1024, 2 * CL, BF16, dst_cols=CL)

    def unpack_A(self, names):
        S = self.S
        dr = self.dr
        G = self.G
        B1, J1 = bass.ds(self.bat, 1), bass.ds(self.seg, 1)

        def cp(nm, dst, src, Bgg):
            S.dma("sp", lambda e: e.dma_start(out=dst, in_=src), reads=[Bgg], writes=[self.Bloc[nm]])
        for nm in names:
            self.Bloc[nm] = Buf()
        if "qk" in names:
            for half, nm in ((0, "q"), (1, "k")):
                g, Bgg = G[nm]
                v5 = g.rearrange("(b s j p) c -> b s j p c", b=2, s=4, j=4)
                cp("qk", dr["qk_loc"][half * 256:(half + 1) * 256, TCX:TS].rearrange("p (s t) -> s p t", s=4),
                   v5[B1, :, J1, :, :].rearrange("a s j p c -> (a s) (j p) c"), Bgg)
            g, Bgg = G["qkc"]
            v6 = g.rearrange("(b s h j p) c -> b s h j p c", b=2, s=4, h=2, j=4)
            cp("qk", dr["qk_loc"][:, 0:TCX].rearrange("(h p) (s t) -> s h p t", h=2, s=4),
               v6[B1, :, :, J1, :, 0:CL].rearrange("a s h j p c -> (a s) h (j p) c"), Bgg)
        if "glu" in names:
            g, Bgg = G["glu"]
            v5 = g.rearrange("(b s j p) c -> b s j p c", b=2, s=4, j=4)
            cp("glu", dr["glu_loc"][:, TCX:TS].rearrange("p (s t) -> s p t", s=4), v5[B1, :, J1, :, :].rearrange("a s j p c -> (a s) (j p) c"), Bgg)
            g, Bgg = G["gluc"]
            v5 = g.rearrange("(b s j p) c -> b s j p c", b=2, s=4, j=4)
            cp("glu", dr["glu_loc"][:, 0:TCX].rearrange("p (s t) -> s p t", s=4), v5[B1, :, J1, :, 0:CL].rearrange("a s j p c -> (a s) (j p) c"), Bgg)
        for nm in ("v", "f"):
            if nm in names:
                for sfx, rows in (("", slice(TCX, TS)), ("c", slice(0, TCX))):
                    g, Bgg = G[nm + sfx]
                    v5 = g.rearrange("(b s t) (j c) -> b s t j c", b=2, s=4, j=4)
                    cp(nm, dr[nm + "_loc"][rows, :].rearrange("(s t) c -> s t c", s=4), v5[B1, :, :, J1, :].rearrange("a s t j c -> (a s) t (j c)"), Bgg)
        if "so" in names:
            for hh in range(2):
                g, Bgg = G[f"so{hh}"]
                v5 = g.rearrange("(b s t) (j c) -> b s t j c", b=2, s=4, j=4)
                cp("so", dr["so_loc"][TCX:TS, :].rearrange("(s h t) c -> s h t c", s=4, h=2)[:, hh],
                   v5[B1, :, :, J1, :].rearrange("a s t j c -> (a s) t (j c)"), Bgg)
            g, Bgg = G["soc"]
            v5 = g.rearrange("(b s t) (j c) -> b s t j c", b=2, s=4, j=4)
            cp("so", dr["so_loc"][0:TCX, :].rearrange("(s t) c -> s t c", s=4), v5[B1, :, :, J1, :].rearrange("a s t j c -> (a s) t (j c)"), Bgg)
        if "g" in names:
            g, Bgg = G["g"]
            v5 = g.rearrange("(b s h t) k -> b s h t k", b=2, s=4, h=4)
            cp("g", dr["g_loc"][TCX:TS, :].rearrange("(s t) k -> s t k", s=4), v5[B1, :, J1, :, :].rearrange("a s h t k -> (a s) (h t) k"), Bgg)
            g, Bgg = G["gc"]
            v5 = g.rearrange("(b s h t) k -> b s h t k", b=2, s=4, h=4)
            cp("g", dr["g_loc"][0:TCX, :].rearrange("(s t) k -> s t k", s=4), v5[B1, :, J1, :, 0:4].rearrange("a s h t k -> (a s) (h t) k"), Bgg)

    def exchange_B(self, nm):
        dr = self.dr
        if nm in ("ha", "hb"):
            self.ag(nm + "l", dr[nm + "_s"][:, TCX:TS], self.Bsend[nm], 256, T, BF16)
            self.ag(nm + "c", dr[nm + "_s"][:, 0:TCX], self.Bsend[nm], 256, TCX, BF16)
        else:
            self.ag("cvl0", dr["cv_s"][0:128, TCX:TS], self.Bsend["cv"], 128, T, F32)
            self.ag("cvl1", dr["cv_s"][128:256, TCX:TS], self.Bsend["cv"], 128, T, F32)
            self.ag("cvc", dr["cv_s"][:, 0:TCX], self.Bsend["cv"], 256, TCX, F32)

    def unpack_B(self):
        S = self.S
        B1, J1 = bass.ds(self.bat, 1), bass.ds(self.seg, 1)

        def cp(dst, src, Bgg):
            S.dma("sp", lambda e: e.dma_start(out=dst, in_=src), reads=[Bgg], writes=[Buf()])
        for nm in ("ha", "hb"):
            d3 = self.dr[nm + "_tp"].rearrange("(c p) t -> c p t", p=128)
            g = self.dr["gg_" + nm + "l"].rearrange("(b c p) (s t) -> b c p s t", b=2, c=8, s=4)
            cp(d3[:, :, 0:TL], g[B1, :, :, J1, :].rearrange("a c p s t -> (a c) p (s t)"), self.Bg[nm + "l"])
            g = self.dr["gg_" + nm + "c"].rearrange("(b c p) (s t) -> b c p s t", b=2, c=8, s=4)
            cp(d3[:, :, TL:NT], g[B1, :, :, J1, :].rearrange("a c p s t -> (a c) p (s t)"), self.Bg[nm + "c"])
        d4 = self.dr["cv_tp"].rearrange("(r h p) t -> r h p t", h=2, p=128)
        for hh in range(2):
            g = self.dr[f"gg_cvl{hh}"].rearrange("(b r p) (s t) -> b r p s t", b=2, r=4, s=4)
            cp(d4[:, hh, :, 0:TL], g[B1, :, :, J1, :].rearrange("a r p s t -> (a r) p (s t)"), self.Bg[f"cvl{hh}"])
        d3 = self.dr["cv_tp"].rearrange("(c p) t -> c p t", p=128)
        g = self.dr["gg_cvc"].rearrange("(b c p) (s t) -> b c p s t", b=2, c=8, s=4)
        cp(d3[:, :, TL:NT], g[B1, :, :, J1, :].rearrange("a c p s t -> (a c) p (s t)"), self.Bg["cvc"])
        S.barrier()

    def load_mix_consts(self):
        S = self.S
        st = self.top
        self.cst = self.sb(st, [128, 5, 128], F32, "cst")
        self.Bcst = Buf()
        S.dma("sp", lambda e: e.dma_start(out=self.cst[:], in_=self.dr["cmask"]), writes=[self.Bcst])
        self.cb = self.sb(st, [128, 4, 128], BF16, "cstb")
        self.Bcb = Buf()
        S.op("dve", lambda e: e.tensor_copy(out=self.cb[:, 0:3, :], in_=self.cst[:, 0:3, :]), reads=[self.Bcst], writes=[self.Bcb])
        S.op("dve", lambda e: e.memset(self.cb[:, 3, :], 1.0), writes=[self.Bcb])
        self.mixp = self.sb(st, [128, DEPTH, MP_N], F32, "mixp")
        self.Bmixp = Buf()
        S.dma("sp", lambda e: e.dma_start(out=self.mixp[:], in_=self.dr["mixp"]), writes=[self.Bmixp])
        self.mhg = self.sb(st, [128, DEPTH, 256], F32, "mhg")
        self.Bmhg = Buf()
        S.dma("sp", lambda e: e.dma_start(out=self.mhg[:], in_=self.dr["mhg"]), writes=[self.Bmhg])

    def dwconv(self, l, src, nch, ktaps, wcol0, scale_fn, emit_out):
        S = self.S
        half = ktaps // 2
        with contextlib.ExitStack() as st:
            dg = self.sb(st, [128, nch * ktaps, 128], BF16, "dg")
            Bdg = Buf()
            for idx in range(nch * ktaps):
                S.op("dve", lambda e, idx=idx: e.tensor_scalar(
                    out=dg[:, idx, :], in0=self.cst[:, 0, :], scalar1=self.mixp[:, l, wcol0 + idx:wcol0 + idx + 1],
                    scalar2=scale_fn(idx // ktaps), op0=ALU.mult, op1=ALU.mult),
                    reads=[self.Bcst, self.Bmixp], writes=[Bdg])
            xr = Rot([self.sb(st, [128, nch, 512 + 2 * half], BF16) for _ in range(2)])
            pr = Rot([self.ps(st, [128, 512]) for _ in range(4)])
            srcv = self.dr[src].rearrange("(c p) t -> p c t", p=128)
            for (off, L) in SEQS:
                for t0 in range(0, L, 512):
                    n = min(512, L - t0)
                    x, Bx = xr.next()
                    lo, hi = max(t0 - half, 0), min(t0 + n + half, L)
                    if lo > t0 - half or hi < t0 + n + half:
                        S.op("pool", lambda e, x=x: e.memset(x[:], 0.0), writes=[Bx])
                    d0 = lo - (t0 - half)
                    S.dma("sp", lambda e, x=x, d0=d0, lo=lo, hi=hi, off=off: e.dma_start(
                        out=x[:, :, d0:d0 + hi - lo], in_=srcv[:, :, off + lo:off + hi]),
                        reads=[self.Bloc_src], writes=[Bx])
                    for ch in range(nch):
                        p, Bp = pr.next()
                        for jt in range(ktaps):
                            S.op("pe", lambda e, p=p, ch=ch, jt=jt, x=x, n=n: e.matmul(
                                p[:, 0:n], dg[:, ch * ktaps + jt, :], x[:, ch, jt:jt + n], start=(jt == 0), stop=(jt == ktaps - 1)),
                                reads=[Bdg, Bx], writes=[Bp])
                        emit_out(p, Bp, ch, off, t0, n)

    def mlstm(self, l):
        S = self.S
        dr = self.dr
        hF = dr["hF"]; hB = dr["hB"]
        self.BhF = [Buf() for _ in range(NCH)]
        self.BhB = [Buf() for _ in range(NCH)]
        with contextlib.ExitStack() as st:
            qkc = self.sb(st, [128, 4, TS], BF16, "qkc")
            Bqkc = Buf()
            self.Bloc_src = self.Bloc["qk"]
            cnt = [0]

            def out_qk(p, Bp, ch, off, t0, n):
                cnt[0] += 1
                if cnt[0] % 2:
                    S.op("act", lambda e: e.activation(out=qkc[:, ch, off + t0:off + t0 + n], in_=p[:, 0:n], func=AF.Copy),
                         reads=[Bp], writes=[Bqkc])
                else:
                    S.op("dve", lambda e: e.tensor_copy(out=qkc[:, ch, off + t0:off + t0 + n], in_=p[:, 0:n]),
                         reads=[Bp], writes=[Bqkc])
            self.dwconv(l, "qk_loc", 4, 5, MP_QKW, lambda ch: (DH ** -0.5 if ch >= 2 else 1.0), out_qk)
            S.barrier()
            vaug = self.sb(st, [128, NCH, 257], BF16, "vaug"); Bv = Buf()
            S.dma("sp", lambda e: e.dma_start(out=vaug[:, :, 0:256], in_=dr["v_loc"].rearrange("(c p) d -> p c d", p=128)),
                  reads=[self.Bloc["v"]], writes=[Bv])
            S.op("pool", lambda e: e.memset(vaug[:, :, 256:257], 1.0), writes=[Bv])
            G = self.sb(st, [128, NCH, 4], F32, "G"); BG = Buf()
            S.dma("sp", lambda e: e.dma_start(out=G[:], in_=dr["g_loc"].rearrange("(c p) g -> p c g", p=128)),
                  reads=[self.Bloc["g"]], writes=[BG])
            for col in range(4):
                S.op("dve", lambda e, col=col: e.tensor_scalar_add(
                    out=G[:, :, col], in0=G[:, :, col], scalar1=self.mixp[:, l, MP_BG + col:MP_BG + col + 1]),
                    reads=[BG, self.Bmixp], writes=[BG])
            tmp = self.sb(st, [128, NCH], F32); Btmp = Buf()
            for col in (1, 3):
                S.op("act", lambda e, col=col: e.activation(out=tmp[:], in_=G[:, :, col], func=AF.Exp, scale=-1.0), reads=[BG], writes=[Btmp])
                S.op("act", lambda e: e.activation(out=tmp[:], in_=tmp[:], func=AF.Ln, bias=1.0), reads=[Btmp], writes=[Btmp])
                S.op("dve", lambda e, col=col: e.tensor_scalar_mul(out=G[:, :, col], in0=tmp[:], scalar1=-1.0), reads=[Btmp], writes=[BG])
            NG = NCH * 4
            Gf = G[:].rearrange("p c g -> p (c g)")
            Ghi = self.sb(st, [128, NG], BF16); Glo = self.sb(st, [128, NG], BF16); Gd = self.sb(st, [128, NG], F32)
            BGh = Buf()
            S.op("dve", lambda e: e.tensor_copy(out=Ghi[:], in_=Gf), reads=[BG], writes=[BGh])
            S.op("dve", lambda e: e.tensor_tensor(out=Gd[:], in0=Gf, in1=Ghi[:], op=ALU.subtract), reads=[BG, BGh], writes=[BGh])
            S.op("dve", lambda e: e.tensor_copy(out=Glo[:], in_=Gd[:]), reads=[BGh], writes=[BGh])
            cum = self.sb(st, [128, 3, NCH, 4], F32, "cum"); Bcum = Buf()
            with contextlib.ExitStack() as st2:
                pc = self.ps(st2, [128, 512]); Bpc = Buf()
                for i, m in enumerate((1, 2, 3)):
                    S.op("pe", lambda e, m=m: e.matmul(pc[:, 0:NG], self.cb[:, m, :], Ghi[:], start=True, stop=False), reads=[self.Bcb, BGh], writes=[Bpc])
                    S.op("pe", lambda e, m=m: e.matmul(pc[:, 0:NG], self.cb[:, m, :], Glo[:], start=False, stop=True), reads=[self.Bcb, BGh], writes=[Bpc])
                    S.op("dve", lambda e, i=i: e.tensor_copy(out=cum[:, i].rearrange("p c g -> p (c g)"), in_=pc[:, 0:NG]), reads=[Bpc], writes=[Bcum])
                S.barrier()
            sm = self.sb(st, [128, 2, 5, NCH], F32, "sm"); Bsm = Buf()
            bTs = [(self.sb(st, [NCH, 2, 128], BF16, "bT"), Buf()) for _ in range(2)]
            with contextlib.ExitStack() as st2:
                pT = self.ps(st2, [128, 128], BF16); BpT = Buf()
                hl = self.sb(st2, [128, 2, NCH], BF16); Bhl = Buf()
                for d in range(2):
                    ci, cf = 2 * d, 2 * d + 1
                    bcol = cum[:, d, :, cf]
                    S.op("dve", lambda e, d=d, ci=ci, bcol=bcol: e.tensor_tensor(out=sm[:, d, 0, :], in0=G[:, :, ci], in1=bcol, op=ALU.subtract),
                         reads=[BG, Bcum], writes=[Bsm])
                    S.op("dve", lambda e, d=d, cf=cf: e.tensor_tensor(out=sm[:, d, 4, :], in0=sm[:, d, 0, :], in1=cum[:, 2, :, cf], op=ALU.add),
                         reads=[Bsm, Bcum], writes=[Bsm])
                    S.op("act", lambda e, d=d: e.activation(out=sm[:, d, 1, :], in_=sm[:, d, 4, :], func=AF.Exp), reads=[Bsm], writes=[Bsm])
                    S.op("act", lambda e, d=d, cf=cf: e.activation(out=sm[:, d, 2, :], in_=cum[:, 2, :, cf], func=AF.Exp), reads=[Bcum], writes=[Bsm])
                    S.op("dve", lambda e, bcol=bcol: e.tensor_copy(out=hl[:, 0, :], in_=bcol), reads=[Bcum], writes=[Bhl])
                    S.op("dve", lambda e, d=d, bcol=bcol: e.tensor_tensor(out=sm[:, d, 3, :], in0=bcol, in1=hl[:, 0, :], op=ALU.subtract),
                         reads=[Bcum, Bhl], writes=[Bsm])
                    S.op("dve", lambda e, d=d: e.tensor_copy(out=hl[:, 1, :], in_=sm[:, d, 3, :]), reads=[Bsm], writes=[Bhl])
                    bT, BbT = bTs[d]
                    for h in range(2):
                        S.op("pe", lambda e, h=h: e.transpose(pT[0:NCH, :], hl[:, h, :], self.cb[:, 0, :]), reads=[Bhl, self.Bcb], writes=[BpT])
                        S.op("act", lambda e, h=h, bT=bT: e.activation(out=bT[:, h, :], in_=pT[0:NCH, :], func=AF.Copy), reads=[BpT], writes=[BbT])
                S.barrier()
            oneh = self.sb(st, [NCH, TS], BF16, "oneh"); Boh = Buf()
            S.dma("pool", lambda e: e.dma_start(out=oneh[:], in_=dr["onehot"]), writes=[Boh])
            ktok = self.sb(st, [128, NCH, 256], BF16, "ktok"); Bkt = Buf()
            with contextlib.ExitStack() as st2:
                ptr = Rot([self.ps(st2, [128, 128], BF16) for _ in range(4)])
                for c in range(NCH):
                    for dch in range(2):
                        p, Bp = ptr.next()
                        S.op("pe", lambda e, p=p, c=c, dch=dch: e.transpose(p[:], qkc[:, 2 + dch, c * 128:(c + 1) * 128], self.cb[:, 0, :]),
                             reads=[Bqkc, self.Bcb], writes=[Bp])
                        if (c + dch) % 2:
                            S.op("act", lambda e, p=p, c=c, dch=dch: e.activation(out=ktok[:, c, dch * 128:(dch + 1) * 128], in_=p[:], func=AF.Copy), reads=[Bp], writes=[Bkt])
                        else:
                            S.op("dve", lambda e, p=p, c=c, dch=dch: e.tensor_copy(out=ktok[:, c, dch * 128:(dch + 1) * 128], in_=p[:]), reads=[Bp], writes=[Bkt])
                S.barrier()
            dirs = []
            for d in range(2):
                C32 = self.sb(st, [128, 2, 257], F32, "C32"); Cb = self.sb(st, [128, 2, 257], BF16, "Cb")
                BC32, BCb = Buf(), Buf()
                S.op("pool", lambda e, C32=C32: e.memset(C32[:], 0.0), writes=[BC32])
                S.op("pool", lambda e, Cb=Cb: e.memset(Cb[:], 0.0), writes=[BCb])
                bankA = self.ps(st, [128, 512]); pnum = self.ps(st, [128, 512]); pst0 = self.ps(st, [128, 512]); pst1 = self.ps(st, [128, 512])
                dirs.append(dict(C32=C32, Cb=Cb, BC32=BC32, BCb=BCb, bankA=bankA, BST=Buf(), BBt=Buf(), pnum=pnum, Bnum=Buf(),
                                 pst=(pst0, pst1), Bst=(Buf(), Buf()),
                                 tmp=self.sb(st, [128, 128], F32), Btmp=Buf(), DT=self.sb(st, [128, 128], F32), BDT=Buf(),
                                 PT=self.sb(st, [128, 128], BF16), BPT=Buf(), eBt=self.sb(st, [128, 128], F32), BeBt=Buf(),
                                 qs=self.sb(st, [128, 2, 128], BF16), Bqs=Buf(), vw=self.sb(st, [128, 257], BF16), Bvw=Buf(),
                                 dn=self.sb(st, [128, 2], F32), Bdn=Buf(),
                                 hr=Rot([self.sb(st, [128, 256], F32) for _ in range(2)])))
            orderF = list(range(NCH))
            orderB = [1, 0] + list(range(NCH - 1, 1, -1))
            for step in range(NCH):
                for d, c in ((0, orderF[step]), (1, orderB[step])):
                    X = dirs[d]
                    bT, BbT = bTs[d]
                    cs = slice(c * 128, (c + 1) * 128)
                    ST = X["bankA"][:, 0:128]; Bt = X["bankA"][:, 128:256]
                    for dch in range(2):
                        S.op("pe", lambda e, dch=dch: e.matmul(ST, qkc[:, 2 + dch, cs], qkc[:, dch, cs], start=(dch == 0), stop=(dch == 1)),
                             reads=[Bqkc], writes=[X["BST"]])
                    for h in range(2):
                        S.op("pe", lambda e, h=h: e.matmul(Bt, oneh[:, cs], bT[:, h, :], start=(h == 0), stop=(h == 1)),
                             reads=[Boh, BbT], writes=[X["BBt"]])
                    S.op("dve", lambda e: e.tensor_tensor(out=X["tmp"][:], in0=Bt, in1=self.cst[:, 3 + d, :], op=ALU.add),
                         reads=[X["BBt"], self.Bcst], writes=[X["Btmp"]])
                    S.op("act", lambda e: e.activation(out=X["DT"][:], in_=X["tmp"][:], func=AF.Exp, bias=sm[:, d, 0, c:c + 1]),
                         reads=[X["Btmp"], Bsm], writes=[X["BDT"]])
                    S.op("dve", lambda e: e.tensor_tensor(out=X["PT"][:], in0=ST, in1=X["DT"][:], op=ALU.mult),
                         reads=[X["BST"], X["BDT"]], writes=[X["BPT"]])
                    S.op("act", lambda e: e.activation(out=X["eBt"][:], in_=Bt, func=AF.Exp), reads=[X["BBt"]], writes=[X["BeBt"]])
                    for dch in range(2):
                        S.op("dve", lambda e, dch=dch: e.tensor_tensor(out=X["qs"][:, dch, :], in0=qkc[:, dch, cs], in1=X["eBt"][:], op=ALU.mult),
                             reads=[Bqkc, X["BeBt"]], writes=[X["Bqs"]])
                    S.op("dve", lambda e: e.tensor_scalar_mul(out=X["vw"][:], in0=vaug[:, c, :], scalar1=sm[:, d, 1, c:c + 1]),
                         reads=[Bv, Bsm], writes=[X["Bvw"]])
                    num = X["pnum"][:, 0:257]
                    S.op("pe", lambda e: e.matmul(num, X["PT"][:], vaug[:, c, :], start=True, stop=False),
                         reads=[X["BPT"], Bv], writes=[X["Bnum"]])
                    for dch in range(2):
                        S.op("pe", lambda e, dch=dch: e.matmul(num, X["qs"][:, dch, :], X["Cb"][:, dch, :], start=False, stop=(dch == 1)),
                             reads=[X["Bqs"], X["BCb"]], writes=[X["Bnum"]])
                    S.op("act", lambda e: e.activation(out=X["dn"][:, 0:1], in_=X["pnum"][:, 256:257], func=AF.Abs),
                         reads=[X["Bnum"]], writes=[X["Bdn"]])
                    S.op("dve", lambda e: e.tensor_scalar_max(out=X["dn"][:, 0:1], in0=X["dn"][:, 0:1], scalar1=1.0),
                         reads=[X["Bdn"]], writes=[X["Bdn"]])
                    S.op("dve", lambda e: e.reciprocal(out=X["dn"][:, 1:2], in_=X["dn"][:, 0:1]), reads=[X["Bdn"]], writes=[X["Bdn"]])
                    h, Bh = X["hr"].next()
                    S.op("act", lambda e, h=h: e.activation(out=h[:], in_=X["pnum"][:, 0:256], func=AF.Identity, scale=X["dn"][:, 1:2]),
                         reads=[X["Bnum"], X["Bdn"]], writes=[Bh])
                    S.dma("sp", lambda e, h=h: e.dma_start(out=(hF if d == 0 else hB)[cs, :], in_=h[:]),
                          reads=[Bh], writes=[(self.BhF if d == 0 else self.BhB)[c]])
                    for dch in range(2):
                        S.op("pe", lambda e, dch=dch: e.matmul(X["pst"][dch][:, 0:257], ktok[:, c, dch * 128:(dch + 1) * 128], X["vw"][:],
                                                               start=True, stop=True), reads=[Bkt, X["Bvw"]], writes=[X["Bst"][dch]])
                    for dch in range(2):
                        S.op("dve", lambda e, dch=dch: e.scalar_tensor_tensor(
                            out=X["C32"][:, dch, :], in0=X["C32"][:, dch, :], scalar=sm[:, d, 2, c:c + 1], in1=X["pst"][dch][:, 0:257],
                            op0=ALU.mult, op1=ALU.add), reads=[X["BC32"], X["Bst"][dch], Bsm], writes=[X["BC32"]])
                    S.op("act", lambda e: e.activation(out=X["Cb"][:], in_=X["C32"][:], func=AF.Copy), reads=[X["BC32"]], writes=[X["BCb"]])
            S.barrier()

    def headnorm(self, l):
        S = self.S
        dr = self.dr
        with contextlib.ExitStack() as st:
            hs = self.sb(st, [128, NCH, 256], F32, "hs"); Bhs = Buf()
            s12 = self.sb(st, [128, 4, NCH], F32, "s12"); Bs = Buf()
            junk = self.sb(st, [128, 256], F32); Bj = Buf()
            S.op("dve", lambda e: e.memset(s12[:], 0.0), writes=[Bs])
            ar = Rot([self.sb(st, [128, 256], F32) for _ in range(2)])
            br = Rot([self.sb(st, [128, 256], F32) for _ in range(2)])
            for c in range(NCH):
                cs = slice(c * 128, (c + 1) * 128)
                a, Ba = ar.next(); b, Bb = br.next()
                S.dma("sp", lambda e, a=a: e.dma_start(out=a[:], in_=dr["hF"][cs, :]), reads=[self.BhF[c]], writes=[Ba])
                S.dma("sp", lambda e, b=b: e.dma_start(out=b[:], in_=dr["hB"][cs, :]), reads=[self.BhB[c]], writes=[Bb])
                S.op("dve", lambda e, a=a, b=b: e.tensor_tensor(out=hs[:, c, :], in0=a[:], in1=b[:], op=ALU.add), reads=[Ba, Bb], writes=[Bhs])
                S.op("act", lambda e: e.activation(out=junk[:], in_=hs[:, c, :], func=AF.Identity, accum_out=s12[:, 0, c:c + 1]),
                     reads=[Bhs], writes=[Bs, Bj])
                S.op("act", lambda e: e.activation(out=junk[:], in_=hs[:, c, :], func=AF.Square, accum_out=s12[:, 1, c:c + 1]),
                     reads=[Bhs], writes=[Bs, Bj])
            inv = 1.0 / DH
            S.op("dve", lambda e: e.tensor_scalar_mul(out=s12[:, 0, :], in0=s12[:, 0, :], scalar1=inv), reads=[Bs], writes=[Bs])
            S.op("dve", lambda e: e.tensor_tensor(out=s12[:, 3, :], in0=s12[:, 0, :], in1=s12[:, 0, :], op=ALU.mult), reads=[Bs], writes=[Bs])
            S.op("dve", lambda e: e.scalar_tensor_tensor(out=s12[:, 2, :], in0=s12[:, 1, :], scalar=inv, in1=s12[:, 3, :], op0=ALU.mult, op1=ALU.subtract),
                 reads=[Bs], writes=[Bs])
            S.op("dve", lambda e: e.tensor_scalar(out=s12[:, 2, :], in0=s12[:, 2, :], scalar1=0.0, scalar2=LN_EPS, op0=ALU.max, op1=ALU.add), reads=[Bs], writes=[Bs])
            S.op("act", lambda e: e.activation(out=s12[:, 2, :], in_=s12[:, 2, :], func=AF.Sqrt), reads=[Bs], writes=[Bs])
            S.op("dve", lambda e: e.reciprocal(out=s12[:, 2, :], in_=s12[:, 2, :]), reads=[Bs], writes=[Bs])
            S.op("dve", lambda e: e.scalar_tensor_tensor(out=s12[:, 3, :], in0=s12[:, 0, :], scalar=-1.0, in1=s12[:, 2, :], op0=ALU.mult, op1=ALU.mult),
                 reads=[Bs], writes=[Bs])
            hafm = self.sb(st, [128, 2, TS], BF16, "hafm"); Bhafm = Buf()
            sor = Rot([self.sb(st, [128, 256], F32) for _ in range(2)])
            t1r = Rot([self.sb(st, [128, 256], F32) for _ in range(2)])
            habr = Rot([self.sb(st, [128, 256], BF16) for _ in range(2)])
            ptr = Rot([self.ps(st, [128, 128], BF16) for _ in range(4)])
            for c in range(NCH):
                cs = slice(c * 128, (c + 1) * 128)
                so, Bso = sor.next(); t1, Bt1 = t1r.next(); hab, Bhab = habr.next()
                S.dma("sp", lambda e, so=so: e.dma_start(out=so[:], in_=dr["so_loc"][cs, :]), reads=[self.Bloc["so"]], writes=[Bso])
                S.op("act", lambda e, t1=t1: e.activation(out=t1[:], in_=hs[:, c, :], func=AF.Identity, scale=s12[:, 2, c:c + 1], bias=s12[:, 3, c:c + 1]),
                     reads=[Bhs, Bs], writes=[Bt1])
                S.op("dve", lambda e, t1=t1: e.tensor_tensor(out=t1[:], in0=t1[:], in1=self.mhg[:, l, :], op=ALU.mult), reads=[Bt1, self.Bmhg], writes=[Bt1])
                S.op("dve", lambda e, t1=t1, so=so, hab=hab: e.tensor_tensor(out=hab[:], in0=t1[:], in1=so[:], op=ALU.mult), reads=[Bt1, Bso], writes=[Bhab])
                for dch in range(2):
                    p, Bp = ptr.next()
                    S.op("pe", lambda e, p=p, hab=hab, dch=dch: e.transpose(p[:], hab[:, dch * 128:(dch + 1) * 128], self.cb[:, 0, :]),
                         reads=[Bhab, self.Bcb], writes=[Bp])
                    S.op("act", lambda e, p=p, dch=dch: e.activation(out=hafm[:, dch, cs], in_=p[:], func=AF.Copy), reads=[Bp], writes=[Bhafm])
            self.Bsend["ha"] = [Buf()]
            S.dma("sp", lambda e: e.dma_start(out=dr["ha_s"].rearrange("(c p) t -> p c t", p=128), in_=hafm[:]),
                  reads=[Bhafm], writes=self.Bsend["ha"])
            S.barrier()

    def fnet(self, l):
        S = self.S
        dr = self.dr
        with contextlib.ExitStack() as st:
            hbfm = self.sb(st, [128, 2, TS], BF16, "hbfm"); Bhb = Buf()
            cg = self.sb(st, [128, 2, 2, 2, 256], BF16, "cg"); Bcg = Buf()
            S.dma("pool", lambda e: e.dma_start(out=cg[:], in_=dr["dft_cg"]), writes=[Bcg])
            for si, (off, L) in enumerate(SEQS):
                N1 = L // 128
                with contextlib.ExitStack() as st2:
                    w1 = self.sb(st2, [N1, 2, N1], BF16, "w1"); Bw1 = Buf()
                    S.dma("pool", lambda e, w1=w1, si=si: e.dma_start(out=w1[:], in_=dr[f"dft_w1_{si}"]), writes=[Bw1])
                    m2 = self.sb(st2, [128, N1, 3, 128], BF16, "m2"); Bm2 = Buf()
                    S.dma("pool", lambda e, m2=m2, si=si: e.dma_start(out=m2[:], in_=dr[f"dft_m2_{si}"]), writes=[Bm2])
                    yd = dr[f"dft_y{si}"]
                    Byd = Buf()
                    fsrc = dr["f_loc"][off:off + L, :].rearrange("(t1 t2) c -> t1 (t2 c)", t2=128)
                    xr = Rot([self.sb(st2, [N1, 4096], BF16) for _ in range(2)])
                    yr = Rot([self.sb(st2, [N1, 2, 4096], BF16) for _ in range(2)])
                    pr = Rot([self.ps(st2, [128, 512]) for _ in range(4)])
                    for blk in range(8):
                        x, Bx = xr.next(); y, By = yr.next()
                        S.dma("sp", lambda e, x=x, blk=blk: e.dma_start(out=x[:], in_=fsrc[:, blk * 4096:(blk + 1) * 4096]),
                              reads=[self.Bloc["f"]], writes=[Bx])
                        for sub in range(8):
                            for ri in range(2):
                                p, Bp = pr.next()
                                S.op("pe", lambda e, p=p, ri=ri, x=x, sub=sub: e.matmul(p[0:N1, :], w1[:, ri, :], x[:, sub * 512:(sub + 1) * 512], start=True, stop=True),
                                     reads=[Bw1, Bx], writes=[Bp])
                                if ri == 0:
                                    S.op("act", lambda e, p=p, y=y, sub=sub, ri=ri: e.activation(out=y[:, ri, sub * 512:(sub + 1) * 512], in_=p[0:N1, :], func=AF.Copy), reads=[Bp], writes=[By])
                                else:
                                    S.op("dve", lambda e, p=p, y=y, sub=sub, ri=ri: e.tensor_copy(out=y[:, ri, sub * 512:(sub + 1) * 512], in_=p[0:N1, :]), reads=[Bp], writes=[By])
                        S.dma("sp", lambda e, y=y, blk=blk: e.dma_start(
                            out=yd[:, :, blk * 4096:(blk + 1) * 4096].rearrange("r k n -> k r n"), in_=y[:]), reads=[By], writes=[Byd])
                    KB = min(8, N1)
                    ykr = Rot([self.sb(st2, [128, KB, 2, 256], BF16) for _ in range(2)])
                    xkr = Rot([self.sb(st2, [128, 2, 2, 128], BF16) for _ in range(2)])
                    ydv = yd.rearrange("r k (t c) -> t k r c", c=256)
                    for kb in range(0, N1, KB):
                        yk, Byk = ykr.next()
                        for ri in range(2):
                            S.dma("sp", lambda e, yk=yk, kb=kb, ri=ri: e.dma_start(out=yk[:, :, ri, :], in_=ydv[:, kb:kb + KB, ri, :]),
                                  reads=[Byd], writes=[Byk])
                        for ki in range(KB):
                            k1 = kb + ki
                            xk, Bxk = xkr.next()
                            for cch in range(2):
                                for ro in range(2):
                                    p, Bp = pr.next()
                                    ta, tb = (0, 2) if ro == 0 else (1, 0)
                                    S.op("pe", lambda e, p=p, cch=cch, ta=ta, yk=yk, ki=ki, k1=k1: e.matmul(
                                        p[:, 0:128], yk[:, ki, 0, cch * 128:(cch + 1) * 128], m2[:, k1, ta, :], start=True, stop=False),
                                        reads=[Byk, Bm2], writes=[Bp])
                                    S.op("pe", lambda e, p=p, cch=cch, tb=tb, yk=yk, ki=ki, k1=k1: e.matmul(
                                        p[:, 0:128], yk[:, ki, 1, cch * 128:(cch + 1) * 128], m2[:, k1, tb, :], start=False, stop=True),
                                        reads=[Byk, Bm2], writes=[Bp])
                                    if ro == 0:
                                        S.op("act", lambda e, p=p, xk=xk, cch=cch, ro=ro: e.activation(out=xk[:, cch, ro, :], in_=p[:, 0:128], func=AF.Copy), reads=[Bp], writes=[Bxk])
                                    else:
                                        S.op("dve", lambda e, p=p, xk=xk, cch=cch, ro=ro: e.tensor_copy(out=xk[:, cch, ro, :], in_=p[:, 0:128]), reads=[Bp], writes=[Bxk])
                            for oc in range(2):
                                p, Bp = pr.next()
                                i = 0
                                for cch in range(2):
                                    for ri in range(2):
                                        S.op("pe", lambda e, p=p, cch=cch, ri=ri, oc=oc, xk=xk, i=i: e.matmul(
                                            p[:, 0:128], cg[:, si, cch, ri, oc * 128:(oc + 1) * 128], xk[:, cch, ri, :], start=(i == 0), stop=(i == 3)),
                                            reads=[Bcg, Bxk], writes=[Bp])
                                        i += 1
                                dst = hbfm[:, oc, off + k1:off + k1 + N1 * 127 + 1:N1]
                                if oc == 0:
                                    S.op("act", lambda e, p=p, dst=dst: e.activation(out=dst, in_=p[:, 0:128], func=AF.Copy), reads=[Bp], writes=[Bhb])
                                else:
                                    S.op("dve", lambda e, p=p, dst=dst: e.tensor_copy(out=dst, in_=p[:, 0:128]), reads=[Bp], writes=[Bhb])
                    S.barrier()
            self.Bsend["hb"] = [Buf()]
            S.dma("sp", lambda e: e.dma_start(out=dr["hb_s"].rearrange("(c p) t -> p c t", p=128), in_=hbfm[:]),
                  reads=[Bhb], writes=self.Bsend["hb"])
            S.barrier()

    def convbranch(self, l):
        S = self.S
        dr = self.dr
        self.Bloc_src = self.Bloc["glu"]
        self.Bsend["cv"] = []
        with contextlib.ExitStack() as st:
            cvr = Rot([self.sb(st, [128, 512], F32) for _ in range(3)])
            dst = dr["cv_s"].rearrange("(c p) t -> p c t", p=128)

            def out_cv(p, Bp, ch, off, t0, n):
                cv, Bcv = cvr.next()
                S.op("act", lambda e: e.activation(out=cv[:, 0:n], in_=p[:, 0:n], func=AF.Identity,
                                                   bias=self.mixp[:, l, MP_DWB + ch:MP_DWB + ch + 1]), reads=[Bp, self.Bmixp], writes=[Bcv])
                bb = Buf()
                self.Bsend["cv"].append(bb)
                S.dma("sp", lambda e: e.dma_start(out=dst[:, ch, off + t0:off + t0 + n], in_=cv[:, 0:n]), reads=[Bcv], writes=[bb])
            self.dwconv(l, "glu_loc", 2, 31, MP_DWW, lambda ch: 1.0, out_cv)
            S.barrier()


def vec_layout():
    cols = {}
    o = 0
    for l in range(DEPTH):
        for s in range(3):
            cols[f"ln_g{l}_{s}"] = o; o += 8
            cols[f"ln_b{l}_{s}"] = o; o += 8
        cols[f"cn_g{l}"] = o; o += 8
        cols[f"cn_b{l}"] = o; o += 8
        cols[f"b_ada{l}"] = o; o += 72
    cols["selb"] = o; o += 2
    return cols, o


class MK(TP, MixMixin):
    pass


WEIGHTS = {}
for _l in range(DEPTH):
    WEIGHTS[f"w_in{_l}"] = (D, NIN)
    WEIGHTS[f"wo_{_l}"] = (D, D)
    for _i in range(2):
        WEIGHTS[f"w1_{_l}{_i}"] = (D, DFF)
        WEIGHTS[f"w3_{_l}{_i}"] = (D, DFF)
        WEIGHTS[f"w2_{_l}{_i}"] = (DFF, D)
    for _b in range(3):
        WEIGHTS[f"wb_{_l}{_b}"] = (D, D)


def mix_io():
    io = {"cmask": ("in", (128, 5, 128), F32), "mixp": ("in", (128, DEPTH, MP_N), F32),
          "mhg": ("in", (128, DEPTH, 256), F32), "onehot": ("in", (NCH, TS), F32),
          "dft_cg": ("in", (128, 2, 2, 2, 256), F32)}
    for si, (off, L) in enumerate(SEQS):
        N1 = L // 128
        io[f"dft_w1_{si}"] = ("in", (N1, 2, N1), F32)
        io[f"dft_m2_{si}"] = ("in", (128, N1, 3, 128), F32)
    return io


def mix_scratch(P):
    for si, (off, L) in enumerate(SEQS):
        P.scratch(f"dft_y{si}", (2, L // 128, 128 * 256), BF16)
    P.scratch("hF", (TS, 256), F32)
    P.scratch("hB", (TS, 256), F32)


def build_mixtest(parts):
    cols, nv = vec_layout()
    io = {"vecs": ("in", (128, nv), F32)}
    io.update(mix_io())
    io.update({"qk_loc": ("in", (512, TS), BF16), "v_loc": ("in", (TS, 256), BF16), "so_loc": ("in", (TS, 256), F32),
               "g_loc": ("in", (TS, 4), F32), "f_loc": ("in", (TS, 256), BF16), "glu_loc": ("in", (256, TS), BF16),
               "ha_s": ("out", (256, TS), BF16), "hb_s": ("out", (256, TS), BF16), "cv_s": ("out", (256, TS), F32)})
    P = MK(io, cols)
    P.Bloc = {n: Buf() for n in ("qk", "v", "so", "g", "f", "glu")}
    mix_scratch(P)
    P.load_mix_consts()
    if "mlstm" in parts:
        P.mlstm(0)
        P.headnorm(0)
    if "fnet" in parts:
        P.fnet(0)
    if "conv" in parts:
        P.convbranch(0)
    P.S.finish()
    P.top.close()
    return P.nc


def build_full(nsteps=99, dbg=None):
    global NUM_DEV
    NUM_DEV = NCORE
    cols, nv = vec_layout()
    io = {"vecs": ("in", (128, nv), F32), "cT3": ("in", (128, KK, 3), F32),
          "xin": ("in", (D, NT), F32), "pos": ("in", (D, TL), F32),
          "xout": ("out", (D, NT), F32)}
    for l in range(DEPTH):
        io[f"w_ada{l}"] = ("in", (D // NR, 9 * D), F32)
    for nm, (K, N) in WEIGHTS.items():
        io[nm] = ("in", (K // NR, N), F32)
    io.update(mix_io())
    P = MK(io, cols)
    S = P.S
    mix_scratch(P)
    sc = P.scratch
    for nm, shape, dt in (("qk", (2048, NT), BF16), ("v", (NT, 1024), BF16), ("so", (NT, 1024), F32),
                          ("g", (4 * NT, 4), F32), ("f", (NT, 1024), BF16), ("glu", (1024, NT), BF16)):
        sc(nm + "_s", shape, dt)
    for nm, shape, dt in (("qk", (512, TS), BF16), ("v", (TS, 256), BF16), ("so", (TS, 256), F32),
                          ("g", (TS, 4), F32), ("f", (TS, 256), BF16), ("glu", (256, TS), BF16)):
        sc(nm + "_loc", shape, dt)
    for nm, dt in (("ha", BF16), ("hb", BF16), ("cv", F32)):
        sc(nm + "_s", (256, TS), dt)
        sc(nm + "_tp", (D, NT), dt)
    sc("mg", (3072, NT), F32)
    for nm in ("xa0", "xa1", "xa2", "xa3"):
        sc(nm, (D, NT), F32)
    for nm in ("u0", "u1", "u2", "u3"):
        sc(nm, (D, NT), BF16)
    sc("a", (DFF, NT), BF16)

    P.load_mix_consts()
    P.compute_ada_all()
    P.derive_mod("m00", 0, 0)
    P.derive_ep("e00", 0, 0, (0, 1)); P.derive_ep("e01", 0, 1, (0, 2)); P.derive_ep("e02", 0, 2, (1, 0))
    P.derive_ep("e10", 1, 0, (1, 1)); P.derive_ep("e11", 1, 1, (1, 2)); P.derive_ep("e12", 1, 2, None)

    def gw(names):
        for nm in names:
            P.gather_weight(nm, *WEIGHTS[nm])

    steps = []
    steps.append(lambda: gw(["w1_00", "w3_00", "w2_00", "w_in0"]))
    steps.append(lambda: P.step_prep("xin", "pos", "xa0", "u0", "m00"))
    for l in range(DEPTH):
        steps.append(lambda l=l: P.step_ffn_up(l, 0, "u0", "a"))
        steps.append(lambda l=l: P.step_ffn_down(l, 0, "a", "xa0", "xa1", "u1", f"e{l}0"))
        steps.append(lambda l=l: P.step_inproj(l, "u1"))
        if l == 0:
            steps.append(lambda: gw(["wb_00", "wb_01", "wb_02", "wo_0", "w1_01", "w3_01", "w2_01", "w1_10", "w3_10", "w2_10", "w_in1"]))
        else:
            steps.append(lambda: gw(["wb_10", "wb_11", "wb_12", "wo_1", "w1_11", "w3_11", "w2_11"]))
        steps.append(lambda l=l: P.unpack_A(["qk", "v", "g"]))
        steps.append(lambda l=l: P.mlstm(l))
        steps.append(lambda l=l: P.unpack_A(["so"]))
        steps.append(lambda l=l: P.headnorm(l))
        steps.append(lambda l=l: P.exchange_B("ha"))
        steps.append(lambda l=l: P.unpack_A(["f"]))
        steps.append(lambda l=l: P.fnet(l))
        steps.append(lambda l=l: P.exchange_B("hb"))
        steps.append(lambda l=l: P.unpack_A(["glu"]))
        steps.append(lambda l=l: P.convbranch(l))
        steps.append(lambda l=l: P.exchange_B("cv"))
        steps.append(lambda l=l: P.unpack_B())
        steps.append(lambda l=l: P.step_postmix(l, "xa1", "xa2", "u2", f"e{l}1"))
        steps.append(lambda l=l: P.step_ffn_up(l, 1, "u2", "a"))
        if l == 0:
            steps.append(lambda l=l: P.step_ffn_down(l, 1, "a", "xa2", "xa0", "u0", f"e{l}2"))
        else:
            steps.append(lambda l=l: P.step_ffn_down(l, 1, "a", "xa2", "xout", "u3", f"e{l}2", final=True))
    for i, fn in enumerate(steps):
        if i < nsteps:
            fn()
    if dbg:
        for nm in dbg:
            src = P.dr[nm]
            dst = P.nc.dram_tensor("dbg_" + nm, list(src.shape), src.dtype, kind="ExternalOutput").ap()
            S.dma("sp", lambda e, src=src, dst=dst: e.dma_start(out=dst, in_=src))
    S.finish()
    P.top.close()
    return P.nc


def fmaj(vec):
    return np.ascontiguousarray(np.asarray(vec, np.float32).reshape(-1, 128).T)


def sincos_table():
    quarter = D // 4
    omega = (1.0 / (10000.0 ** (np.arange(quarter, dtype=np.float32) / np.float32(quarter)))).astype(np.float32)
    rows = T // GRID_W
    r = np.arange(rows, dtype=np.float32)[:, None] * omega
    cl = np.arange(GRID_W, dtype=np.float32)[:, None] * omega
    row_emb = np.concatenate([np.sin(r), np.cos(r)], axis=-1).astype(np.float32)
    col_emb = np.concatenate([np.sin(cl), np.cos(cl)], axis=-1).astype(np.float32)
    emb = np.concatenate([np.broadcast_to(row_emb[:, None, :], (rows, GRID_W, D // 2)),
                          np.broadcast_to(col_emb[None, :, :], (rows, GRID_W, D // 2))], axis=-1)
    return emb.reshape(T, D).astype(np.float32)


def const_tables():
    c = {}
    s = np.arange(128)[:, None]; t = np.arange(128)[None, :]
    cm = np.zeros((128, 5, 128), np.float32)
    cm[:, 0] = np.eye(128)
    cm[:, 1] = (s <= t)
    cm[:, 2] = (s >= t)
    cm[:, 3] = np.where(s <= t, 0.0, -30000.0)
    cm[:, 4] = np.where(s >= t, 0.0, -30000.0)
    c["cmask"] = cm
    oh = np.zeros((NCH, TS), np.float32)
    for ch in range(NCH):
        oh[ch, ch * 128:(ch + 1) * 128] = 1.0
    c["onehot"] = oh
    cg = np.zeros((128, 2, 2, 2, 256), np.float32)
    cc = np.arange(256, dtype=np.float64)
    ang = 2 * np.pi * np.outer(cc, cc) / 256.0
    for si, (off, L) in enumerate(SEQS):
        N1 = L // 128
        scale = 1.0 / np.sqrt(L * 256.0)
        for cch in range(2):
            cg[:, si, cch, 0, :] = (np.cos(ang) * scale)[cch * 128:(cch + 1) * 128]
            cg[:, si, cch, 1, :] = (np.sin(ang) * scale)[cch * 128:(cch + 1) * 128]
        k1 = np.arange(N1, dtype=np.float64)
        a1 = 2 * np.pi * np.outer(k1, k1) / N1
        w1 = np.zeros((N1, 2, N1), np.float32)
        w1[:, 0, :] = np.cos(a1); w1[:, 1, :] = -np.sin(a1)
        c[f"dft_w1_{si}"] = w1
        t2 = np.arange(128, dtype=np.float64)[:, None, None]
        kk1 = np.arange(N1, dtype=np.float64)[None, :, None]
        k2 = np.arange(128, dtype=np.float64)[None, None, :]
        th = 2 * np.pi * (kk1 * t2 / L + k2 * t2 / 128.0)
        m2 = np.zeros((128, N1, 3, 128), np.float32)
        m2[:, :, 0, :] = np.cos(th); m2[:, :, 1, :] = -np.sin(th); m2[:, :, 2, :] = np.sin(th)
        c[f"dft_m2_{si}"] = m2
    c["dft_cg"] = cg
    return c


def make_vecs(inp, b):
    cols, nv = vec_layout()
    v = np.zeros((128, nv), np.float32)
    for l in range(DEPTH):
        for s in range(3):
            v[:, cols[f"ln_g{l}_{s}"]:cols[f"ln_g{l}_{s}"] + 8] = fmaj(inp["ln_g"][l, s])
            v[:, cols[f"ln_b{l}_{s}"]:cols[f"ln_b{l}_{s}"] + 8] = fmaj(inp["ln_b"][l, s])
        v[:, cols[f"cn_g{l}"]:cols[f"cn_g{l}"] + 8] = fmaj(inp["conv_norm_g"][l])
        v[:, cols[f"cn_b{l}"]:cols[f"cn_b{l}"] + 8] = fmaj(inp["conv_norm_b"][l])
        v[:, cols[f"b_ada{l}"]:cols[f"b_ada{l}"] + 72] = fmaj(inp["b_ada"][l])
    v[:, cols["selb"] + b] = 1.0
    return v


def mix_params(inp, j):
    mp = np.zeros((128, DEPTH, MP_N), np.float32)
    mh = np.zeros((128, DEPTH, 256), np.float32)
    for l in range(DEPTH):
        qkw = inp["qk_conv_w"][l]
        for ch in range(4):
            base = (0 if ch < 2 else 1024) + j * 256 + (ch % 2) * 128
            mp[:, l, MP_QKW + ch * 5:MP_QKW + ch * 5 + 5] = qkw[:, base:base + 128].T
        dww = inp["dw_w"][l]
        for ch in range(2):
            base = j * 256 + ch * 128
            mp[:, l, MP_DWW + ch * 31:MP_DWW + (ch + 1) * 31] = dww[:, base:base + 128].T
            mp[:, l, MP_DWB + ch] = inp["dw_b"][l][base:base + 128]
        bg = inp["b_gates"][l].reshape(2, 2, HEADS)
        mp[:, l, MP_BG:MP_BG + 4] = np.array([bg[0, 0, j], bg[0, 1, j], bg[1, 0, j], bg[1, 1, j]], np.float32)[None, :]
        mh[:, l, :] = inp["mh_norm_g"][l][j * 256:(j + 1) * 256][None, :]
    return mp, mh


def host_inputs(inp):
    pos_tab = sincos_table()
    consts = const_tables()
    wsrc = {}
    for l in range(DEPTH):
        wsrc[f"w_in{l}"] = inp["w_in"][l]
        wsrc[f"wo_{l}"] = inp["w_out"][l]
        wsrc[f"w_ada{l}"] = inp["w_ada"][l]
        for i in range(2):
            wsrc[f"w1_{l}{i}"] = inp["ffn_w1"][l, i]
            wsrc[f"w3_{l}{i}"] = inp["ffn_w3"][l, i]
            wsrc[f"w2_{l}{i}"] = inp["ffn_w2"][l, i]
        for br in range(3):
            wsrc[f"wb_{l}{br}"] = inp["w_branch"][l, br]
    maps = []
    for r in range(NCORE):
        b, s = r // 4, r % 4
        m = dict(consts)
        m["vecs"] = make_vecs(inp, b)
        rg = r % NR
        c3 = np.stack([inp["c"][0], inp["c"][1], inp["c_ctx"]], axis=-1).astype(np.float32)[rg * KK * 128:(rg + 1) * KK * 128]
        m["cT3"] = np.ascontiguousarray(c3.reshape(KK, 128, 3).transpose(1, 0, 2))
        xin = np.concatenate([inp["x"][b, TL * s:TL * (s + 1), :].T, inp["ctx"][b, CL * s:CL * (s + 1), :].T], axis=1)
        m["xin"] = np.ascontiguousarray(xin, dtype=np.float32)
        m["pos"] = np.ascontiguousarray(pos_tab[TL * s:TL * (s + 1), :].T)
        m["mixp"], m["mhg"] = mix_params(inp, s)
        for nm, w in wsrc.items():
            k = w.shape[0] // NR
            m[nm] = np.ascontiguousarray(w[rg * k:(rg + 1) * k], dtype=np.float32)
        maps.append(m)
    return maps


_NC_CACHE = {}


def kernel(**inputs):
    inp = {k: np.asarray(v) for k, v in inputs.items()}
    if "nc" not in _NC_CACHE:
        _NC_CACHE["nc"] = build_full()
    maps = host_inputs(inp)
    res = run_bass_kernel_spmd(_NC_CACHE["nc"], maps, core_ids=list(range(NCORE)))
    out = np.empty((B, T, D), np.float32)
    for r in range(NCORE):
        b, s = r // 4, r % 4
        out[b, TL * s:TL * (s + 1), :] = np.asarray(res.results[r]["xout"])[:, 0:TL].T
    return out
```
